# Optimizing a Trainium2 kernel written in Bass

```python
import math
import jax, jax.numpy as jnp
from jax import lax
import numpy as np

D_MODEL = 1024
BATCH = 8
SEQ = 2048
DEPTH = 4

PLE_DIM = 256
Q_BLOCK = 128
NORM_EPS = 1e-6

MLA_HEADS = 8
MLA_NOPE = 64
MLA_ROPE = 32
MLA_V = 64
MLA_Q_LORA = 384
MLA_KV_LORA = 256
MLA_WIDTH = MLA_HEADS * MLA_V
ROPE_THETA = 10000.0

SSM_HEADS = 16
SSM_HEAD_DIM = 64
SSM_INNER = SSM_HEADS * SSM_HEAD_DIM
SSM_GROUPS = 2
SSM_STATE = 128
SSM_CONV = 4
SSM_CHUNK = 128
SSM_CONV_DIM = SSM_INNER + 2 * SSM_GROUPS * SSM_STATE

DSA_HEADS = 8
DSA_HEAD_DIM = 64
DSA_WIDTH = DSA_HEADS * DSA_HEAD_DIM
IDX_HEADS = 8
IDX_DIM = 64
TOPK_MAX = 256

REL_BUCKETS = 32
REL_MAX_DIST = 128

N_BRANCHES = 3
SPLIT_SIZES = (
    MLA_Q_LORA,
    MLA_KV_LORA,
    MLA_ROPE,
    MLA_WIDTH,
    SSM_INNER,
    SSM_CONV_DIM,
    SSM_HEADS,
    DSA_WIDTH,
    DSA_HEAD_DIM,
    DSA_HEAD_DIM,
    IDX_HEADS * IDX_DIM,
    IDX_DIM,
    IDX_HEADS,
    DSA_WIDTH,
    N_BRANCHES * D_MODEL,
)
IN_TOTAL = sum(SPLIT_SIZES)

kernel_name = 'hybrid_mla_ssd_dsa_gated_merge'


def rms_norm(x, g):
    xf = x.astype(jnp.float32)
    y = xf * lax.rsqrt(jnp.mean(xf * xf, axis=-1, keepdims=True) + NORM_EPS)
    return (y * g.astype(jnp.float32)).astype(x.dtype)


def apply_rope(x, cos, sin):
    xf = x.astype(jnp.float32)
    x1, x2 = jnp.split(xf, 2, axis=-1)
    return jnp.concatenate([x1 * cos - x2 * sin, x2 * cos + x1 * sin], axis=-1).astype(x.dtype)


def t5_bucket(rel):
    n = jnp.maximum(rel, 0)
    exact = REL_BUCKETS // 2
    log_ratio = jnp.log(jnp.maximum(n, exact).astype(jnp.float32) / exact) / math.log(REL_MAX_DIST / exact)
    large = jnp.minimum(exact + (log_ratio * (REL_BUCKETS - exact)).astype(jnp.int32), REL_BUCKETS - 1)
    return jnp.where(n < exact, n, large)


def mla_branch(c_q, c_kv, k_rope, q_norm, w_uq, kv_norm, w_ukv, cos, sin):
    b, s, _ = c_q.shape
    q = (rms_norm(c_q, q_norm) @ w_uq).reshape(b, s, MLA_HEADS, MLA_NOPE + MLA_ROPE)
    q_nope = q[..., :MLA_NOPE]
    q_rope = apply_rope(q[..., MLA_NOPE:], cos[:, :, None], sin[:, :, None])
    kv = (rms_norm(c_kv, kv_norm) @ w_ukv).reshape(b, s, MLA_HEADS, MLA_NOPE + MLA_V)
    k_nope, v = kv[..., :MLA_NOPE], kv[..., MLA_NOPE:]
    k_rope = apply_rope(k_rope, cos, sin)
    scale = (MLA_NOPE + MLA_ROPE) ** -0.5
    outs = []
    for blk in range(s // Q_BLOCK):
        qs, qe = blk * Q_BLOCK, (blk + 1) * Q_BLOCK
        logits = (jnp.einsum('bqhd,bkhd->bhqk', q_nope[:, qs:qe], k_nope[:, :qe])
                  + jnp.einsum('bqhd,bkd->bhqk', q_rope[:, qs:qe], k_rope[:, :qe])).astype(jnp.float32) * scale
        causal = jnp.arange(qs, qe)[:, None] >= jnp.arange(qe)[None, :]
        probs = jax.nn.softmax(jnp.where(causal, logits, -jnp.inf), axis=-1).astype(v.dtype)
        outs.append(jnp.einsum('bhqk,bkhd->bqhd', probs, v[:, :qe]))
    return jnp.concatenate(outs, axis=1).reshape(b, s, MLA_WIDTH)


def ssd_chunked(x, dt, a, bm, cm, d_skip):
    b, s, h, pdim = x.shape
    nc, q = s // SSM_CHUNK, SSM_CHUNK
    hg = SSM_HEADS // SSM_GROUPS
    xc = x.reshape(b, nc, q, SSM_GROUPS, hg, pdim)
    dtc = dt.reshape(b, nc, q, SSM_GROUPS, hg)
    xdt = xc * dtc[..., None]
    bc = bm.reshape(b, nc, q, SSM_GROUPS, SSM_STATE)
    cc = cm.reshape(b, nc, q, SSM_GROUPS, SSM_STATE)
    cum = jnp.cumsum(jnp.moveaxis(dtc * a.reshape(SSM_GROUPS, hg), 2, -1), axis=-1)
    tril = jnp.tril(jnp.ones((q, q), dtype=bool))
    decay_in = jnp.exp(jnp.where(tril, cum[..., :, None] - cum[..., None, :], -jnp.inf))
    cb = jnp.einsum('bclgn,bcsgn->bcgls', cc, bc)
    y_diag = jnp.einsum('bcghls,bcsghp->bclghp', cb[:, :, :, None] * decay_in, xdt)
    decay_to_end = jnp.exp(cum[..., -1:] - cum)
    chunk_states = jnp.einsum('bcsgn,bcghs,bcsghp->bcghpn', bc, decay_to_end, xdt)
    chunk_decay = jnp.exp(cum[..., -1])

    def step(state, inp):
        st, dec = inp
        return state * dec[..., None, None] + st, state

    init = jnp.zeros((b, SSM_GROUPS, hg, pdim, SSM_STATE), x.dtype)
    _, prev = lax.scan(step, init, (jnp.moveaxis(chunk_states, 1, 0), jnp.moveaxis(chunk_decay, 1, 0)))
    prev = jnp.moveaxis(prev, 0, 1)
    y_off = jnp.einsum('bclgn,bcghpn,bcghl->bclghp', cc, prev, jnp.exp(cum))
    y = y_diag + y_off + xc * d_skip.reshape(SSM_GROUPS, hg)[..., None]
    return y.reshape(b, s, h * pdim)


def ssd_branch(z, xbc, dt_raw, conv_w, conv_b, dt_bias, a_log, d_skip, ssm_norm):
    b, s, _ = xbc.shape
    xbc = lax.conv_general_dilated(xbc, conv_w.astype(xbc.dtype)[:, None, :], window_strides=(1,),
                                   padding=[(SSM_CONV - 1, 0)], dimension_numbers=('NWC', 'WIO', 'NWC'),
                                   feature_group_count=SSM_CONV_DIM)
    xbc = jax.nn.silu(xbc + conv_b)
    xs, bm, cm = jnp.split(xbc, [SSM_INNER, SSM_INNER + SSM_GROUPS * SSM_STATE], axis=-1)
    dt = jax.nn.softplus(dt_raw.astype(jnp.float32) + dt_bias.astype(jnp.float32))
    a = -jnp.exp(a_log.astype(jnp.float32))
    y = ssd_chunked(xs.reshape(b, s, SSM_HEADS, SSM_HEAD_DIM).astype(jnp.float32), dt, a,
                    bm.reshape(b, s, SSM_GROUPS, SSM_STATE).astype(jnp.float32),
                    cm.reshape(b, s, SSM_GROUPS, SSM_STATE).astype(jnp.float32),
                    d_skip.astype(jnp.float32)).astype(z.dtype)
    return rms_norm(y * jax.nn.silu(z), ssm_norm)


def dsa_branch(q_c, k_c, v_c, q_idx, k_idx, w_idx, positions, rel_bias):
    b, s, _ = q_c.shape
    n_sel = min(TOPK_MAX, s // 4)
    q_c = q_c.reshape(b, s, DSA_HEADS, DSA_HEAD_DIM)
    q_idx = q_idx.reshape(b, s, IDX_HEADS, IDX_DIM)
    gather = jax.vmap(lambda arr, ids: arr[ids])
    key_ids = jnp.arange(s)
    scale = DSA_HEAD_DIM ** -0.5
    outs = []
    for blk in range(s // Q_BLOCK):
        qs, qe = blk * Q_BLOCK, (blk + 1) * Q_BLOCK
        q_ids = jnp.arange(qs, qe)
        rel = jax.nn.relu(jnp.einsum('bqhd,bkd->bqhk', q_idx[:, qs:qe], k_idx).astype(jnp.float32))
        index_score = jnp.einsum('bqh,bqhk->bqk', w_idx[:, qs:qe].astype(jnp.float32), rel)
        index_score = jnp.where(key_ids[None, None, :] <= q_ids[None, :, None], index_score, -jnp.inf)
        _, sel = lax.top_k(index_score, n_sel)
        k_sel = gather(k_c, sel)
        v_sel = gather(v_c, sel)
        pos_sel = gather(positions, sel)
        logits = jnp.einsum('bqhd,bqkd->bhqk', q_c[:, qs:qe], k_sel).astype(jnp.float32) * scale
        bucket = t5_bucket(positions[:, qs:qe, None] - pos_sel)
        logits = logits + jnp.moveaxis(rel_bias[bucket].astype(jnp.float32), -1, 1)
        valid = sel <= q_ids[None, :, None]
        probs = jax.nn.softmax(jnp.where(valid[:, None], logits, -jnp.inf), axis=-1).astype(v_c.dtype)
        outs.append(jnp.einsum('bhqk,bqkd->bqhd', probs, v_sel))
    return jnp.concatenate(outs, axis=1).reshape(b, s, DSA_WIDTH)


def setup_inputs(seed: int = 0) -> dict:
    key = jax.random.key(seed)
    ks = jax.random.split(key, 24)
    f32 = jnp.float32

    def nrm(k, shape, scale):
        return jax.random.normal(k, shape, f32) * scale

    def gain(k, shape):
        return 1.0 + 0.1 * jax.random.normal(k, shape, f32)

    dt0 = jnp.exp(jax.random.uniform(ks[12], (DEPTH, SSM_HEADS), f32) * (math.log(0.1) - math.log(0.001)) + math.log(0.001))
    return {
        'x': jax.random.normal(ks[0], (BATCH, SEQ, D_MODEL), f32),
        'p': jax.random.normal(ks[1], (DEPTH, BATCH, SEQ, PLE_DIM), f32),
        'positions': (jnp.arange(SEQ, dtype=jnp.int32)[None, :]
                      + jax.random.randint(ks[2], (BATCH, 1), 0, 1024, dtype=jnp.int32)),
        'norm_g': gain(ks[3], (DEPTH, D_MODEL)),
        'w_in': nrm(ks[4], (DEPTH, D_MODEL, IN_TOTAL), D_MODEL ** -0.5),
        'mla_q_norm': gain(ks[5], (DEPTH, MLA_Q_LORA)),
        'w_uq': nrm(ks[6], (DEPTH, MLA_Q_LORA, MLA_HEADS * (MLA_NOPE + MLA_ROPE)), MLA_Q_LORA ** -0.5),
        'mla_kv_norm': gain(ks[7], (DEPTH, MLA_KV_LORA)),
        'w_ukv': nrm(ks[8], (DEPTH, MLA_KV_LORA, MLA_HEADS * (MLA_NOPE + MLA_V)), MLA_KV_LORA ** -0.5),
        'conv_w': nrm(ks[9], (DEPTH, SSM_CONV, SSM_CONV_DIM), SSM_CONV ** -0.5),
        'conv_b': nrm(ks[10], (DEPTH, SSM_CONV_DIM), 0.01),
        'dt_bias': dt0 + jnp.log(-jnp.expm1(-dt0)),
        'a_log': jnp.log(jax.random.uniform(ks[11], (DEPTH, SSM_HEADS), f32, minval=1.0, maxval=16.0)),
        'd_skip': gain(ks[13], (DEPTH, SSM_HEADS)),
        'ssm_norm': gain(ks[14], (DEPTH, SSM_INNER)),
        'w_br_a': nrm(ks[15], (DEPTH, MLA_WIDTH, D_MODEL), MLA_WIDTH ** -0.5),
        'w_br_b': nrm(ks[16], (DEPTH, SSM_INNER, D_MODEL), SSM_INNER ** -0.5),
        'w_br_c': nrm(ks[17], (DEPTH, DSA_WIDTH, D_MODEL), DSA_WIDTH ** -0.5),
        'w_out': nrm(ks[18], (DEPTH, D_MODEL, D_MODEL), D_MODEL ** -0.5),
        'rel_bias': nrm(ks[19], (REL_BUCKETS, DSA_HEADS), 0.5),
        'w_ple': nrm(ks[20], (DEPTH, PLE_DIM, D_MODEL), PLE_DIM ** -0.5),
        'w_ple_gate': nrm(ks[21], (DEPTH, D_MODEL, D_MODEL), D_MODEL ** -0.5),
        'final_norm': gain(ks[22], (D_MODEL,)),
    }


def reference(x, p, positions, norm_g, w_in, mla_q_norm, w_uq, mla_kv_norm, w_ukv, conv_w, conv_b,
              dt_bias, a_log, d_skip, ssm_norm, w_br_a, w_br_b, w_br_c, w_out, rel_bias, w_ple,
              w_ple_gate, final_norm):
    inv_freq = 1.0 / (ROPE_THETA ** (jnp.arange(0, MLA_ROPE, 2, dtype=jnp.float32) / MLA_ROPE))
    ang = positions.astype(jnp.float32)[..., None] * inv_freq
    cos, sin = jnp.cos(ang), jnp.sin(ang)
    split_at = np.cumsum(SPLIT_SIZES)[:-1].tolist()
    for i in range(DEPTH):
        h = rms_norm(x, norm_g[i])
        (c_q, c_kv, k_rope, gate_a, z, xbc, dt_raw, q_c, k_c, v_c, q_idx, k_idx, w_idx, gate_c,
         merge_logits) = jnp.split(h @ w_in[i], split_at, axis=-1)
        o_a = mla_branch(c_q, c_kv, k_rope, mla_q_norm[i], w_uq[i], mla_kv_norm[i], w_ukv[i], cos, sin) * jax.nn.silu(gate_a)
        o_b = ssd_branch(z, xbc, dt_raw, conv_w[i], conv_b[i], dt_bias[i], a_log[i], d_skip[i], ssm_norm[i])
        o_c = dsa_branch(q_c, k_c, v_c, q_idx, k_idx, w_idx, positions, rel_bias) * jax.nn.silu(gate_c)
        g_a, g_b, g_c = jnp.split(jax.nn.sigmoid(merge_logits), N_BRANCHES, axis=-1)
        merged = g_a * (o_a @ w_br_a[i]) + g_b * (o_b @ w_br_b[i]) + g_c * (o_c @ w_br_c[i])
        x = x + merged @ w_out[i]
        x = x + jax.nn.sigmoid(x @ w_ple_gate[i]) * (p[i] @ w_ple[i])
    return rms_norm(x, final_norm)
```

```python
import math
import numpy as np
import concourse.bass as bass
import concourse.mybir as mybir
from concourse.bass_utils import run_bass_kernel_spmd

F32 = mybir.dt.float32
BF16 = mybir.dt.bfloat16
I32 = mybir.dt.int32
AF = mybir.ActivationFunctionType
ALU = mybir.AluOpType
AX = mybir.AxisListType

S = 2048
D = 1024
KC = 8
NTB = 4
TBW = 512
NBLK = 16
DEPTH = 4
EPS = 1e-6
IN_TOTAL = 8568
OFF = {}
_o = 0
for _n, _s in [("c_q", 384), ("c_kv", 256), ("k_rope", 32), ("gate_a", 512), ("z", 1024), ("xbc", 1536),
               ("dt", 16), ("q_c", 512), ("k_c", 64), ("v_c", 64), ("q_idx", 512), ("k_idx", 64), ("w_idx", 8),
               ("gate_c", 512), ("merge", 3072)]:
    OFF[_n] = _o
    _o += _s
assert _o == IN_TOTAL


class Buf:
    __slots__ = ("name", "w", "r")

    def __init__(self, name):
        self.name = name
        self.w = None
        self.r = []


class Prog:
    NDMA = 24

    def __init__(self, nc):
        self.nc = nc
        self.e = dict(pe=nc.tensor, act=nc.scalar, dve=nc.vector, pool=nc.gpsimd, sp=nc.sync)
        self.sem = {}
        for k in ("pe", "act", "dve", "pool"):
            self.sem[k] = nc.alloc_semaphore(name=f"s_{k}")
        for j in range(self.NDMA):
            self.sem[f"d{j}"] = nc.alloc_semaphore(name=f"s_d{j}")
            self.sem[f"g{j}"] = nc.alloc_semaphore(name=f"s_g{j}")
        self.cnt = {k: 0 for k in self.sem}
        self.seen = {k: {} for k in self.e}
        self.dma_rr = {"sp": 0, "pool": 0}
        self.swq = []
        self.SW_DESC_BUDGET = 3072
        self.ninst = {k: 0 for k in self.e}

    def _wait(self, eng, toks):
        need = {}
        for t in toks:
            if t is None:
                continue
            k, v = t
            if v > need.get(k, 0):
                need[k] = v
        for k, v in need.items():
            if self.seen[eng].get(k, 0) < v:
                self.e[eng].wait_ge(self.sem[k], v)
                self.seen[eng][k] = v
                self.ninst[eng] += 1

    @staticmethod
    def _flat(bs):
        out = []
        for b in bs:
            if isinstance(b, (list, tuple)):
                out.extend(Prog._flat(b))
            else:
                out.append(b)
        return out

    @staticmethod
    def _deps(reads, writes):
        reads = Prog._flat(reads)
        writes = Prog._flat(writes)
        toks = []
        for b in reads:
            toks.append(b.w)
        for b in writes:
            toks.append(b.w)
            toks.extend(b.r)
        return toks

    def _mark(self, tok, reads, writes):
        reads = Prog._flat(reads)
        writes = Prog._flat(writes)
        for b in reads:
            b.r.append(tok)
        for b in writes:
            b.w = tok
            b.r = []

    def op(self, eng, fn, reads=(), writes=(), acc=()):
        toks = self._deps(reads, writes)
        acc = Prog._flat(acc)
        for b in acc:
            if b.w is not None and b.w[0] != eng:
                toks.append(b.w)
            toks.extend(b.r)
        self._wait(eng, toks)
        ins = fn(self.e[eng])
        self.cnt[eng] += 1
        ins.then_inc(self.sem[eng], 1)
        self.ninst[eng] += 1
        tok = (eng, self.cnt[eng])
        self._mark(tok, reads, list(writes) + list(acc))
        return tok

    def dma(self, out, in_, reads=(), writes=(), q="sp"):
        j = self.dma_rr[q]
        self.dma_rr[q] = (j + 1) % self.NDMA
        k = ("d" if q == "sp" else "g") + str(j)
        toks = self._deps(reads, writes)
        if self.cnt[k] > 0:
            toks.append((k, self.cnt[k]))
        if q == "pool":
            nd = 1
            for d in tuple(out.shape)[:-1]:
                nd *= int(d)
            nd = max(nd, 128)
            while self.swq and sum(x[1] for x in self.swq) + nd > self.SW_DESC_BUDGET:
                toks.append(self.swq.pop(0)[0])
        self._wait(q, toks)
        self.cnt[k] += 16
        self.e[q].dma_start(out=out, in_=in_).then_inc(self.sem[k], 16)
        self.ninst[q] += 1
        tok = (k, self.cnt[k])
        if q == "pool":
            self.swq.append((tok, nd))
        self._mark(tok, reads, writes)
        return tok

    def barrier(self):
        toks = [(k, v) for k, v in self.cnt.items() if v > 0]
        for eng in self.e:
            self._wait(eng, toks)


class Rot:
    def __init__(self, items):
        self.items = items
        self.i = 0

    def next(self):
        it = self.items[self.i]
        self.i = (self.i + 1) % len(self.items)
        return it


def vt_layout():
    cols = {}
    o = 0
    for name, n in [("norm_g", DEPTH * 8), ("q_norm", DEPTH * 3), ("kv_norm", DEPTH * 2),
                    ("conv_w", DEPTH * 4 * 12), ("conv_b", DEPTH * 12), ("ssm_norm", DEPTH * 8),
                    ("d_skip", DEPTH * 8)]:
        cols[name] = o
        o += n
    return cols, o


def build_program(depth=DEPTH, use_a=True, use_b=True, use_c=True, dbg=()):
    nc = bass.Bass("TRN2", target_bir_lowering=False)
    P = Prog(nc)
    dr = {}

    def din(name, shape, dt=F32):
        dr[name] = nc.dram_tensor(name, shape, dt, kind="ExternalInput").ap()

    def dout(name, shape, dt=F32):
        dr[name] = nc.dram_tensor(name, shape, dt, kind="ExternalOutput").ap()

    VTC, NV = vt_layout()
    din("x", [S, D])
    din("p", [DEPTH, S, 256])
    din("pos", [1, S], I32)
    din("w_in", [DEPTH, D, IN_TOTAL])
    din("w_uq", [DEPTH, 384, 768])
    din("w_ukv", [DEPTH, 256, 1024])
    din("w_br_a", [DEPTH, 512, D])
    din("w_br_b", [DEPTH, 1024, D])
    din("w_br_c", [DEPTH, 512, D])
    din("w_out", [DEPTH, D, D])
    din("w_ple", [DEPTH, 256, D])
    din("w_ple_gate", [DEPTH, D, D])
    din("vt", [128, NV])
    din("fnb", [128, D])
    din("ident", [128, 128])
    din("tri", [128, 128])
    din("ropec", [128, 4])
    din("negm", [128, 128])
    din("negs", [128, 128])
    din("sut", [128, 128])
    din("dt_bias", [DEPTH, 16])
    din("a_log", [DEPTH, 16])
    din("rel_bias", [1, 256])
    dout("y", [S, D])
    if "o_a" in dbg:
        dout("dbg_o_a", [S, 512])
    if "o_c" in dbg:
        dout("dbg_o_c", [S, 512])
    if "o_b" in dbg:
        dout("dbg_o_bT", [1024, S])
    if "sm" in dbg:
        dout("dbg_sm", [S, 32])
        dout("dbg_I", [128, 512])
        dout("dbg_msk", [128, 512])

    base0 = (nc.sbuf_base + 31) // 32 * 32
    slab = nc.alloc_sbuf_tensor("slab", [128, 212800 // 2], BF16)
    cur = [base0]

    def salloc(name, shape, dt, at=None):
        nbytes = int(np.prod(shape[1:])) * (4 if dt in (F32, I32) else 2)
        nbytes = (nbytes + 31) // 32 * 32
        if at is None:
            at = cur[0]
            cur[0] += nbytes
        return nc.alloc_sbuf_tensor_at(name, shape, dt, offset=at)

    xT = salloc("xT", [128, KC, S], F32)
    hT = salloc("hT", [128, KC, S], BF16)
    MT_OFF = cur[0]
    mT = salloc("mT", [128, KC, S], BF16)
    vt = salloc("vt_sb", [128, NV], F32)
    ident = salloc("ident_sb", [128, 128], F32)
    identb = salloc("identb_sb", [128, 128], BF16)
    trib = salloc("trib_sb", [128, 128], BF16)
    ropec = salloc("ropec_sb", [128, 4], F32)
    ones32 = salloc("ones32", [128, 128], F32)
    epsc = salloc("epsc", [128, 1], F32)
    xB = [Buf(f"x{t}") for t in range(NTB)]
    hB = [Buf(f"h{t}") for t in range(NTB)]
    mB = [Buf(f"m{t}") for t in range(NTB)]
    cB = Buf("consts")

    wbufs = []
    for i in range(3):
        t = salloc(f"wbuf{i}", [128, 2048], BF16)
        wbufs.append((t, [Buf(f"wbuf{i}a"), Buf(f"wbuf{i}b")]))
    wpool = Rot(wbufs)
    psl = []
    for i in range(8):
        t = nc.alloc_psum_tensor(f"ps{i}", [128, 512], F32)
        psl.append((t, Buf(f"ps{i}")))
    pspool = Rot(psl[:6])
    accpool = Rot(psl[6:])
    scr = []
    for i in range(3):
        t = salloc(f"scr{i}", [128, 512], F32)
        scr.append((t, Buf(f"scr{i}")))
    scrpool = Rot(scr)
    small = salloc("small", [128, 64], F32)
    smallB = Buf("small")
    EBr = salloc("EBr", [128, 2, 8, 128], BF16)
    negm = salloc("negm", [128, 128], F32)
    tri32 = salloc("tri32", [128, 128], F32)
    negs = salloc("negs", [128, 128], BF16)
    onec = salloc("onec", [128, 1], F32)
    onesb = salloc("onesb", [128, 128], BF16)
    sutb = salloc("sutb", [128, 128], BF16)
    ARENA = cur[0]
    ARENA_SZ = 212800 - (ARENA - base0) - 64
    assert ARENA_SZ >= 53000, ARENA_SZ
    tok = []
    for i in range(2):
        t = salloc(f"tok{i}", [128, 1024], F32, at=ARENA + i * 4096)
        tok.append((t, Buf(f"tok{i}")))
    tokpool = Rot(tok)
    fnb = salloc("fnb_sb", [128, D], F32, at=ARENA + 8192)

    def tbs(tb):
        return slice(tb * TBW, (tb + 1) * TBW)

    def load_w(src2d, K, N):
        buf, B = wpool.next()
        kc = K // 128
        dst = buf[:, :kc * N].rearrange("p (k n) -> p k n", k=kc)
        P.dma(dst, src2d.rearrange("(k p) n -> p k n", p=128), writes=[B], q="pool")
        return dst, B

    def mm_group(out, pairs, reads, writes):
        n = len(pairs)

        def fn(e):
            ins = None
            for i, (l, r) in enumerate(pairs):
                ins = e.matmul(out, l, r, start=(i == 0), stop=(i == n - 1))
            return ins
        return P.op("pe", fn, reads, writes)

    evac_flip = [0]

    def evac_engine():
        evac_flip[0] ^= 1
        return "act" if evac_flip[0] else "dve"

    def copy_op(eng, out, in_, reads, writes):
        if eng == "act":
            return P.op("act", lambda e: e.activation(out=out, in_=in_, func=AF.Copy), reads, writes)
        return P.op(eng, lambda e: e.tensor_copy(out=out, in_=in_), reads, writes)

    P.dma(vt[:], dr["vt"], writes=[cB])
    P.dma(ident[:], dr["ident"], writes=[cB])
    P.dma(identb[:], dr["ident"], writes=[cB], q="pool")
    P.dma(trib[:], dr["tri"], writes=[cB], q="pool")
    P.dma(ropec[:], dr["ropec"], writes=[cB])
    P.dma(tri32[:], dr["tri"], writes=[cB])
    P.dma(sutb[:], dr["sut"], writes=[cB], q="pool")
    P.dma(negs[:], dr["negs"], writes=[cB], q="pool")
    P.op("dve", lambda e: e.memset(onec[:], 1.0), writes=[cB])
    P.op("dve", lambda e: e.memset(onesb[:], 1.0), writes=[cB])
    P.op("dve", lambda e: e.memset(ones32[:], 1.0), writes=[cB])
    P.op("dve", lambda e: e.memset(epsc[:], EPS), writes=[cB])

    for blk in range(NBLK):
        xt, xtB = tokpool.next()
        P.dma(xt[:], dr["x"][blk * 128:(blk + 1) * 128, :], writes=[xtB])
        for half in range(2):
            ps, psB = pspool.next()

            def fn(e, xt=xt, ps=ps, half=half):
                ins = None
                for j in range(4):
                    c = half * 4 + j
                    ins = e.transpose(ps[:, j * 128:(j + 1) * 128], xt[:, c * 128:(c + 1) * 128], ident[:])
                return ins
            P.op("pe", fn, reads=[xtB, cB], writes=[psB])
            copy_op(evac_engine(), xT[:, half * 4:(half + 1) * 4, blk * 128:(blk + 1) * 128],
                    ps[:].rearrange("p (j t) -> p j t", j=4), reads=[psB], writes=[xB[blk // 4]])

    Ct = salloc("Ct", [128, S], BF16, at=ARENA + 40960)
    Sgt = salloc("Sgt", [128, S], BF16, at=ARENA + 45056)
    ropeB = Buf("ropetab")
    dr["ropeC"] = nc.dram_tensor("ropeC", [128, S], BF16, kind="Internal").ap()
    dr["ropeS"] = nc.dram_tensor("ropeS", [128, S], BF16, kind="Internal").ap()
    posi = salloc("posi", [128, S], I32, at=ARENA + 8192)
    angf = salloc("angf", [128, S], F32, at=ARENA + 16384)
    kff = salloc("kff", [128, S], F32, at=ARENA + 24576)
    kii = salloc("kii", [128, S], I32, at=ARENA + 8192)
    rB = Buf("ropetmp")
    P.dma(posi[:], dr["pos"].partition_broadcast(128), writes=[rB])
    P.op("dve", lambda e: e.tensor_copy(out=angf[:], in_=posi[:]), reads=[rB], writes=[rB])
    P.op("dve", lambda e: e.tensor_scalar(out=angf[:], in0=angf[:], scalar1=ropec[:, 0:1], scalar2=None, op0=ALU.mult),
         reads=[rB, cB], writes=[rB])
    TWO_PI = 2.0 * math.pi

    def reduced_sin(out_bf, shift, post_scale_col):
        P.op("dve", lambda e: e.tensor_scalar(out=kff[:], in0=angf[:], scalar1=shift, scalar2=1.0 / TWO_PI,
                                              op0=ALU.add, op1=ALU.mult), reads=[rB], writes=[rB])
        P.op("dve", lambda e: e.tensor_copy(out=kii[:], in_=kff[:]), reads=[rB], writes=[rB])
        P.op("dve", lambda e: e.tensor_copy(out=kff[:], in_=kii[:]), reads=[rB], writes=[rB])
        P.op("dve", lambda e: e.scalar_tensor_tensor(out=kff[:], in0=kff[:], scalar=-TWO_PI, in1=angf[:],
                                                     op0=ALU.mult, op1=ALU.add), reads=[rB], writes=[rB])
        P.op("dve", lambda e: e.tensor_scalar(out=kff[:], in0=kff[:], scalar1=shift, scalar2=None, op0=ALU.add),
             reads=[rB], writes=[rB])
        tmpf = angf2
        P.op("dve", lambda e: e.tensor_scalar(out=tmpf[:], in0=kff[:], scalar1=math.pi, scalar2=TWO_PI,
                                              op0=ALU.is_gt, op1=ALU.mult), reads=[rB], writes=[rB])
        P.op("dve", lambda e: e.tensor_tensor(out=kff[:], in0=kff[:], in1=tmpf[:], op=ALU.subtract), reads=[rB], writes=[rB])
        P.op("dve", lambda e: e.tensor_scalar(out=tmpf[:], in0=kff[:], scalar1=-math.pi, scalar2=TWO_PI,
                                              op0=ALU.is_lt, op1=ALU.mult), reads=[rB], writes=[rB])
        P.op("dve", lambda e: e.tensor_tensor(out=kff[:], in0=kff[:], in1=tmpf[:], op=ALU.add), reads=[rB], writes=[rB])
        P.op("act", lambda e: e.activation(out=kff[:], in_=kff[:], func=AF.Sin), reads=[rB], writes=[rB])
        if post_scale_col is None:
            P.op("dve", lambda e: e.tensor_copy(out=out_bf[:], in_=kff[:]), reads=[rB], writes=[cB])
        else:
            P.op("dve", lambda e: e.tensor_scalar(out=out_bf[:], in0=kff[:], scalar1=post_scale_col, scalar2=None,
                                                  op0=ALU.mult), reads=[rB, cB], writes=[cB])
    angf2 = salloc("angf2", [128, S], F32, at=ARENA + 32768)
    reduced_sin(Ct, math.pi / 2.0, None)
    reduced_sin(Sgt, 0.0, ropec[:, 1:2])
    sB1, sB2 = Buf("ropeCd"), Buf("ropeSd")
    P.dma(dr["ropeC"], Ct[:], reads=[cB], writes=[sB1])
    P.dma(dr["ropeS"], Sgt[:], reads=[cB], writes=[sB2])
    P.barrier()

    def rms_rstd(ps, n_feat, out_rstd, reads, writes):
        P.op("act", lambda e: e.activation(out=out_rstd, in_=ps, func=AF.Sqrt, scale=1.0 / n_feat, bias=epsc[:, 0:1]),
             reads=list(reads) + [cB], writes=writes)
        P.op("dve", lambda e: e.reciprocal(out=out_rstd, in_=out_rstd), reads=writes, writes=writes)


    SCALE_A = 96.0 ** -0.5

    def merge_branch(l, first, K, w_br, oT, oBufs, gate_col0, tb_list=None, local=False):
        kc = K // 128
        for oc in range(8):
            if kc == 8:
                wb, WB0 = load_w(w_br[:, oc * 128:(oc + 1) * 128], K, 128)
                wm, WB1 = load_w(dr["w_in"][l][:, gate_col0 + oc * 128:gate_col0 + (oc + 1) * 128], D, 128)
                WB = [WB0, WB1]
            else:
                buf, WB = wpool.next()
                wb = buf[:, 0:kc * 128].rearrange("p (k n) -> p k n", k=kc)
                wm = buf[:, kc * 128:(kc + 8) * 128].rearrange("p (k n) -> p k n", k=KC)
                P.dma(wb, w_br[:, oc * 128:(oc + 1) * 128].rearrange("(k p) n -> p k n", p=128), writes=[WB[0]], q="pool")
                P.dma(wm, dr["w_in"][l][:, gate_col0 + oc * 128:gate_col0 + (oc + 1) * 128].rearrange("(k p) n -> p k n", p=128),
                      writes=[WB[1]], q="pool")
            for tb in (tb_list if tb_list is not None else range(NTB)):
                osl = slice(0, TBW) if local else tbs(tb)
                psg, psgB = pspool.next()
                mm_group(psg[:], [(wm[:, k, :], hT[:, k, tbs(tb)]) for k in range(KC)], reads=[WB[1], hB[tb]], writes=[psgB])
                sg, sgB = scrpool.next()
                P.op("act", lambda e: e.activation(out=sg[:], in_=psg[:], func=AF.Sigmoid), reads=[psgB], writes=[sgB])
                psy, psyB = pspool.next()
                mm_group(psy[:], [(wb[:, k, :], oT[:, k, osl]) for k in range(kc)], reads=[WB[0]] + oBufs, writes=[psyB])
                if first:
                    P.op("dve", lambda e: e.tensor_tensor(out=mT[:, oc, tbs(tb)], in0=sg[:], in1=psy[:], op=ALU.mult),
                         reads=[sgB, psyB], writes=[mB[tb]])
                else:
                    P.op("dve", lambda e: e.tensor_tensor(out=sg[:], in0=sg[:], in1=psy[:], op=ALU.mult),
                         reads=[sgB, psyB], writes=[sgB])
                    P.op("dve", lambda e: e.tensor_tensor(out=mT[:, oc, tbs(tb)], in0=mT[:, oc, tbs(tb)], in1=sg[:], op=ALU.add),
                         reads=[sgB, mB[tb]], writes=[mB[tb]])

    def norm_proj(l, col0, nfeat, gname, outT, outB):
        nch = nfeat // 128
        gcol = VTC[gname] + l * nch
        ws = []
        for c0 in range(0, nch, 2):
            n = min(2, nch - c0) * 128
            ws.append(load_w(dr["w_in"][l][:, col0 + c0 * 128:col0 + c0 * 128 + n], D, n))
        for tb in range(NTB):
            pss = []
            pq, pqB = pspool.next()
            for c in range(nch):
                w, wB = ws[c // 2]
                ps, psB = pspool.next()
                mm_group(ps[:], [(w[:, k, (c % 2) * 128:(c % 2 + 1) * 128], hT[:, k, tbs(tb)]) for k in range(KC)],
                         reads=[wB, hB[tb]], writes=[psB])
                sq, sqB = scrpool.next()
                P.op("act", lambda e: e.activation(out=sq[:].bitcast(BF16)[:, 0:512], in_=ps[:], func=AF.Square), reads=[psB], writes=[sqB])
                P.op("pe", lambda e: e.matmul(pq[:], onesb[:], sq[:].bitcast(BF16)[:, 0:512], start=(c == 0), stop=(c == nch - 1)),
                     reads=[sqB, cB], acc=[pqB])
                pss.append((ps, psB))
            rs, rsB = scrpool.next()
            rms_rstd(pq[:], nfeat, rs[:], [pqB], [rsB])
            for c in range(nch):
                ps, psB = pss[c]
                P.op("dve", lambda e: e.scalar_tensor_tensor(
                    out=outT[:, c, tbs(tb)], in0=ps[:], scalar=vt[:, gcol + c:gcol + c + 1], in1=rs[:],
                    op0=ALU.mult, op1=ALU.mult), reads=[psB, rsB, cB], writes=[outB])

    def phase_A(l):
        o_tok = salloc("o_tok", [128, NBLK, 512], BF16, at=ARENA)
        oaT = salloc("oaT", [128, 4, S], BF16, at=ARENA + 16384)
        v_aug = salloc("v_aug", [128, NBLK, 66], BF16, at=ARENA + 32768)
        pTs = [(salloc(f"pT{i}", [128, 512], BF16, at=ARENA + 34944 + i * 1024), Buf(f"pT{i}")) for i in range(3)]
        ptpool = Rot(pTs)
        gsl = [(salloc(f"gsil{i}", [128, 512], BF16, at=ARENA + 38016 + i * 1024), Buf(f"gsil{i}")) for i in range(2)]
        gspool = Rot(gsl)
        recs = salloc("recs", [128, 8], F32, at=ARENA + 40064)
        ckvn = salloc("ckvn", [128, 2, S], BF16, at=MT_OFF)
        cqn = salloc("cqn", [128, 3, S], BF16, at=MT_OFF + 8192)
        krT = salloc("krT", [128, S], BF16, at=MT_OFF + 20480)
        kfull = salloc("kfull", [128, S], BF16, at=MT_OFF + 24576)
        qfull = salloc("qfull", [128, S], BF16, at=MT_OFF + 28672)
        otB, oaB, vB, ckvB, cqB, krB, kfB, qfB, recB = [Buf(n) for n in
                                                     ("o_tok", "oaT", "v_aug", "ckvn", "cqn", "krT", "kfull", "qfull", "recs")]
        P.barrier()
        P.dma(Ct[:], dr["ropeC"], writes=[ropeB])
        P.dma(Sgt[:], dr["ropeS"], writes=[ropeB])
        P.op("pool", lambda e: e.memset(v_aug[:, :, 64:66], 1.0), writes=[vB])
        norm_proj(l, OFF["c_kv"], 256, "kv_norm", ckvn, ckvB)
        norm_proj(l, OFF["c_q"], 384, "q_norm", cqn, cqB)
        wbuf, wB = wpool.next()
        wk = wbuf[:, :KC * 96].rearrange("p (k n) -> p k n", k=KC)
        wbuf2, wB2 = wpool.next()
        wk2 = wbuf2[:, :KC * 96].rearrange("p (k n) -> p k n", k=KC)
        src = dr["w_in"][l].rearrange("(k p) n -> p k n", p=128)
        c0 = OFF["k_rope"]
        P.dma(wk[:, :, 64:96], src[:, :, c0:c0 + 32], writes=[wB], q="pool")
        P.dma(wk2[:, :, 64:80], src[:, :, c0 + 16:c0 + 32], writes=[wB2], q="pool")
        P.dma(wk2[:, :, 80:96], src[:, :, c0:c0 + 16], writes=[wB2], q="pool")

        def rope_rows(psA, psAB, psSw, psSwB, out_rows, outB_, tb):
            t1, t1B = scrpool.next()
            t2, t2B = scrpool.next()
            P.op("dve", lambda e: e.tensor_tensor(out=t1[64:96, :], in0=psA[64:96, :], in1=Ct[64:96, tbs(tb)], op=ALU.mult),
                 reads=[psAB, ropeB], writes=[t1B])
            P.op("dve", lambda e: e.tensor_tensor(out=t2[64:96, :], in0=psSw[64:96, :], in1=Sgt[64:96, tbs(tb)], op=ALU.mult),
                 reads=[psSwB, ropeB], writes=[t2B])
            P.op("dve", lambda e: e.tensor_tensor(out=out_rows, in0=t1[64:96, :], in1=t2[64:96, :], op=ALU.add),
                 reads=[t1B, t2B], writes=[outB_])

        for tb in range(NTB):
            psA, psAB = pspool.next()
            mm_group(psA[0:96, :], [(wk[:, k, :], hT[:, k, tbs(tb)]) for k in range(KC)], reads=[wB, hB[tb]], writes=[psAB])
            psS, psSB = pspool.next()
            mm_group(psS[0:96, :], [(wk2[:, k, :], hT[:, k, tbs(tb)]) for k in range(KC)], reads=[wB2, hB[tb]], writes=[psSB])
            rope_rows(psA, psAB, psS, psSB, krT[64:96, tbs(tb)], krB, tb)

        uq = dr["w_uq"][l].rearrange("(k p) n -> p k n", p=128)
        ukv = dr["w_ukv"][l].rearrange("(k p) n -> p k n", p=128)
        def load_head(h):
            b1, b1B = wpool.next()
            wq = b1[:, 0:288].rearrange("p (k n) -> p k n", k=3)
            wqs = b1[:, 288:576].rearrange("p (k n) -> p k n", k=3)
            wkn = b1[:, 576:704].rearrange("p (k n) -> p k n", k=2)
            wv = b1[:, 704:832].rearrange("p (k n) -> p k n", k=2)
            hb = [Buf(f"hw{h}_{i}") for i in range(5)]
            P.dma(wq, uq[:, :, h * 96:(h + 1) * 96], reads=[], writes=[b1B, hb[0]], q="pool")
            P.dma(wqs[:, :, 64:80], uq[:, :, h * 96 + 80:h * 96 + 96], writes=[hb[1]], q="pool")
            P.dma(wqs[:, :, 80:96], uq[:, :, h * 96 + 64:h * 96 + 80], writes=[hb[2]], q="pool")
            P.dma(wkn, ukv[:, :, h * 128:h * 128 + 64], writes=[hb[3]], q="pool")
            P.dma(wv, ukv[:, :, h * 128 + 64:h * 128 + 128], writes=[hb[4]], q="pool")
            return wq, wqs, wkn, wv, [b1B, hb]
        nxt = load_head(0)
        for h in range(8):
            wq, wqs, wkn, wv, b1B = nxt
            if h < 7:
                nxt = load_head(h + 1)
            for tb in range(NTB):
                ps, psB = pspool.next()
                mm_group(ps[0:64, :], [(wkn[:, k, :], ckvn[:, k, tbs(tb)]) for k in range(2)], reads=[b1B, ckvB], writes=[psB])
                copy_op(evac_engine(), kfull[0:64, tbs(tb)], ps[0:64, :], reads=[psB], writes=[kfB])
                copy_op("dve", kfull[64:96, tbs(tb)], krT[64:96, tbs(tb)], reads=[krB], writes=[kfB])
                psA, psAB = pspool.next()
                mm_group(psA[0:96, :], [(wq[:, k, :], cqn[:, k, tbs(tb)]) for k in range(3)], reads=[b1B, cqB], writes=[psAB])
                psS, psSB = pspool.next()
                mm_group(psS[0:96, :], [(wqs[:, k, :], cqn[:, k, tbs(tb)]) for k in range(3)], reads=[b1B, cqB], writes=[psSB])
                copy_op(evac_engine(), qfull[0:64, tbs(tb)], psA[0:64, :], reads=[psAB], writes=[qfB])
                rope_rows(psA, psAB, psS, psSB, qfull[64:96, tbs(tb)], qfB, tb)
            for g4 in range(4):
                ps, psB = pspool.next()

                def fnv(e):
                    ins = None
                    for j in range(4):
                        blk = g4 * 4 + j
                        for k in range(2):
                            ins = e.matmul(ps[:, j * 64:(j + 1) * 64], ckvn[:, k, blk * 128:(blk + 1) * 128], wv[:, k, :],
                                           start=(k == 0), stop=(k == 1))
                    return ins
                P.op("pe", fnv, reads=[b1B, ckvB], writes=[psB])
                copy_op(evac_engine(), v_aug[:, g4 * 4:(g4 + 1) * 4, 0:64], ps[:, 0:256].rearrange("p (j d) -> p j d", j=4),
                        reads=[psB], writes=[vB])
            for Qb in range(4):
                po, poB = accpool.next()
                po3 = po[:].rearrange("p (j d) -> p j d", j=4)
                P.op("dve", lambda e: e.memset(po[:], 0.0), writes=[poB])
                nkb = 4 * Qb + 4
                def pv_a(st):
                    kb, j0, pT, pTB = st

                    def fpv(e):
                        ins = None
                        for j in range(j0, 4):
                            ins = e.matmul(po3[:, j, 0:65], pT[:, (j - j0) * 128:(j - j0 + 1) * 128], v_aug[:, kb, 0:65],
                                           start=False, stop=(kb == 4 * Qb + j), skip_group_check=True)
                        return ins
                    P.op("pe", fpv, reads=[pTB, vB], acc=[poB])
                pend = None
                for kb in range(nkb):
                    j0 = max(0, kb - 4 * Qb)
                    n = 512 - j0 * 128
                    qlo = Qb * 512 + j0 * 128
                    ps, psB = pspool.next()
                    P.op("pe", lambda e: e.matmul(ps[:, 0:n], kfull[0:96, kb * 128:(kb + 1) * 128], qfull[0:96, qlo:qlo + n],
                                                  start=True, stop=True), reads=[kfB, qfB], writes=[psB])
                    pT, pTB = ptpool.next()
                    P.op("act", lambda e: e.activation(out=pT[:, 0:n], in_=ps[:, 0:n], func=AF.Exp, scale=SCALE_A),
                         reads=[psB], writes=[pTB])
                    if kb >= 4 * Qb:
                        P.op("pool", lambda e: e.tensor_tensor(out=pT[:, 0:128], in0=pT[:, 0:128], in1=trib[:], op=ALU.mult),
                             reads=[pTB, cB], writes=[pTB])
                    if pend is not None:
                        pv_a(pend)
                    pend = (kb, j0, pT, pTB)
                pv_a(pend)
                P.op("dve", lambda e: e.reciprocal(out=recs[:, 0:4], in_=po3[:, :, 64]), reads=[poB], writes=[recB])
                for j in range(4):
                    P.op("dve" if j % 2 else "act", (lambda e: e.tensor_scalar(
                        out=o_tok[:, Qb * 4 + j, h * 64:(h + 1) * 64], in0=po3[:, j, 0:64], scalar1=recs[:, j:j + 1],
                        scalar2=None, op0=ALU.mult)) if j % 2 else (lambda e: e.activation(
                            out=o_tok[:, Qb * 4 + j, h * 64:(h + 1) * 64], in_=po3[:, j, 0:64], func=AF.Copy,
                            scale=recs[:, j:j + 1])), reads=[poB, recB], writes=[otB])
        wg0 = load_w(dr["w_in"][l][:, OFF["gate_a"]:OFF["gate_a"] + 256], D, 256)
        wg1 = load_w(dr["w_in"][l][:, OFF["gate_a"] + 256:OFF["gate_a"] + 512], D, 256)
        if "o_a" in dbg and l == 0:
            pass
        for blk in range(NBLK):
            gs, gsB = gspool.next()
            for hf, (wg, wgB) in enumerate((wg0, wg1)):
                ps, psB = pspool.next()
                mm_group(ps[:, 0:256], [(hT[:, k, blk * 128:(blk + 1) * 128], wg[:, k, :]) for k in range(KC)],
                         reads=[wgB, hB[blk // 4]], writes=[psB])
                P.op("act", lambda e: e.activation(out=gs[:, hf * 256:(hf + 1) * 256], in_=ps[:, 0:256], func=AF.Silu),
                     reads=[psB], writes=[gsB])
            P.op("dve", lambda e: e.tensor_tensor(out=o_tok[:, blk, :], in0=o_tok[:, blk, :], in1=gs[:], op=ALU.mult),
                 reads=[gsB, otB], writes=[otB])
            ps, psB = pspool.next()
            psb = ps[:].bitcast(BF16)

            def ftr(e):
                ins = None
                for c in range(4):
                    ins = e.transpose(psb[:, c * 128:(c + 1) * 128], o_tok[:, blk, c * 128:(c + 1) * 128], identb[:])
                return ins
            P.op("pe", ftr, reads=[otB, cB], writes=[psB])
            copy_op(evac_engine(), oaT[:, :, blk * 128:(blk + 1) * 128], psb[:, 0:512].rearrange("p (c t) -> p c t", c=4),
                    reads=[psB], writes=[oaB])
        if "o_a" in dbg and l == 0:
            P.dma(dr["dbg_o_a"].rearrange("(b p) f -> p b f", p=128), o_tok[:], reads=[otB], q="pool")
        P.barrier()
        merge_branch(l, True, 512, dr["w_br_a"][l], oaT, [oaB], OFF["merge"])
        P.barrier()


    def phase_B(l, first):
        a = [ARENA]

        def al(name, shape, dt):
            nb = (int(np.prod(shape[1:])) * (4 if dt in (F32, I32) else 2) + 31) // 32 * 32
            t = salloc(name, shape, dt, at=a[0])
            a[0] += nb
            return t
        xbcs = al("xbcs", [128, 12, TBW], BF16)
        zs = al("zs", [128, 8, TBW], BF16)
        xpre = [(al(f"xpre{i}", [128, 516], F32), Buf(f"xpre{i}")) for i in range(2)]
        xppool = Rot(xpre)
        carry = al("carry", [128, 12, 4], F32)
        state = al("state", [128, 2, 512], F32)
        stbf = al("stbf", [128, 2, 512], BF16)
        sB_ = al("ssm_small", [128, 128], F32)
        sm4 = al("ssm_small4", [128, 8, 64], F32)
        s4B = Buf("ssm_small4")
        Rhl = al("Rhl", [128, 2, 4, 128], BF16)
        dthl = al("dthl", [128, 32], F32)
        dthb = al("dthb", [128, 16], BF16)
        hlB = Buf("dthl")
        decs = [(al(f"dec{i}", [128, 4, 128], BF16), Buf(f"dec{i}")) for i in range(2)]
        decpool = Rot(decs)
        cbT = al("cbT", [128, 2, 128], BF16)
        xdtz = al("xdtz", [128, 16, 128], BF16)
        xdte = al("xdte", [128, 16, 64], BF16)
        Btok = al("Btok", [128, 2, 128], BF16)
        wdt = al("wdt", [128, KC, 16], BF16)
        bcs = al("bcs", [128, 32], F32)
        dtAx = al("dtAx", [128, 8, 64], F32)
        dxB = Buf("dtAx")
        assert a[0] - ARENA <= ARENA_SZ, (a[0] - ARENA, ARENA_SZ)
        xbB, zsB, caB, stB, sbB, smB, RB, cbB, xzB, xeB, btB, wdB, bcB = [Buf(n) for n in
            ("xbcs", "zs", "carry", "state", "stbf", "ssm_small", "Rhl", "cbT", "xdtz", "xdte", "Btok", "wdt", "bcs")]
        DTX, DT, DTA, CC, DTE, CD, NDA, DTD = [i * 16 for i in range(8)]
        P.barrier()
        src = dr["w_in"][l].rearrange("(k p) n -> p k n", p=128)
        P.dma(wdt[:], src[:, :, OFF["dt"]:OFF["dt"] + 16], writes=[wdB], q="pool")
        P.dma(bcs[:, 0:16], dr["dt_bias"][l:l + 1, :].partition_broadcast(128), writes=[bcB])
        P.dma(bcs[:, 16:32], dr["a_log"][l:l + 1, :].partition_broadcast(128), writes=[bcB])
        P.op("act", lambda e: e.activation(out=bcs[:, 16:32], in_=bcs[:, 16:32], func=AF.Exp), reads=[bcB], writes=[bcB])
        P.op("dve", lambda e: e.tensor_scalar(out=bcs[:, 16:32], in0=bcs[:, 16:32], scalar1=-1.0, scalar2=None, op0=ALU.mult),
             reads=[bcB], writes=[bcB])
        P.op("pool", lambda e: e.memset(state[:], 0.0), writes=[stB])
        P.op("pool", lambda e: e.memset(stbf[:], 0.0), writes=[sbB])
        P.op("pool", lambda e: e.memset(xdtz[:], 0.0), writes=[xzB])
        P.op("pool", lambda e: e.memset(carry[:], 0.0), writes=[caB])
        P.op("pool", lambda e: e.memset(sm4[:], 0.0), writes=[s4B])
        cw = VTC["conv_w"] + l * 48
        cbcol = VTC["conv_b"] + l * 12
        dsk = VTC["d_skip"] + l * 8
        gsn = VTC["ssm_norm"] + l * 8
        for tb in range(NTB):
            ps, psB = pspool.next()
            for ch in range(4):
                gsl4 = slice((tb * 4 + ch) * 128, (tb * 4 + ch + 1) * 128)
                mm_group(ps[:, ch * 16:(ch + 1) * 16], [(hT[:, k, gsl4], wdt[:, k, :]) for k in range(KC)], reads=[wdB, hB[tb]], writes=[psB])
            Q = lambda i: sm4[:, i, :]
            P.op("dve", lambda e: e.tensor_tensor(out=Q(0).rearrange("p (c h) -> p c h", c=4), in0=ps[:, 0:64].rearrange("p (c h) -> p c h", c=4),
                                                  in1=bcs[:, 0:16].unsqueeze(1).broadcast_to([128, 4, 16]), op=ALU.add),
                 reads=[psB, bcB], writes=[s4B])
            P.op("act", lambda e: e.activation(out=Q(0), in_=Q(0), func=AF.Exp), reads=[s4B], writes=[s4B])
            P.op("act", lambda e: e.activation(out=Q(1), in_=Q(0), func=AF.Ln, bias=onec[:, 0:1]), reads=[s4B, cB], writes=[s4B])
            P.op("dve", lambda e: e.tensor_tensor(out=Q(2).rearrange("p (c h) -> p c h", c=4), in0=Q(1).rearrange("p (c h) -> p c h", c=4),
                                                  in1=bcs[:, 16:32].unsqueeze(1).broadcast_to([128, 4, 16]), op=ALU.mult),
                 reads=[s4B, bcB], writes=[s4B])
            psc, pscB = pspool.next()
            P.op("pe", lambda e: e.matmul(psc[:, 0:64], tri32[:], Q(2), start=True, stop=True), reads=[s4B, cB], writes=[pscB])
            pst, pstB = pspool.next()
            P.op("pe", lambda e: e.matmul(pst[:, 0:64], ones32[:], Q(2), start=True, stop=True), reads=[s4B, cB], writes=[pstB])
            copy_op("dve", Q(3), psc[:, 0:64], reads=[pscB], writes=[s4B])
            P.op("dve", lambda e: e.tensor_tensor(out=Q(4), in0=pst[:, 0:64], in1=Q(3), op=ALU.subtract), reads=[pstB, s4B], writes=[s4B])
            P.op("act", lambda e: e.activation(out=Q(4), in_=Q(4), func=AF.Exp), reads=[s4B], writes=[s4B])
            P.op("act", lambda e: e.activation(out=Q(5), in_=pst[:, 0:64], func=AF.Exp), reads=[pstB, s4B], writes=[s4B])
            P.op("dve", lambda e: e.tensor_tensor(out=Q(7), in0=Q(1), in1=Q(4), op=ALU.mult), reads=[s4B], writes=[s4B])
            for c2 in range(4):
                w_, wB_ = load_w(dr["w_in"][l][:, OFF["z"] + c2 * 256:OFF["z"] + (c2 + 1) * 256], D, 256)
                for sub in range(2):
                    c = c2 * 2 + sub
                    ps, psB = pspool.next()
                    mm_group(ps[:], [(w_[:, k, sub * 128:(sub + 1) * 128], hT[:, k, tbs(tb)]) for k in range(KC)],
                             reads=[wB_, hB[tb]], writes=[psB])
                    P.op("act", lambda e: e.activation(out=zs[:, c, :], in_=ps[:], func=AF.Silu), reads=[psB], writes=[zsB])
            def conv_s1(c, w_, wB_, sub):
                ps, psB = pspool.next()
                mm_group(ps[:], [(w_[:, k, sub * 128:(sub + 1) * 128], hT[:, k, tbs(tb)]) for k in range(KC)],
                         reads=[wB_, hB[tb]], writes=[psB])
                xp, xpB = xppool.next()
                P.op("act", lambda e: e.activation(out=xp[:, 3:515], in_=ps[:], func=AF.Copy), reads=[psB], writes=[xpB])
                P.op("act", lambda e: e.activation(out=xp[:, 0:3], in_=carry[:, c, 0:3], func=AF.Copy), reads=[caB], writes=[xpB])
                P.op("act", lambda e: e.activation(out=carry[:, c, 0:3], in_=xp[:, 512:515], func=AF.Copy), reads=[xpB], writes=[caB])
                return (c, xp, xpB)

            def conv_s2(st):
                c, xp, xpB = st
                acc, accB = scrpool.next()
                P.op("dve", lambda e: e.tensor_scalar(out=acc[:], in0=xp[:, 3:515], scalar1=vt[:, cw + 36 + c:cw + 36 + c + 1],
                                                      scalar2=vt[:, cbcol + c:cbcol + c + 1], op0=ALU.mult, op1=ALU.add),
                     reads=[xpB, cB], writes=[accB])
                for j in range(3):
                    P.op("dve", lambda e: e.scalar_tensor_tensor(out=acc[:], in0=xp[:, j:j + 512],
                                                                 scalar=vt[:, cw + j * 12 + c:cw + j * 12 + c + 1], in1=acc[:],
                                                                 op0=ALU.mult, op1=ALU.add), reads=[xpB, accB, cB], writes=[accB])
                P.op("act", lambda e: e.activation(out=xbcs[:, c, :], in_=acc[:], func=AF.Silu), reads=[accB], writes=[xbB])
            pend = None
            for c2 in range(6):
                w_, wB_ = load_w(dr["w_in"][l][:, OFF["xbc"] + c2 * 256:OFF["xbc"] + (c2 + 1) * 256], D, 256)
                for sub in range(2):
                    st = conv_s1(c2 * 2 + sub, w_, wB_, sub)
                    if pend is not None:
                        conv_s2(pend)
                    pend = st
            conv_s2(pend)
            for ch in range(4):
                cg = tb * 4 + ch
                lsl = slice(ch * 128, (ch + 1) * 128)
                gsl = slice(cg * 128, (cg + 1) * 128)
                sm = sB_
                P.op("dve", lambda e: e.tensor_copy(out=sm[:].rearrange("p (q h) -> p q h", q=8), in_=sm4[:, :, ch * 16:(ch + 1) * 16]),
                     reads=[s4B], writes=[smB])
                ps, psB = pspool.next()
                psb = ps[:].bitcast(BF16)

                def ftr(e):
                    ins = None
                    for c in range(8):
                        ins = e.transpose(psb[:, c * 128:(c + 1) * 128], xbcs[:, c, lsl], identb[:])
                    return ins
                P.op("pe", ftr, reads=[xbB, cB], writes=[psB])
                psb4 = psb.rearrange("p (j two d) -> p j two d", two=2, d=64)
                xz4 = xdtz[:].rearrange("p (j two) d -> p j two d", two=2)
                for par in range(2):
                    P.op("dve", lambda e: e.tensor_tensor(
                        out=xz4[:, :, par, par * 64:(par + 1) * 64], in0=psb4[:, :, par, :],
                        in1=sm[:, DT:DT + 16].rearrange("p (j two) -> p j two", two=2)[:, :, par].unsqueeze(2).broadcast_to([128, 8, 64]),
                        op=ALU.mult), reads=[psB, smB], writes=[xzB])
                P.op("dve", lambda e: e.tensor_tensor(out=xdte[:], in0=psb.rearrange("p (h d) -> p h d", d=64),
                                                      in1=sm[:, DTD:DTD + 16].unsqueeze(2).broadcast_to([128, 16, 64]), op=ALU.mult),
                     reads=[psB, smB], writes=[xeB])
                ps, psB = pspool.next()
                psb = ps[:].bitcast(BF16)

                def ftb(e):
                    ins = None
                    for g in range(2):
                        ins = e.transpose(psb[:, g * 128:(g + 1) * 128], xbcs[:, 8 + g, lsl], identb[:])
                    return ins
                P.op("pe", ftb, reads=[xbB, cB], writes=[psB])
                copy_op("act", Btok[:].rearrange("p g n -> p (g n)"), psb[:, 0:256], reads=[psB], writes=[btB])
                ps, psB = pspool.next()

                def fcb(e):
                    ins = None
                    for g in range(2):
                        ins = e.matmul(ps[:, g * 128:(g + 1) * 128], xbcs[:, 8 + g, lsl], xbcs[:, 10 + g, lsl], start=True, stop=True)
                    return ins
                P.op("pe", fcb, reads=[xbB], writes=[psB])
                copy_op("act", cbT[:].rearrange("p g n -> p (g n)"), ps[:, 0:256], reads=[psB], writes=[cbB])
                ETs = []
                for half in range(2):
                    ps, psB = pspool.next()
                    P.op("dve", lambda e: e.tensor_copy(out=dtAx[:], in_=sm[:, DTA + 8 * half:DTA + 8 * half + 8].unsqueeze(2).broadcast_to([128, 8, 64])),
                         reads=[smB], writes=[dxB])

                    def fet(e):
                        ins = None
                        for cc in range(4):
                            ins = e.matmul(ps[:, cc * 128:(cc + 1) * 128],
                                           dtAx[:, 2 * cc:2 * cc + 2, :].rearrange("p h d -> p (h d)"),
                                           tri32[:], start=True, stop=True)
                        return ins
                    P.op("pe", fet, reads=[dxB, cB], writes=[psB])
                    ET, ETB = scrpool.next()
                    P.op("act", lambda e: e.activation(out=ET[:], in_=ps[:], func=AF.Exp), reads=[psB], writes=[ETB])
                    pso, psoB = pspool.next()

                    def fyo(e):
                        ins = None
                        for cc in range(4):
                            c = half * 4 + cc
                            ins = e.matmul(pso[:, cc * 128:(cc + 1) * 128], stbf[:, c // 4, (c % 4) * 128:(c % 4 + 1) * 128],
                                           xbcs[:, 10 + c // 4, lsl], start=True, stop=True)
                        return ins
                    P.op("pe", fyo, reads=[sbB, xbB], writes=[psoB])
                    P.op("dve", lambda e: e.tensor_tensor(out=ET[:], in0=pso[:], in1=ET[:], op=ALU.mult), reads=[psoB, ETB], writes=[ETB])
                    ETs.append((ET, ETB))
                P.op("dve", lambda e: e.tensor_copy(out=dthb[:], in_=sm[:, DTA:DTA + 16]), reads=[smB], writes=[hlB])
                P.op("dve", lambda e: e.tensor_copy(out=dthl[:, 0:16], in_=dthb[:]), reads=[hlB], writes=[hlB])
                P.op("dve", lambda e: e.tensor_tensor(out=dthl[:, 16:32], in0=sm[:, DTA:DTA + 16], in1=dthl[:, 0:16], op=ALU.subtract),
                     reads=[smB, hlB], writes=[hlB])

                def dec_s1(q4):
                    for hl in range(2):
                        P.op("dve", lambda e: e.tensor_tensor(
                            out=Rhl[:, hl, :, :], in0=dthl[:, hl * 16 + 4 * q4:hl * 16 + 4 * q4 + 4].unsqueeze(2).broadcast_to([128, 4, 128]),
                            in1=tri32[:].unsqueeze(1).broadcast_to([128, 4, 128]), op=ALU.mult), reads=[hlB, cB], writes=[RB])
                    psx, psxB = pspool.next()

                    def fdec(e):
                        e.matmul(psx[:], sutb[:], Rhl[:, 0, :, :].rearrange("p j l -> p (j l)"), start=True, stop=False)
                        e.matmul(psx[:], sutb[:], Rhl[:, 1, :, :].rearrange("p j l -> p (j l)"), start=False, stop=False)
                        return e.matmul(psx[:], identb[:], negs[:].unsqueeze(1).broadcast_to([128, 4, 128]), start=False, stop=True)
                    P.op("pe", fdec, reads=[RB, cB], writes=[psxB])
                    dec, decB = decpool.next()
                    P.op("act", lambda e: e.activation(out=dec[:].rearrange("p j l -> p (j l)"), in_=psx[:], func=AF.Exp),
                         reads=[psxB], writes=[decB])
                    return (q4, dec, decB)

                def dec_s2(st, psd, psdB):
                    q4, dec, decB = st
                    g = q4 // 2
                    P.op("dve", lambda e: e.tensor_tensor(out=dec[:], in0=dec[:], in1=cbT[:, g:g + 1, :].broadcast_to([128, 4, 128]),
                                                          op=ALU.mult), reads=[decB, cbB], writes=[decB])

                    def fyd(e):
                        ins = None
                        for cc2 in range(2):
                            c = q4 * 2 + cc2
                            col = (c % 4) * 128
                            e.matmul(psd[:, col:col + 128], xdtz[:, 2 * c, :], dec[:, (2 * c) % 4, :], start=True, stop=False)
                            ins = e.matmul(psd[:, col:col + 128], xdtz[:, 2 * c + 1, :], dec[:, (2 * c + 1) % 4, :], start=False, stop=True)
                        return ins
                    P.op("pe", fyd, reads=[xzB, decB], acc=[psdB])

                def y_tail(half, psd, psdB):
                    ET, ETB = ETs[half]
                    P.op("dve", lambda e: e.tensor_tensor(out=ET[:], in0=psd[:], in1=ET[:], op=ALU.add), reads=[psdB, ETB], writes=[ETB])
                    t2, t2B = scrpool.next()
                    t23 = t2[:].rearrange("p (c t) -> p c t", c=4)
                    P.op("dve", lambda e: e.tensor_tensor(
                        out=t23, in0=xbcs[:, half * 4:(half + 1) * 4, lsl],
                        in1=vt[:, dsk + half * 4:dsk + half * 4 + 4].unsqueeze(2).broadcast_to([128, 4, 128]), op=ALU.mult),
                        reads=[xbB, cB], writes=[t2B])
                    P.op("dve", lambda e: e.tensor_tensor(out=t2[:], in0=t2[:], in1=ET[:], op=ALU.add), reads=[t2B, ETB], writes=[t2B])
                    P.op("dve", lambda e: e.tensor_tensor(out=zs[:, half * 4:(half + 1) * 4, lsl], in0=zs[:, half * 4:(half + 1) * 4, lsl],
                                                           in1=t23, op=ALU.mult), reads=[t2B, zsB], writes=[zsB])
                psds = [pspool.next(), pspool.next()]
                pend = dec_s1(0)
                for q4 in range(4):
                    nx = dec_s1(q4 + 1) if q4 < 3 else None
                    dec_s2(pend, *psds[q4 // 2])
                    if q4 % 2 == 1:
                        y_tail(q4 // 2, *psds[q4 // 2])
                    pend = nx
                for g in range(2):
                    pss, pssB = pspool.next()
                    P.op("pe", lambda e: e.matmul(pss[:], Btok[:, g, :], xdte[:, 8 * g:8 * g + 8, :].rearrange("p h d -> p (h d)"),
                                                  start=True, stop=True), reads=[btB, xeB], writes=[pssB])
                    st3 = state[:, g, :].rearrange("p (h d) -> p h d", d=64)
                    P.op("dve", lambda e: e.tensor_tensor(out=st3, in0=st3,
                                                          in1=sm[:, CD + 8 * g:CD + 8 * g + 8].unsqueeze(2).broadcast_to([128, 8, 64]),
                                                          op=ALU.mult), reads=[stB, smB], writes=[stB])
                    P.op("dve", lambda e: e.tensor_tensor(out=state[:, g, :], in0=state[:, g, :], in1=pss[:], op=ALU.add),
                         reads=[stB, pssB], writes=[stB])
                    copy_op("act", stbf[:, g, :], state[:, g, :], reads=[stB], writes=[sbB])
            pq, pqB = pspool.next()
            for c in range(8):
                sq, sqB = scrpool.next()
                P.op("act", lambda e: e.activation(out=sq[:].bitcast(BF16)[:, 0:512], in_=zs[:, c, :], func=AF.Square), reads=[zsB], writes=[sqB])
                P.op("pe", lambda e: e.matmul(pq[:], onesb[:], sq[:].bitcast(BF16)[:, 0:512], start=(c == 0), stop=(c == 7)), reads=[sqB, cB], acc=[pqB])
            rs, rsB = scrpool.next()
            rms_rstd(pq[:], 1024, rs[:], [pqB], [rsB])
            for c in range(8):
                P.op("dve", lambda e: e.scalar_tensor_tensor(out=zs[:, c, :], in0=zs[:, c, :], scalar=vt[:, gsn + c:gsn + c + 1], in1=rs[:],
                                                             op0=ALU.mult, op1=ALU.mult), reads=[zsB, rsB, cB], writes=[zsB])
            if "o_b" in dbg and l == 0:
                for c in range(8):
                    P.dma(dr["dbg_o_bT"][c * 128:(c + 1) * 128, tbs(tb)], zs[:, c, :], reads=[zsB], q="pool")
            merge_branch(l, first, 1024, dr["w_br_b"][l], zs, [zsB], OFF["merge"] + D, tb_list=[tb], local=True)
        P.barrier()

    SCALE_C = 64.0 ** -0.5
    import os as _os
    NBIS = int(_os.environ.get("DSA_NBIS", "12"))
    NEG = -1.0e30

    def t5_thresholds():
        n = np.arange(0, 400)
        nf = np.maximum(n, 16).astype(np.float32)
        lr = (np.log(nf / np.float32(16.0)) / np.float32(math.log(128 / 16))).astype(np.float32)
        large = np.minimum(16 + (lr * np.float32(16.0)).astype(np.int32), 31)
        bucket = np.where(n < 16, n, large)
        return [int(np.argmax(bucket >= j)) for j in range(1, 32)]

    ebB = Buf("EBr")

    def setup_bias():
        base = ARENA + 16384
        posr_i = salloc("posr_i", [128, 256], I32, at=base)
        posc_i = salloc("posc_i", [128, 1], I32, at=base + 1024)
        posr = salloc("posr", [128, 256], F32, at=base + 2048)
        posc = salloc("posc", [128, 1], F32, at=base + 3072)
        Dd = salloc("Dd", [128, 2, 128], F32, at=base + 4096)
        ind = salloc("ind", [128, 2, 128], F32, at=base + 5120)
        accb = salloc("accb", [128, 2, 8, 128], F32, at=base + 6144)
        rbb = salloc("rbb", [128, 256], F32, at=base + 14336)
        dl = salloc("dl", [128, 248], F32, at=base + 15360)
        bs = salloc("bs", [128, 8], F32, at=base + 16384)
        tB = Buf("biastmp")
        P.dma(posr_i[:], dr["pos"][:, 0:256].partition_broadcast(128), writes=[tB])
        P.dma(posc_i[:], dr["pos"][0, 0:128].rearrange("(p o) -> p o", o=1), writes=[tB])
        P.dma(rbb[:], dr["rel_bias"].partition_broadcast(128), writes=[tB])
        P.dma(negm[:], dr["negm"], writes=[cB])
        P.op("dve", lambda e: e.tensor_copy(out=posr[:], in_=posr_i[:]), reads=[tB], writes=[tB])
        P.op("dve", lambda e: e.tensor_copy(out=posc[:], in_=posc_i[:]), reads=[tB], writes=[tB])
        P.op("dve", lambda e: e.tensor_scalar(out=Dd[:].rearrange("p a t -> p (a t)"), in0=posr[:], scalar1=posc[:, 0:1],
                                              scalar2=None, op0=ALU.subtract), reads=[tB], writes=[tB])
        P.op("dve", lambda e: e.tensor_tensor(out=dl[:], in0=rbb[:, 8:256], in1=rbb[:, 0:248], op=ALU.subtract), reads=[tB], writes=[tB])
        P.op("dve", lambda e: e.tensor_tensor(out=bs[:], in0=rbb[:, 0:8], in1=rbb[:, 248:256], op=ALU.subtract), reads=[tB], writes=[tB])
        P.op("dve", lambda e: e.memset(accb[:], 0.0), writes=[tB])
        for j, T in enumerate(t5_thresholds()):
            P.op("dve", lambda e: e.tensor_scalar(out=ind[:], in0=Dd[:], scalar1=float(T) - 0.5, scalar2=None, op0=ALU.is_ge),
                 reads=[tB], writes=[tB])
            for h in range(8):
                P.op("dve", lambda e: e.scalar_tensor_tensor(out=accb[:, :, h, :], in0=ind[:], scalar=dl[:, j * 8 + h:j * 8 + h + 1],
                                                             in1=accb[:, :, h, :], op0=ALU.mult, op1=ALU.add), reads=[tB], writes=[tB])
        for h in range(8):
            P.op("act", lambda e: e.activation(out=EBr[:, :, (h // 2) + 4 * (h % 2), :], in_=accb[:, :, h, :], func=AF.Exp, bias=bs[:, h:h + 1]),
                 reads=[tB], writes=[ebB])
        P.barrier()

    class _Stop(Exception):
        pass
    DSA_STOP = int(_os.environ.get("DSA_STOP", "0"))

    def stop_at(n):
        if DSA_STOP == n:
            raise _Stop()

    def phase_C(l, first):
        try:
            phase_C_inner(l, first)
        except _Stop:
            pass
        P.barrier()

    def phase_C_inner(l, first):
        a = [ARENA]

        def al(name, shape, dt):
            nb = (int(np.prod(shape[1:])) * (4 if dt in (F32, I32) else 2) + 31) // 32 * 32
            t = salloc(name, shape, dt, at=a[0])
            a[0] += nb
            return t
        kc2 = al("kc2", [128, S], BF16)
        kidx2 = al("kidx2", [128, S], BF16)
        vc = al("vc_aug", [128, NBLK, 66], BF16)
        qcT = al("qcT", [128, 4, TBW], BF16)
        qiT = al("qiT", [128, 4, TBW], BF16)
        Isc = al("Isc", [128, S], F32)
        msk = al("msk", [128, S], BF16)
        mskT2 = [al(f"mskT{i}", [128, NBLK, 128], BF16) for i in range(2)]
        mtB2 = [Buf("mskT0"), Buf("mskT1")]
        recC = al("recC", [128, 8], F32)
        rcB = Buf("recC")
        PTs = [(al(f"PT{i}", [128, 8, 128], BF16), Buf(f"PT{i}")) for i in range(2)]
        PTpool = Rot(PTs)
        ocT = al("ocT", [128, 4, TBW], BF16)
        oblk = al("oblk", [128, 512], BF16)
        gsc = al("gsc", [128, 512], BF16)
        relupool = scrpool
        sm = al("smC", [128, 32], F32)
        widx = al("widx", [128, NBLK, 8], F32)
        assert a[0] - ARENA <= ARENA_SZ, (a[0] - ARENA, ARENA_SZ)
        kcB, kiB, vcB, qcB, qiB, IB, mkB, ocB, obB, gsB, smB, wxB = [Buf(n) for n in
            ("kc2", "kidx2", "vc", "qcT", "qiT", "Isc", "msk", "ocT", "oblk", "gsc", "smC", "widx")]
        WI, LO, W0, MID, CNT, G, REC = 0, 8, 9, 10, 11, 12, 16
        P.barrier()
        src = dr["w_in"][l].rearrange("(k p) n -> p k n", p=128)
        b1, b1B = wpool.next()
        wkc = b1[:, 0:1024].rearrange("p (k n) -> p k n", k=KC)
        wki = b1[:, 1024:2048].rearrange("p (k n) -> p k n", k=KC)
        for hh in range(2):
            P.dma(wkc[:, :, hh * 64:(hh + 1) * 64], src[:, :, OFF["k_c"]:OFF["k_c"] + 64], writes=[b1B], q="pool")
            P.dma(wki[:, :, hh * 64:(hh + 1) * 64], src[:, :, OFF["k_idx"]:OFF["k_idx"] + 64], writes=[b1B], q="pool")
        b2, b2B = wpool.next()
        wvc = b2[:, 0:512].rearrange("p (k n) -> p k n", k=KC)
        wwi = b2[:, 512:576].rearrange("p (k n) -> p k n", k=KC)
        P.dma(wvc, src[:, :, OFF["v_c"]:OFF["v_c"] + 64], writes=[b2B], q="pool")
        P.dma(wwi, src[:, :, OFF["w_idx"]:OFF["w_idx"] + 8], writes=[b2B], q="pool")
        P.op("pool", lambda e: e.memset(vc[:, :, 64:66], 1.0), writes=[vcB])
        P.op("pool", lambda e: e.memset(sm[:], 0.0), writes=[smB])
        for tb in range(NTB):
            for (w_, dst, dB) in ((wkc, kc2, kcB), (wki, kidx2, kiB)):
                ps, psB = pspool.next()
                mm_group(ps[:], [(w_[:, k, :], hT[:, k, tbs(tb)]) for k in range(KC)], reads=[b1B, hB[tb]], writes=[psB])
                copy_op(evac_engine(), dst[:, tbs(tb)], ps[:], reads=[psB], writes=[dB])
            ps, psB = pspool.next()

            def fnv(e):
                ins = None
                for j in range(4):
                    blk = tb * 4 + j
                    for k in range(KC):
                        ins = e.matmul(ps[:, j * 64:(j + 1) * 64], hT[:, k, blk * 128:(blk + 1) * 128], wvc[:, k, :],
                                       start=(k == 0), stop=(k == KC - 1))
                return ins
            P.op("pe", fnv, reads=[b2B, hB[tb]], writes=[psB])
            copy_op(evac_engine(), vc[:, tb * 4:(tb + 1) * 4, 0:64], ps[:, 0:256].rearrange("p (j d) -> p j d", j=4),
                    reads=[psB], writes=[vcB])
            ps, psB = pspool.next()

            def fnw(e):
                ins = None
                for j in range(4):
                    blk = tb * 4 + j
                    for k in range(KC):
                        ins = e.matmul(ps[:, j * 8:(j + 1) * 8], hT[:, k, blk * 128:(blk + 1) * 128], wwi[:, k, :],
                                       start=(k == 0), stop=(k == KC - 1))
                return ins
            P.op("pe", fnw, reads=[b2B, hB[tb]], writes=[psB])
            copy_op("dve", widx[:, tb * 4:(tb + 1) * 4, :], ps[:, 0:32].rearrange("p (j d) -> p j d", j=4), reads=[psB], writes=[wxB])
        stop_at(1)
        for tb in range(NTB):
            for (name, dstT, dB) in (("q_c", qcT, qcB), ("q_idx", qiT, qiB)):
                for c2 in range(2):
                    w_, wB_ = load_w(dr["w_in"][l][:, OFF[name] + c2 * 256:OFF[name] + (c2 + 1) * 256], D, 256)
                    for sub in range(2):
                        ps, psB = pspool.next()
                        mm_group(ps[:], [(w_[:, k, sub * 128:(sub + 1) * 128], hT[:, k, tbs(tb)]) for k in range(KC)],
                                 reads=[wB_, hB[tb]], writes=[psB])
                        copy_op(evac_engine(), dstT[:, c2 * 2 + sub, :], ps[:], reads=[psB], writes=[dB])
            wg0 = load_w(dr["w_in"][l][:, OFF["gate_c"]:OFF["gate_c"] + 256], D, 256)
            wg1 = load_w(dr["w_in"][l][:, OFF["gate_c"] + 256:OFF["gate_c"] + 512], D, 256)
            def qblock(qi):
                mskT = mskT2[qi % 2]
                mtB = mtB2[qi % 2]
                qb = tb * 4 + qi
                nk = (qb + 1) * 128
                tl = slice(qi * 128, (qi + 1) * 128)
                for k0 in range(0, nk, 512):
                    n = min(512, nk - k0)
                    for h in range(8):
                        hp = slice((h % 2) * 64, (h % 2) * 64 + 64)
                        ps, psB = pspool.next()
                        P.op("pe", lambda e: e.matmul(ps[:, 0:n], qiT[hp, h // 2, tl], kidx2[hp, k0:k0 + n], start=True, stop=True),
                             reads=[qiB, kiB], writes=[psB])
                        if h == 0:
                            P.op("dve", lambda e: e.tensor_scalar(out=Isc[:, k0:k0 + n], in0=ps[:, 0:n], scalar1=0.0,
                                                                  scalar2=widx[:, qb, 0:1], op0=ALU.max, op1=ALU.mult),
                                 reads=[psB, wxB], writes=[IB])
                        else:
                            rl, rlB = relupool.next()
                            P.op("act", lambda e: e.activation(out=rl[:, 0:n], in_=ps[:, 0:n], func=AF.Relu), reads=[psB], writes=[rlB])
                            P.op("dve", lambda e: e.scalar_tensor_tensor(out=Isc[:, k0:k0 + n], in0=rl[:, 0:n],
                                                                         scalar=widx[:, qb, h:h + 1], in1=Isc[:, k0:k0 + n],
                                                                         op0=ALU.mult, op1=ALU.add), reads=[rlB, wxB, IB], writes=[IB])
                yield
                if qb >= 2:
                    P.op("dve", lambda e: e.tensor_reduce(out=sm[:, LO:LO + 1], in_=Isc[:, 0:nk], axis=AX.X, op=ALU.min),
                         reads=[IB], writes=[smB])
                    P.op("dve", lambda e: e.tensor_reduce(out=sm[:, W0:W0 + 1], in_=Isc[:, 0:nk - 128], axis=AX.X, op=ALU.max),
                         reads=[IB], writes=[smB])
                P.op("dve", lambda e: e.tensor_tensor(out=Isc[:, nk - 128:nk], in0=Isc[:, nk - 128:nk], in1=negm[:], op=ALU.add),
                     reads=[IB, cB], writes=[IB])
                if qb >= 2:
                    rl, rlB = relupool.next()
                    P.op("dve", lambda e: e.tensor_reduce(out=rl[:, 0:1], in_=Isc[:, nk - 128:nk], axis=AX.X, op=ALU.max),
                         reads=[IB], writes=[rlB])
                    P.op("dve", lambda e: e.tensor_tensor(out=sm[:, W0:W0 + 1], in0=sm[:, W0:W0 + 1], in1=rl[:, 0:1], op=ALU.max),
                         reads=[rlB, smB], writes=[smB])
                    P.op("dve", lambda e: e.tensor_tensor(out=sm[:, W0:W0 + 1], in0=sm[:, W0:W0 + 1], in1=sm[:, LO:LO + 1], op=ALU.subtract),
                         reads=[smB], writes=[smB])
                    midB, midmB, cntB, gB = Buf("mid"), Buf("midm"), Buf("cnt"), Buf("g")
                    MIDM = 13
                    P.op("dve", lambda e: e.scalar_tensor_tensor(out=sm[:, MID:MID + 1], in0=sm[:, W0:W0 + 1], scalar=0.5,
                                                                 in1=sm[:, LO:LO + 1], op0=ALU.mult, op1=ALU.add),
                         reads=[smB], writes=[midB])
                    for it in range(NBIS):
                        q = 0.5 ** (it + 2)
                        P.op("dve", lambda e: e.scalar_tensor_tensor(out=sm[:, MIDM:MIDM + 1], in0=sm[:, W0:W0 + 1], scalar=-q,
                                                                     in1=sm[:, MID:MID + 1], op0=ALU.mult, op1=ALU.add),
                             reads=[smB, midB], writes=[midmB])
                        P.op("dve", lambda e: e.tensor_scalar(out=msk[:, 0:nk], in0=Isc[:, 0:nk], scalar1=sm[:, MID:MID + 1], scalar2=0.0,
                                                              op0=ALU.is_ge, op1=ALU.add, accum_out=sm[:, CNT:CNT + 1]),
                             reads=[IB, midB], writes=[mkB, cntB])
                        P.op("dve", lambda e: e.tensor_scalar(out=sm[:, G:G + 1], in0=sm[:, CNT:CNT + 1], scalar1=255.5, scalar2=2.0 * q,
                                                              op0=ALU.is_ge, op1=ALU.mult), reads=[cntB], writes=[gB])
                        P.op("dve", lambda e: e.scalar_tensor_tensor(out=sm[:, MID:MID + 1], in0=sm[:, G:G + 1], scalar=sm[:, W0:W0 + 1],
                                                                     in1=sm[:, MIDM:MIDM + 1], op0=ALU.mult, op1=ALU.add),
                             reads=[smB, gB, midmB], writes=[midB])
                    P.op("dve", lambda e: e.scalar_tensor_tensor(out=sm[:, LO:LO + 1], in0=sm[:, W0:W0 + 1], scalar=-(0.5 ** (NBIS + 1)),
                                                                 in1=sm[:, MID:MID + 1], op0=ALU.mult, op1=ALU.add),
                         reads=[smB, midB], writes=[smB])
                    P.op("dve", lambda e: e.tensor_scalar(out=msk[:, 0:nk], in0=Isc[:, 0:nk], scalar1=sm[:, LO:LO + 1], scalar2=None,
                                                          op0=ALU.is_ge), reads=[IB, smB], writes=[mkB])
                else:
                    P.op("dve", lambda e: e.tensor_scalar(out=msk[:, 0:nk], in0=Isc[:, 0:nk], scalar1=-1.0e29, scalar2=None,
                                                          op0=ALU.is_ge), reads=[IB], writes=[mkB])
                if "sm" in dbg and l == 0:
                    P.dma(dr["dbg_sm"][qb * 128:(qb + 1) * 128, :], sm[:], reads=[smB])
                    if qb == 2:
                        P.dma(dr["dbg_I"][:, 0:nk], Isc[:, 0:nk], reads=[IB])
                        P.dma(dr["dbg_msk"][:, 0:nk], msk[:, 0:nk], reads=[mkB], q="pool")
                for g0 in range(0, qb + 1, 8):
                    gn = min(8, qb + 1 - g0)
                    ps, psB = pspool.next()
                    psb = ps[:].bitcast(BF16)

                    def ftr(e):
                        ins = None
                        for i in range(gn):
                            ins = e.transpose(psb[:, i * 128:(i + 1) * 128], msk[:, (g0 + i) * 128:(g0 + i + 1) * 128], identb[:])
                        return ins
                    P.op("pe", ftr, reads=[mkB, cB], writes=[psB])
                    P.op("dve", lambda e: e.tensor_scalar(out=mskT[:, g0:g0 + gn, :], in0=psb[:, 0:gn * 128].rearrange("p (g t) -> p g t", g=gn),
                                                          scalar1=-1.0, scalar2=30000.0, op0=ALU.add, op1=ALU.mult),
                         reads=[psB], writes=[mtB])
                yield
                po0, po0B = accpool.next()
                po1, po1B = accpool.next()
                P.op("dve", lambda e: e.memset(po0[:], 0.0), writes=[po0B])
                P.op("dve", lambda e: e.memset(po1[:], 0.0), writes=[po1B])
                pos_ = (po0[:].rearrange("p (j d) -> p j d", j=4), po1[:].rearrange("p (j d) -> p j d", j=4))
                def pv_c(st):
                    kb, PT, PTB = st
                    for half in range(2):
                        def fpv(e):
                            ins = None
                            for hh in range(4):
                                ins = e.matmul(pos_[half][:, hh, 0:65], PT[:, half * 4 + hh, :], vc[:, kb, 0:65],
                                               start=False, stop=(kb == qb), skip_group_check=True)
                            return ins
                        P.op("pe", fpv, reads=[PTB, vcB], acc=[(po0B, po1B)[half]])
                pend = None
                for kb in range(qb + 1):
                    PT, PTB = PTpool.next()
                    for half in range(2):
                        ps, psB = pspool.next()

                        def fqk(e):
                            ins = e.matmul(ps[:], identb[:], mskT[:, kb:kb + 1, :].broadcast_to([128, 4, 128]), start=True, stop=False)
                            for hh in range(4):
                                h = hh * 2 + half
                                hp = slice((h % 2) * 64, (h % 2) * 64 + 64)
                                ins = e.matmul(ps[:, hh * 128:(hh + 1) * 128], kc2[hp, kb * 128:(kb + 1) * 128], qcT[hp, h // 2, tl],
                                               start=False, stop=(hh == 3))
                            return ins
                        P.op("pe", fqk, reads=[kcB, qcB, mtB, cB], writes=[psB])
                        P.op("act", lambda e: e.activation(out=PT[:, half * 4:(half + 1) * 4, :].rearrange("p h t -> p (h t)"), in_=ps[:],
                                                           func=AF.Exp, scale=SCALE_C), reads=[psB], writes=[PTB])
                    if qb - kb <= 1:
                        P.op("pool", lambda e: e.tensor_tensor(out=PT[:], in0=PT[:], in1=EBr[:, qb - kb, :, :], op=ALU.mult),
                             reads=[PTB, ebB], writes=[PTB])
                    if pend is not None:
                        pv_c(pend)
                    pend = (kb, PT, PTB)
                pv_c(pend)
                yield
                for hf, (wg, wgB) in enumerate((wg0, wg1)):
                    ps, psB = pspool.next()
                    mm_group(ps[:, 0:256], [(hT[:, k, qb * 128:(qb + 1) * 128], wg[:, k, :]) for k in range(KC)],
                             reads=[wgB, hB[tb]], writes=[psB])
                    P.op("act", lambda e: e.activation(out=gsc[:, hf * 256:(hf + 1) * 256], in_=ps[:, 0:256], func=AF.Silu),
                         reads=[psB], writes=[gsB])
                for half in range(2):
                    P.op("dve", lambda e: e.reciprocal(out=recC[:, half * 4:half * 4 + 4], in_=pos_[half][:, :, 64]),
                         reads=[(po0B, po1B)[half]], writes=[rcB])
                    P.op("dve", lambda e: e.tensor_tensor(
                        out=oblk[:].rearrange("p (j two d) -> p j two d", two=2, d=64)[:, :, half, :], in0=pos_[half][:, :, 0:64],
                        in1=recC[:, half * 4:half * 4 + 4].unsqueeze(2).broadcast_to([128, 4, 64]), op=ALU.mult),
                        reads=[(po0B, po1B)[half], rcB], writes=[obB])
                if "o_c" in dbg and l == 0:
                    P.op("pool", lambda e: e.tensor_tensor(out=gsc[:], in0=oblk[:], in1=gsc[:], op=ALU.mult), reads=[gsB, obB], writes=[gsB])
                    P.dma(dr["dbg_o_c"][qb * 128:(qb + 1) * 128, :], gsc[:], reads=[gsB], q="pool")
                    P.op("dve", lambda e: e.tensor_copy(out=oblk[:], in_=gsc[:]), reads=[gsB, obB], writes=[obB])
                else:
                    P.op("dve", lambda e: e.tensor_tensor(out=oblk[:], in0=oblk[:], in1=gsc[:], op=ALU.mult), reads=[gsB, obB], writes=[obB])
                ps, psB = pspool.next()
                psb = ps[:].bitcast(BF16)

                def ftr2(e):
                    ins = None
                    for c in range(4):
                        ins = e.transpose(psb[:, c * 128:(c + 1) * 128], oblk[:, c * 128:(c + 1) * 128], identb[:])
                    return ins
                P.op("pe", ftr2, reads=[obB, cB], writes=[psB])
                copy_op(evac_engine(), ocT[:, :, tl], psb[:, 0:512].rearrange("p (c t) -> p c t", c=4), reads=[psB], writes=[ocB])
                yield
            gens = [qblock(qi) for qi in range(4)]
            next(gens[0]); next(gens[0])
            for qi in range(1, 4):
                next(gens[qi])
                next(gens[qi - 1])
                next(gens[qi])
                next(gens[qi - 1])
            next(gens[3]); next(gens[3])
            stop_at(7)
            merge_branch(l, first, 512, dr["w_br_c"][l], ocT, [ocB], OFF["merge"] + 2 * D, tb_list=[tb], local=True)

    if use_c:
        setup_bias()
    for l in range(depth):
        gcol = VTC["norm_g"] + l * 8
        for tb in range(NTB):
            ps, psB = pspool.next()
            for c in range(KC):
                sq, sqB = scrpool.next()
                P.op("act", lambda e, sq=sq, c=c, tb=tb: e.activation(out=sq[:].bitcast(BF16)[:, 0:512], in_=xT[:, c, tbs(tb)], func=AF.Square),
                     reads=[xB[tb]], writes=[sqB])
                P.op("pe", lambda e, sq=sq, c=c, ps=ps: e.matmul(ps[:], onesb[:], sq[:].bitcast(BF16)[:, 0:512], start=(c == 0), stop=(c == KC - 1)),
                     reads=[sqB, cB], writes=[psB])
            rs, rsB = scrpool.next()
            rms_rstd(ps[:], D, rs[:], [psB], [rsB])
            for c in range(KC):
                P.op("dve", lambda e, c=c, tb=tb, rs=rs: e.scalar_tensor_tensor(
                    out=hT[:, c, tbs(tb)], in0=xT[:, c, tbs(tb)], scalar=vt[:, gcol + c:gcol + c + 1], in1=rs[:],
                    op0=ALU.mult, op1=ALU.mult), reads=[xB[tb], rsB, cB], writes=[hB[tb]])

        any_branch = use_a or use_b or use_c
        if use_a:
            phase_A(l)
        if use_b:
            phase_B(l, first=not use_a)
        if use_c:
            phase_C(l, first=not (use_a or use_b))

        if any_branch:
            for oc2 in range(4):
                w, wB = load_w(dr["w_out"][l][:, oc2 * 256:(oc2 + 1) * 256], D, 256)
                for sub in range(2):
                    oc = oc2 * 2 + sub
                    for tb in range(NTB):
                        ps, psB = pspool.next()
                        mm_group(ps[:], [(w[:, k, sub * 128:(sub + 1) * 128], mT[:, k, tbs(tb)]) for k in range(KC)],
                                 reads=[wB, mB[tb]], writes=[psB])
                        P.op("dve", lambda e, oc=oc, tb=tb, ps=ps: e.tensor_tensor(
                            out=xT[:, oc, tbs(tb)], in0=xT[:, oc, tbs(tb)], in1=ps[:], op=ALU.add),
                            reads=[psB, xB[tb]], writes=[xB[tb]])
        for tb in range(NTB):
            copy_op("dve", hT[:, :, tbs(tb)], xT[:, :, tbs(tb)], reads=[xB[tb]], writes=[hB[tb]])
            pt, ptB = tokpool.next()
            P.dma(pt[:].rearrange("p (j f) -> p j f", j=4),
                  dr["p"][l][tb * TBW:(tb + 1) * TBW, :].rearrange("(j p) f -> p j f", p=128), writes=[ptB])
            for c in range(2):
                ps, psB = pspool.next()

                def fn(e, pt=pt, ps=ps, c=c):
                    ins = None
                    for j in range(4):
                        ins = e.transpose(ps[:, j * 128:(j + 1) * 128], pt[:, j * 256 + c * 128:j * 256 + (c + 1) * 128], ident[:])
                    return ins
                P.op("pe", fn, reads=[ptB, cB], writes=[psB])
                copy_op(evac_engine(), mT[:, c, tbs(tb)], ps[:], reads=[psB], writes=[mB[tb]])
        for oc2 in range(4):
            wg, wgB = load_w(dr["w_ple_gate"][l][:, oc2 * 256:(oc2 + 1) * 256], D, 256)
            wp, wpB = load_w(dr["w_ple"][l][:, oc2 * 256:(oc2 + 1) * 256], 256, 256)
            for sub in range(2):
                oc = oc2 * 2 + sub
                for tb in range(NTB):
                    ps, psB = pspool.next()
                    mm_group(ps[:], [(wg[:, k, sub * 128:(sub + 1) * 128], hT[:, k, tbs(tb)]) for k in range(KC)],
                             reads=[wgB, hB[tb]], writes=[psB])
                    sg, sgB = scrpool.next()
                    P.op("act", lambda e, sg=sg, ps=ps: e.activation(out=sg[:], in_=ps[:], func=AF.Sigmoid),
                         reads=[psB], writes=[sgB])
                    ps2, ps2B = pspool.next()
                    mm_group(ps2[:], [(wp[:, k, sub * 128:(sub + 1) * 128], mT[:, k, tbs(tb)]) for k in range(2)],
                             reads=[wpB, mB[tb]], writes=[ps2B])
                    P.op("dve", lambda e, sg=sg, ps2=ps2: e.tensor_tensor(out=sg[:], in0=sg[:], in1=ps2[:], op=ALU.mult),
                         reads=[ps2B, sgB], writes=[sgB])
                    P.op("dve", lambda e, sg=sg, oc=oc, tb=tb: e.tensor_tensor(
                        out=xT[:, oc, tbs(tb)], in0=xT[:, oc, tbs(tb)], in1=sg[:], op=ALU.add),
                        reads=[sgB, xB[tb]], writes=[xB[tb]])

    yB = Buf("y")
    P.barrier()
    fnbB = Buf("fnb")
    P.dma(fnb[:], dr["fnb"], writes=[fnbB])
    for blk in range(NBLK):
        ot, otB = tokpool.next()
        P.op("dve", lambda e: e.memset(small[:, 0:2], 0.0), writes=[smallB])
        pss = []
        for half in range(2):
            ps, psB = pspool.next()

            def fn(e, ps=ps, half=half, blk=blk):
                ins = None
                for j in range(4):
                    c = half * 4 + j
                    ins = e.transpose(ps[:, j * 128:(j + 1) * 128], xT[:, c, blk * 128:(blk + 1) * 128], ident[:])
                return ins
            P.op("pe", fn, reads=[xB[blk // 4], cB], writes=[psB])
            sq, sqB = scrpool.next()
            P.op("act", lambda e, sq=sq, ps=ps, half=half: e.activation(out=sq[:], in_=ps[:], func=AF.Square,
                                                                       accum_out=small[:, half:half + 1]),
                 reads=[psB], writes=[sqB, smallB])
            pss.append((ps, psB))
        P.op("dve", lambda e: e.tensor_tensor(out=small[:, 2:3], in0=small[:, 0:1], in1=small[:, 1:2], op=ALU.add),
             reads=[smallB], writes=[smallB])
        rms_rstd(small[:, 2:3], D, small[:, 3:4], [smallB], [smallB])
        for half in range(2):
            ps, psB = pss[half]
            P.op("dve", lambda e, ps=ps, half=half, ot=ot: e.scalar_tensor_tensor(
                out=ot[:, half * 512:(half + 1) * 512], in0=ps[:], scalar=small[:, 3:4],
                in1=fnb[:, half * 512:(half + 1) * 512], op0=ALU.mult, op1=ALU.mult),
                reads=[psB, smallB, fnbB], writes=[otB])
        P.dma(dr["y"][blk * 128:(blk + 1) * 128, :], ot[:], reads=[otB], writes=[yB])
    P.barrier()
    return nc, P


def rope_consts():
    c = np.zeros((128, 4), np.float32)
    inv_freq = (1.0 / (10000.0 ** (np.arange(0, 32, 2, dtype=np.float32) / 32.0))).astype(np.float32)
    for p in range(64, 96):
        c[p, 0] = inv_freq[(p - 64) % 16]
        c[p, 1] = -1.0 if p < 80 else 1.0
    return c


def host_layout(inputs, b):
    VTC, NV = vt_layout()
    vt = np.zeros((128, NV), np.float32)

    def put(name, arr2d):
        r, f = arr2d.shape
        ch = f // 128
        vt[:, VTC[name]:VTC[name] + r * ch] = arr2d.reshape(r, ch, 128).transpose(2, 0, 1).reshape(128, r * ch)
    put("norm_g", inputs["norm_g"])
    put("q_norm", inputs["mla_q_norm"])
    put("kv_norm", inputs["mla_kv_norm"])
    put("conv_w", inputs["conv_w"].reshape(DEPTH * 4, 1536))
    put("conv_b", inputs["conv_b"])
    put("ssm_norm", inputs["ssm_norm"])
    put("d_skip", np.repeat(inputs["d_skip"], 64, axis=1))
    m = {
        "x": np.ascontiguousarray(inputs["x"][b]),
        "p": np.ascontiguousarray(inputs["p"][:, b]),
        "pos": np.ascontiguousarray(inputs["positions"][b:b + 1]).astype(np.int32),
        "vt": vt,
        "fnb": np.ascontiguousarray(np.broadcast_to(inputs["final_norm"][None, :], (128, D))).astype(np.float32),
        "ident": np.eye(128, dtype=np.float32),
        "tri": np.triu(np.ones((128, 128), np.float32)),
        "ropec": rope_consts(),
        "negm": np.where(np.arange(128)[None, :] > np.arange(128)[:, None], np.float32(-1.0e30), np.float32(0.0)).astype(np.float32),
        "rel_bias": np.ascontiguousarray(inputs["rel_bias"], dtype=np.float32).reshape(1, 256),
        "negs": np.where(np.arange(128)[None, :] < np.arange(128)[:, None], np.float32(-30000.0), np.float32(0.0)).astype(np.float32),
        "sut": np.tril(np.ones((128, 128), np.float32), -1),
        "dt_bias": np.ascontiguousarray(inputs["dt_bias"], dtype=np.float32),
        "a_log": np.ascontiguousarray(inputs["a_log"], dtype=np.float32),
    }
    for k in ("w_in", "w_uq", "w_ukv", "w_br_a", "w_br_b", "w_br_c", "w_out", "w_ple", "w_ple_gate"):
        m[k] = np.ascontiguousarray(inputs[k], dtype=np.float32)
    return m


def kernel(**inputs):
    inputs = {k: np.asarray(v) for k, v in inputs.items()}
    nc, P = build_program()
    in_maps = [host_layout(inputs, b) for b in range(8)]
    res = run_bass_kernel_spmd(nc, in_maps, core_ids=list(range(8)))
    return np.stack([np.asarray(r["y"]) for r in res.results], axis=0).astype(np.float32)
```

```python
import math
import numpy as np
import concourse.bass as bass
import concourse.mybir as mybir
from concourse.bass_utils import run_bass_kernel_spmd

F32 = mybir.dt.float32
BF16 = mybir.dt.bfloat16
I32 = mybir.dt.int32
AF = mybir.ActivationFunctionType
ALU = mybir.AluOpType
AX = mybir.AxisListType

S = 2048
D = 1024
KC = 8
NTB = 4
TBW = 512
NBLK = 16
DEPTH = 4
EPS = 1e-6
IN_TOTAL = 8568
OFF = {}
_o = 0
for _n, _s in [("c_q", 384), ("c_kv", 256), ("k_rope", 32), ("gate_a", 512), ("z", 1024), ("xbc", 1536),
               ("dt", 16), ("q_c", 512), ("k_c", 64), ("v_c", 64), ("q_idx", 512), ("k_idx", 64), ("w_idx", 8),
               ("gate_c", 512), ("merge", 3072)]:
    OFF[_n] = _o
    _o += _s
assert _o == IN_TOTAL


class Buf:
    __slots__ = ("name", "w", "r")

    def __init__(self, name):
        self.name = name
        self.w = None
        self.r = []


class Prog:
    NDMA = 24

    def __init__(self, nc):
        self.nc = nc
        self.e = dict(pe=nc.tensor, act=nc.scalar, dve=nc.vector, pool=nc.gpsimd, sp=nc.sync)
        self.sem = {}
        for k in ("pe", "act", "dve", "pool"):
            self.sem[k] = nc.alloc_semaphore(name=f"s_{k}")
        for j in range(self.NDMA):
            self.sem[f"d{j}"] = nc.alloc_semaphore(name=f"s_d{j}")
            self.sem[f"g{j}"] = nc.alloc_semaphore(name=f"s_g{j}")
        self.cnt = {k: 0 for k in self.sem}
        self.seen = {k: {} for k in self.e}
        self.dma_rr = {"sp": 0, "pool": 0}
        self.swq = []
        self.SW_DESC_BUDGET = 3072
        self.ninst = {k: 0 for k in self.e}

    def _wait(self, eng, toks):
        need = {}
        for t in toks:
            if t is None:
                continue
            k, v = t
            if v > need.get(k, 0):
                need[k] = v
        for k, v in need.items():
            if self.seen[eng].get(k, 0) < v:
                self.e[eng].wait_ge(self.sem[k], v)
                self.seen[eng][k] = v
                self.ninst[eng] += 1

    @staticmethod
    def _flat(bs):
        out = []
        for b in bs:
            if isinstance(b, (list, tuple)):
                out.extend(Prog._flat(b))
            else:
                out.append(b)
        return out

    @staticmethod
    def _deps(reads, writes):
        reads = Prog._flat(reads)
        writes = Prog._flat(writes)
        toks = []
        for b in reads:
            toks.append(b.w)
        for b in writes:
            toks.append(b.w)
            toks.extend(b.r)
        return toks

    def _mark(self, tok, reads, writes):
        reads = Prog._flat(reads)
        writes = Prog._flat(writes)
        for b in reads:
            b.r.append(tok)
        for b in writes:
            b.w = tok
            b.r = []

    def op(self, eng, fn, reads=(), writes=(), acc=()):
        toks = self._deps(reads, writes)
        acc = Prog._flat(acc)
        for b in acc:
            if b.w is not None and b.w[0] != eng:
                toks.append(b.w)
            toks.extend(b.r)
        self._wait(eng, toks)
        ins = fn(self.e[eng])
        self.cnt[eng] += 1
        ins.then_inc(self.sem[eng], 1)
        self.ninst[eng] += 1
        tok = (eng, self.cnt[eng])
        self._mark(tok, reads, list(writes) + list(acc))
        return tok

    def dma(self, out, in_, reads=(), writes=(), q="sp"):
        j = self.dma_rr[q]
        self.dma_rr[q] = (j + 1) % self.NDMA
        k = ("d" if q == "sp" else "g") + str(j)
        toks = self._deps(reads, writes)
        if self.cnt[k] > 0:
            toks.append((k, self.cnt[k]))
        if q == "pool":
            nd = 1
            for d in tuple(out.shape)[:-1]:
                nd *= int(d)
            nd = max(nd, 128)
            while self.swq and sum(x[1] for x in self.swq) + nd > self.SW_DESC_BUDGET:
                toks.append(self.swq.pop(0)[0])
        self._wait(q, toks)
        self.cnt[k] += 16
        self.e[q].dma_start(out=out, in_=in_).then_inc(self.sem[k], 16)
        self.ninst[q] += 1
        tok = (k, self.cnt[k])
        if q == "pool":
            self.swq.append((tok, nd))
        self._mark(tok, reads, writes)
        return tok

    def barrier(self):
        toks = [(k, v) for k, v in self.cnt.items() if v > 0]
        for eng in self.e:
            self._wait(eng, toks)


class Rot:
    def __init__(self, items):
        self.items = items
        self.i = 0

    def next(self):
        it = self.items[self.i]
        self.i = (self.i + 1) % len(self.items)
        return it


def vt_layout():
    cols = {}
    o = 0
    for name, n in [("norm_g", DEPTH * 8), ("q_norm", DEPTH * 3), ("kv_norm", DEPTH * 2),
                    ("conv_w", DEPTH * 4 * 12), ("conv_b", DEPTH * 12), ("ssm_norm", DEPTH * 8),
                    ("d_skip", DEPTH * 8)]:
        cols[name] = o
        o += n
    return cols, o


def build_program(depth=DEPTH, use_a=True, use_b=True, use_c=True, dbg=()):
    nc = bass.Bass("TRN2", target_bir_lowering=False)
    P = Prog(nc)
    dr = {}

    def din(name, shape, dt=F32):
        dr[name] = nc.dram_tensor(name, shape, dt, kind="ExternalInput").ap()

    def dout(name, shape, dt=F32):
        dr[name] = nc.dram_tensor(name, shape, dt, kind="ExternalOutput").ap()

    VTC, NV = vt_layout()
    din("x", [S, D])
    din("p", [DEPTH, S, 256])
    din("pos", [1, S], I32)
    din("w_in", [DEPTH, D, IN_TOTAL])
    din("w_uq", [DEPTH, 384, 768])
    din("w_ukv", [DEPTH, 256, 1024])
    din("w_br_a", [DEPTH, 512, D])
    din("w_br_b", [DEPTH, 1024, D])
    din("w_br_c", [DEPTH, 512, D])
    din("w_out", [DEPTH, D, D])
    din("w_ple", [DEPTH, 256, D])
    din("w_ple_gate", [DEPTH, D, D])
    din("vt", [128, NV])
    din("fnb", [128, D])
    din("ident", [128, 128])
    din("tri", [128, 128])
    din("ropec", [128, 4])
    din("negm", [128, 128])
    din("negs", [128, 128])
    din("sut", [128, 128])
    din("dt_bias", [DEPTH, 16])
    din("a_log", [DEPTH, 16])
    din("rel_bias", [1, 256])
    dout("y", [S, D])
    if "o_a" in dbg:
        dout("dbg_o_a", [S, 512])
    if "o_c" in dbg:
        dout("dbg_o_c", [S, 512])
    if "o_b" in dbg:
        dout("dbg_o_bT", [1024, S])
    if "sm" in dbg:
        dout("dbg_sm", [S, 32])
        dout("dbg_I", [128, 512])
        dout("dbg_msk", [128, 512])

    base0 = (nc.sbuf_base + 31) // 32 * 32
    slab = nc.alloc_sbuf_tensor("slab", [128, 212800 // 2], BF16)
    cur = [base0]

    def salloc(name, shape, dt, at=None):
        nbytes = int(np.prod(shape[1:])) * (4 if dt in (F32, I32) else 2)
        nbytes = (nbytes + 31) // 32 * 32
        if at is None:
            at = cur[0]
            cur[0] += nbytes
        return nc.alloc_sbuf_tensor_at(name, shape, dt, offset=at)

    xT = salloc("xT", [128, KC, S], F32)
    hT = salloc("hT", [128, KC, S], BF16)
    MT_OFF = cur[0]
    mT = salloc("mT", [128, KC, S], BF16)
    vt = salloc("vt_sb", [128, NV], F32)
    ident = salloc("ident_sb", [128, 128], F32)
    identb = salloc("identb_sb", [128, 128], BF16)
    trib = salloc("trib_sb", [128, 128], BF16)
    ropec = salloc("ropec_sb", [128, 4], F32)
    ones32 = salloc("ones32", [128, 128], F32)
    epsc = salloc("epsc", [128, 1], F32)
    xB = [Buf(f"x{t}") for t in range(NTB)]
    hB = [Buf(f"h{t}") for t in range(NTB)]
    mB = [Buf(f"m{t}") for t in range(NTB)]
    cB = Buf("consts")

    wbufs = []
    for i in range(3):
        t = salloc(f"wbuf{i}", [128, 2048], BF16)
        wbufs.append((t, [Buf(f"wbuf{i}a"), Buf(f"wbuf{i}b")]))
    wpool = Rot(wbufs)
    psl = []
    for i in range(8):
        t = nc.alloc_psum_tensor(f"ps{i}", [128, 512], F32)
        psl.append((t, Buf(f"ps{i}")))
    pspool = Rot(psl[:6])
    accpool = Rot(psl[6:])
    scr = []
    for i in range(3):
        t = salloc(f"scr{i}", [128, 512], F32)
        scr.append((t, Buf(f"scr{i}")))
    scrpool = Rot(scr)
    small = salloc("small", [128, 64], F32)
    smallB = Buf("small")
    EBr = salloc("EBr", [128, 2, 8, 128], BF16)
    negm = salloc("negm", [128, 128], F32)
    tri32 = salloc("tri32", [128, 128], F32)
    negs = salloc("negs", [128, 128], BF16)
    onec = salloc("onec", [128, 1], F32)
    onesb = salloc("onesb", [128, 128], BF16)
    sutb = salloc("sutb", [128, 128], BF16)
    ARENA = cur[0]
    ARENA_SZ = 212800 - (ARENA - base0) - 64
    assert ARENA_SZ >= 53000, ARENA_SZ
    tok = []
    for i in range(2):
        t = salloc(f"tok{i}", [128, 1024], F32, at=ARENA + i * 4096)
        tok.append((t, Buf(f"tok{i}")))
    tokpool = Rot(tok)
    fnb = salloc("fnb_sb", [128, D], F32, at=ARENA + 8192)

    def tbs(tb):
        return slice(tb * TBW, (tb + 1) * TBW)

    def load_w(src2d, K, N):
        buf, B = wpool.next()
        kc = K // 128
        dst = buf[:, :kc * N].rearrange("p (k n) -> p k n", k=kc)
        P.dma(dst, src2d.rearrange("(k p) n -> p k n", p=128), writes=[B], q="pool")
        return dst, B

    def mm_group(out, pairs, reads, writes):
        n = len(pairs)

        def fn(e):
            ins = None
            for i, (l, r) in enumerate(pairs):
                ins = e.matmul(out, l, r, start=(i == 0), stop=(i == n - 1))
            return ins
        return P.op("pe", fn, reads, writes)

    evac_flip = [0]

    def evac_engine():
        evac_flip[0] ^= 1
        return "act" if evac_flip[0] else "dve"

    def copy_op(eng, out, in_, reads, writes):
        if eng == "act":
            return P.op("act", lambda e: e.activation(out=out, in_=in_, func=AF.Copy), reads, writes)
        return P.op(eng, lambda e: e.tensor_copy(out=out, in_=in_), reads, writes)

    P.dma(vt[:], dr["vt"], writes=[cB])
    P.dma(ident[:], dr["ident"], writes=[cB])
    P.dma(identb[:], dr["ident"], writes=[cB], q="pool")
    P.dma(trib[:], dr["tri"], writes=[cB], q="pool")
    P.dma(ropec[:], dr["ropec"], writes=[cB])
    P.dma(tri32[:], dr["tri"], writes=[cB])
    P.dma(sutb[:], dr["sut"], writes=[cB], q="pool")
    P.dma(negs[:], dr["negs"], writes=[cB], q="pool")
    P.op("dve", lambda e: e.memset(onec[:], 1.0), writes=[cB])
    P.op("dve", lambda e: e.memset(onesb[:], 1.0), writes=[cB])
    P.op("dve", lambda e: e.memset(ones32[:], 1.0), writes=[cB])
    P.op("dve", lambda e: e.memset(epsc[:], EPS), writes=[cB])

    for blk in range(NBLK):
        xt, xtB = tokpool.next()
        P.dma(xt[:], dr["x"][blk * 128:(blk + 1) * 128, :], writes=[xtB])
        for half in range(2):
            ps, psB = pspool.next()

            def fn(e, xt=xt, ps=ps, half=half):
                ins = None
                for j in range(4):
                    c = half * 4 + j
                    ins = e.transpose(ps[:, j * 128:(j + 1) * 128], xt[:, c * 128:(c + 1) * 128], ident[:])
                return ins
            P.op("pe", fn, reads=[xtB, cB], writes=[psB])
            copy_op(evac_engine(), xT[:, half * 4:(half + 1) * 4, blk * 128:(blk + 1) * 128],
                    ps[:].rearrange("p (j t) -> p j t", j=4), reads=[psB], writes=[xB[blk // 4]])

    Ct = salloc("Ct", [128, S], BF16, at=ARENA + 40960)
    Sgt = salloc("Sgt", [128, S], BF16, at=ARENA + 45056)
    ropeB = Buf("ropetab")
    dr["ropeC"] = nc.dram_tensor("ropeC", [128, S], BF16, kind="Internal").ap()
    dr["ropeS"] = nc.dram_tensor("ropeS", [128, S], BF16, kind="Internal").ap()
    posi = salloc("posi", [128, S], I32, at=ARENA + 8192)
    angf = salloc("angf", [128, S], F32, at=ARENA + 16384)
    kff = salloc("kff", [128, S], F32, at=ARENA + 24576)
    kii = salloc("kii", [128, S], I32, at=ARENA + 8192)
    rB = Buf("ropetmp")
    P.dma(posi[:], dr["pos"].partition_broadcast(128), writes=[rB])
    P.op("dve", lambda e: e.tensor_copy(out=angf[:], in_=posi[:]), reads=[rB], writes=[rB])
    P.op("dve", lambda e: e.tensor_scalar(out=angf[:], in0=angf[:], scalar1=ropec[:, 0:1], scalar2=None, op0=ALU.mult),
         reads=[rB, cB], writes=[rB])
    TWO_PI = 2.0 * math.pi

    def reduced_sin(out_bf, shift, post_scale_col):
        P.op("dve", lambda e: e.tensor_scalar(out=kff[:], in0=angf[:], scalar1=shift, scalar2=1.0 / TWO_PI,
                                              op0=ALU.add, op1=ALU.mult), reads=[rB], writes=[rB])
        P.op("dve", lambda e: e.tensor_copy(out=kii[:], in_=kff[:]), reads=[rB], writes=[rB])
        P.op("dve", lambda e: e.tensor_copy(out=kff[:], in_=kii[:]), reads=[rB], writes=[rB])
        P.op("dve", lambda e: e.scalar_tensor_tensor(out=kff[:], in0=kff[:], scalar=-TWO_PI, in1=angf[:],
                                                     op0=ALU.mult, op1=ALU.add), reads=[rB], writes=[rB])
        P.op("dve", lambda e: e.tensor_scalar(out=kff[:], in0=kff[:], scalar1=shift, scalar2=None, op0=ALU.add),
             reads=[rB], writes=[rB])
        tmpf = angf2
        P.op("dve", lambda e: e.tensor_scalar(out=tmpf[:], in0=kff[:], scalar1=math.pi, scalar2=TWO_PI,
                                              op0=ALU.is_gt, op1=ALU.mult), reads=[rB], writes=[rB])
        P.op("dve", lambda e: e.tensor_tensor(out=kff[:], in0=kff[:], in1=tmpf[:], op=ALU.subtract), reads=[rB], writes=[rB])
        P.op("dve", lambda e: e.tensor_scalar(out=tmpf[:], in0=kff[:], scalar1=-math.pi, scalar2=TWO_PI,
                                              op0=ALU.is_lt, op1=ALU.mult), reads=[rB], writes=[rB])
        P.op("dve", lambda e: e.tensor_tensor(out=kff[:], in0=kff[:], in1=tmpf[:], op=ALU.add), reads=[rB], writes=[rB])
        P.op("act", lambda e: e.activation(out=kff[:], in_=kff[:], func=AF.Sin), reads=[rB], writes=[rB])
        if post_scale_col is None:
            P.op("dve", lambda e: e.tensor_copy(out=out_bf[:], in_=kff[:]), reads=[rB], writes=[cB])
        else:
            P.op("dve", lambda e: e.tensor_scalar(out=out_bf[:], in0=kff[:], scalar1=post_scale_col, scalar2=None,
                                                  op0=ALU.mult), reads=[rB, cB], writes=[cB])
    angf2 = salloc("angf2", [128, S], F32, at=ARENA + 32768)
    reduced_sin(Ct, math.pi / 2.0, None)
    reduced_sin(Sgt, 0.0, ropec[:, 1:2])
    sB1, sB2 = Buf("ropeCd"), Buf("ropeSd")
    P.dma(dr["ropeC"], Ct[:], reads=[cB], writes=[sB1])
    P.dma(dr["ropeS"], Sgt[:], reads=[cB], writes=[sB2])
    P.barrier()

    def rms_rstd(ps, n_feat, out_rstd, reads, writes):
        P.op("act", lambda e: e.activation(out=out_rstd, in_=ps, func=AF.Sqrt, scale=1.0 / n_feat, bias=epsc[:, 0:1]),
             reads=list(reads) + [cB], writes=writes)
        P.op("dve", lambda e: e.reciprocal(out=out_rstd, in_=out_rstd), reads=writes, writes=writes)


    SCALE_A = 96.0 ** -0.5

    def merge_branch(l, first, K, w_br, oT, oBufs, gate_col0, tb_list=None, local=False):
        kc = K // 128
        for oc in range(8):
            if kc == 8:
                wb, WB0 = load_w(w_br[:, oc * 128:(oc + 1) * 128], K, 128)
                wm, WB1 = load_w(dr["w_in"][l][:, gate_col0 + oc * 128:gate_col0 + (oc + 1) * 128], D, 128)
                WB = [WB0, WB1]
            else:
                buf, WB = wpool.next()
                wb = buf[:, 0:kc * 128].rearrange("p (k n) -> p k n", k=kc)
                wm = buf[:, kc * 128:(kc + 8) * 128].rearrange("p (k n) -> p k n", k=KC)
                P.dma(wb, w_br[:, oc * 128:(oc + 1) * 128].rearrange("(k p) n -> p k n", p=128), writes=[WB[0]], q="pool")
                P.dma(wm, dr["w_in"][l][:, gate_col0 + oc * 128:gate_col0 + (oc + 1) * 128].rearrange("(k p) n -> p k n", p=128),
                      writes=[WB[1]], q="pool")
            for tb in (tb_list if tb_list is not None else range(NTB)):
                osl = slice(0, TBW) if local else tbs(tb)
                psg, psgB = pspool.next()
                mm_group(psg[:], [(wm[:, k, :], hT[:, k, tbs(tb)]) for k in range(KC)], reads=[WB[1], hB[tb]], writes=[psgB])
                sg, sgB = scrpool.next()
                P.op("act", lambda e: e.activation(out=sg[:], in_=psg[:], func=AF.Sigmoid), reads=[psgB], writes=[sgB])
                psy, psyB = pspool.next()
                mm_group(psy[:], [(wb[:, k, :], oT[:, k, osl]) for k in range(kc)], reads=[WB[0]] + oBufs, writes=[psyB])
                if first:
                    P.op("dve", lambda e: e.tensor_tensor(out=mT[:, oc, tbs(tb)], in0=sg[:], in1=psy[:], op=ALU.mult),
                         reads=[sgB, psyB], writes=[mB[tb]])
                else:
                    P.op("dve", lambda e: e.tensor_tensor(out=sg[:], in0=sg[:], in1=psy[:], op=ALU.mult),
                         reads=[sgB, psyB], writes=[sgB])
                    P.op("dve", lambda e: e.tensor_tensor(out=mT[:, oc, tbs(tb)], in0=mT[:, oc, tbs(tb)], in1=sg[:], op=ALU.add),
                         reads=[sgB, mB[tb]], writes=[mB[tb]])

    def norm_proj(l, col0, nfeat, gname, outT, outB):
        nch = nfeat // 128
        gcol = VTC[gname] + l * nch
        ws = []
        for c0 in range(0, nch, 2):
            n = min(2, nch - c0) * 128
            ws.append(load_w(dr["w_in"][l][:, col0 + c0 * 128:col0 + c0 * 128 + n], D, n))
        for tb in range(NTB):
            pss = []
            pq, pqB = pspool.next()
            for c in range(nch):
                w, wB = ws[c // 2]
                ps, psB = pspool.next()
                mm_group(ps[:], [(w[:, k, (c % 2) * 128:(c % 2 + 1) * 128], hT[:, k, tbs(tb)]) for k in range(KC)],
                         reads=[wB, hB[tb]], writes=[psB])
                sq, sqB = scrpool.next()
                P.op("act", lambda e: e.activation(out=sq[:].bitcast(BF16)[:, 0:512], in_=ps[:], func=AF.Square), reads=[psB], writes=[sqB])
                P.op("pe", lambda e: e.matmul(pq[:], onesb[:], sq[:].bitcast(BF16)[:, 0:512], start=(c == 0), stop=(c == nch - 1)),
                     reads=[sqB, cB], acc=[pqB])
                pss.append((ps, psB))
            rs, rsB = scrpool.next()
            rms_rstd(pq[:], nfeat, rs[:], [pqB], [rsB])
            for c in range(nch):
                ps, psB = pss[c]
                P.op("dve", lambda e: e.scalar_tensor_tensor(
                    out=outT[:, c, tbs(tb)], in0=ps[:], scalar=vt[:, gcol + c:gcol + c + 1], in1=rs[:],
                    op0=ALU.mult, op1=ALU.mult), reads=[psB, rsB, cB], writes=[outB])

    def phase_A(l):
        o_tok = salloc("o_tok", [128, NBLK, 512], BF16, at=ARENA)
        oaT = salloc("oaT", [128, 4, S], BF16, at=ARENA + 16384)
        v_aug = salloc("v_aug", [128, NBLK, 66], BF16, at=ARENA + 32768)
        pTs = [(salloc(f"pT{i}", [128, 512], BF16, at=ARENA + 34944 + i * 1024), Buf(f"pT{i}")) for i in range(3)]
        ptpool = Rot(pTs)
        gsl = [(salloc(f"gsil{i}", [128, 512], BF16, at=ARENA + 38016 + i * 1024), Buf(f"gsil{i}")) for i in range(2)]
        gspool = Rot(gsl)
        recs = salloc("recs", [128, 8], F32, at=ARENA + 40064)
        ckvn = salloc("ckvn", [128, 2, S], BF16, at=MT_OFF)
        cqn = salloc("cqn", [128, 3, S], BF16, at=MT_OFF + 8192)
        krT = salloc("krT", [128, S], BF16, at=MT_OFF + 20480)
        kfull = salloc("kfull", [128, S], BF16, at=MT_OFF + 24576)
        qfull = salloc("qfull", [128, S], BF16, at=MT_OFF + 28672)
        otB, oaB, vB, ckvB, cqB, krB, kfB, qfB, recB = [Buf(n) for n in
                                                     ("o_tok", "oaT", "v_aug", "ckvn", "cqn", "krT", "kfull", "qfull", "recs")]
        P.barrier()
        P.dma(Ct[:], dr["ropeC"], writes=[ropeB])
        P.dma(Sgt[:], dr["ropeS"], writes=[ropeB])
        P.op("pool", lambda e: e.memset(v_aug[:, :, 64:66], 1.0), writes=[vB])
        norm_proj(l, OFF["c_kv"], 256, "kv_norm", ckvn, ckvB)
        norm_proj(l, OFF["c_q"], 384, "q_norm", cqn, cqB)
        wbuf, wB = wpool.next()
        wk = wbuf[:, :KC * 96].rearrange("p (k n) -> p k n", k=KC)
        wbuf2, wB2 = wpool.next()
        wk2 = wbuf2[:, :KC * 96].rearrange("p (k n) -> p k n", k=KC)
        src = dr["w_in"][l].rearrange("(k p) n -> p k n", p=128)
        c0 = OFF["k_rope"]
        P.dma(wk[:, :, 64:96], src[:, :, c0:c0 + 32], writes=[wB], q="pool")
        P.dma(wk2[:, :, 64:80], src[:, :, c0 + 16:c0 + 32], writes=[wB2], q="pool")
        P.dma(wk2[:, :, 80:96], src[:, :, c0:c0 + 16], writes=[wB2], q="pool")

        def rope_rows(psA, psAB, psSw, psSwB, out_rows, outB_, tb):
            t1, t1B = scrpool.next()
            t2, t2B = scrpool.next()
            P.op("dve", lambda e: e.tensor_tensor(out=t1[64:96, :], in0=psA[64:96, :], in1=Ct[64:96, tbs(tb)], op=ALU.mult),
                 reads=[psAB, ropeB], writes=[t1B])
            P.op("dve", lambda e: e.tensor_tensor(out=t2[64:96, :], in0=psSw[64:96, :], in1=Sgt[64:96, tbs(tb)], op=ALU.mult),
                 reads=[psSwB, ropeB], writes=[t2B])
            P.op("dve", lambda e: e.tensor_tensor(out=out_rows, in0=t1[64:96, :], in1=t2[64:96, :], op=ALU.add),
                 reads=[t1B, t2B], writes=[outB_])

        for tb in range(NTB):
            psA, psAB = pspool.next()
            mm_group(psA[0:96, :], [(wk[:, k, :], hT[:, k, tbs(tb)]) for k in range(KC)], reads=[wB, hB[tb]], writes=[psAB])
            psS, psSB = pspool.next()
            mm_group(psS[0:96, :], [(wk2[:, k, :], hT[:, k, tbs(tb)]) for k in range(KC)], reads=[wB2, hB[tb]], writes=[psSB])
            rope_rows(psA, psAB, psS, psSB, krT[64:96, tbs(tb)], krB, tb)

        uq = dr["w_uq"][l].rearrange("(k p) n -> p k n", p=128)
        ukv = dr["w_ukv"][l].rearrange("(k p) n -> p k n", p=128)
        def load_head(h):
            b1, b1B = wpool.next()
            wq = b1[:, 0:288].rearrange("p (k n) -> p k n", k=3)
            wqs = b1[:, 288:576].rearrange("p (k n) -> p k n", k=3)
            wkn = b1[:, 576:704].rearrange("p (k n) -> p k n", k=2)
            wv = b1[:, 704:832].rearrange("p (k n) -> p k n", k=2)
            hb = [Buf(f"hw{h}_{i}") for i in range(5)]
            P.dma(wq, uq[:, :, h * 96:(h + 1) * 96], reads=[], writes=[b1B, hb[0]], q="pool")
            P.dma(wqs[:, :, 64:80], uq[:, :, h * 96 + 80:h * 96 + 96], writes=[hb[1]], q="pool")
            P.dma(wqs[:, :, 80:96], uq[:, :, h * 96 + 64:h * 96 + 80], writes=[hb[2]], q="pool")
            P.dma(wkn, ukv[:, :, h * 128:h * 128 + 64], writes=[hb[3]], q="pool")
            P.dma(wv, ukv[:, :, h * 128 + 64:h * 128 + 128], writes=[hb[4]], q="pool")
            return wq, wqs, wkn, wv, [b1B, hb]
        nxt = load_head(0)
        for h in range(8):
            wq, wqs, wkn, wv, b1B = nxt
            if h < 7:
                nxt = load_head(h + 1)
            for tb in range(NTB):
                ps, psB = pspool.next()
                mm_group(ps[0:64, :], [(wkn[:, k, :], ckvn[:, k, tbs(tb)]) for k in range(2)], reads=[b1B, ckvB], writes=[psB])
                copy_op(evac_engine(), kfull[0:64, tbs(tb)], ps[0:64, :], reads=[psB], writes=[kfB])
                copy_op("dve", kfull[64:96, tbs(tb)], krT[64:96, tbs(tb)], reads=[krB], writes=[kfB])
                psA, psAB = pspool.next()
                mm_group(psA[0:96, :], [(wq[:, k, :], cqn[:, k, tbs(tb)]) for k in range(3)], reads=[b1B, cqB], writes=[psAB])
                psS, psSB = pspool.next()
                mm_group(psS[0:96, :], [(wqs[:, k, :], cqn[:, k, tbs(tb)]) for k in range(3)], reads=[b1B, cqB], writes=[psSB])
                copy_op(evac_engine(), qfull[0:64, tbs(tb)], psA[0:64, :], reads=[psAB], writes=[qfB])
                rope_rows(psA, psAB, psS, psSB, qfull[64:96, tbs(tb)], qfB, tb)
            for g4 in range(4):
                ps, psB = pspool.next()

                def fnv(e):
                    ins = None
                    for j in range(4):
                        blk = g4 * 4 + j
                        for k in range(2):
                            ins = e.matmul(ps[:, j * 64:(j + 1) * 64], ckvn[:, k, blk * 128:(blk + 1) * 128], wv[:, k, :],
                                           start=(k == 0), stop=(k == 1))
                    return ins
                P.op("pe", fnv, reads=[b1B, ckvB], writes=[psB])
                copy_op(evac_engine(), v_aug[:, g4 * 4:(g4 + 1) * 4, 0:64], ps[:, 0:256].rearrange("p (j d) -> p j d", j=4),
                        reads=[psB], writes=[vB])
            for Qb in range(4):
                po, poB = accpool.next()
                po3 = po[:].rearrange("p (j d) -> p j d", j=4)
                P.op("dve", lambda e: e.memset(po[:], 0.0), writes=[poB])
                nkb = 4 * Qb + 4
                def pv_a(st):
                    kb, j0, pT, pTB = st

                    def fpv(e):
                        ins = None
                        for j in range(j0, 4):
                            ins = e.matmul(po3[:, j, 0:65], pT[:, (j - j0) * 128:(j - j0 + 1) * 128], v_aug[:, kb, 0:65],
                                           start=False, stop=(kb == 4 * Qb + j), skip_group_check=True)
                        return ins
                    P.op("pe", fpv, reads=[pTB, vB], acc=[poB])
                pendq = []
                for kb in range(nkb):
                    j0 = max(0, kb - 4 * Qb)
                    n = 512 - j0 * 128
                    qlo = Qb * 512 + j0 * 128
                    ps, psB = pspool.next()
                    P.op("pe", lambda e: e.matmul(ps[:, 0:n], kfull[0:96, kb * 128:(kb + 1) * 128], qfull[0:96, qlo:qlo + n],
                                                  start=True, stop=True), reads=[kfB, qfB], writes=[psB])
                    pT, pTB = ptpool.next()
                    P.op("act", lambda e: e.activation(out=pT[:, 0:n], in_=ps[:, 0:n], func=AF.Exp, scale=SCALE_A),
                         reads=[psB], writes=[pTB])
                    if kb >= 4 * Qb:
                        P.op("pool", lambda e: e.tensor_tensor(out=pT[:, 0:128], in0=pT[:, 0:128], in1=trib[:], op=ALU.mult),
                             reads=[pTB, cB], writes=[pTB])
                    pendq.append((kb, j0, pT, pTB))
                    if len(pendq) > 2:
                        pv_a(pendq.pop(0))
                for st_ in pendq:
                    pv_a(st_)
                P.op("dve", lambda e: e.reciprocal(out=recs[:, 0:4], in_=po3[:, :, 64]), reads=[poB], writes=[recB])
                for j in range(4):
                    P.op("dve" if j % 2 else "act", (lambda e: e.tensor_scalar(
                        out=o_tok[:, Qb * 4 + j, h * 64:(h + 1) * 64], in0=po3[:, j, 0:64], scalar1=recs[:, j:j + 1],
                        scalar2=None, op0=ALU.mult)) if j % 2 else (lambda e: e.activation(
                            out=o_tok[:, Qb * 4 + j, h * 64:(h + 1) * 64], in_=po3[:, j, 0:64], func=AF.Copy,
                            scale=recs[:, j:j + 1])), reads=[poB, recB], writes=[otB])
        wg0 = load_w(dr["w_in"][l][:, OFF["gate_a"]:OFF["gate_a"] + 256], D, 256)
        wg1 = load_w(dr["w_in"][l][:, OFF["gate_a"] + 256:OFF["gate_a"] + 512], D, 256)
        if "o_a" in dbg and l == 0:
            pass
        for blk in range(NBLK):
            gs, gsB = gspool.next()
            for hf, (wg, wgB) in enumerate((wg0, wg1)):
                ps, psB = pspool.next()
                mm_group(ps[:, 0:256], [(hT[:, k, blk * 128:(blk + 1) * 128], wg[:, k, :]) for k in range(KC)],
                         reads=[wgB, hB[blk // 4]], writes=[psB])
                P.op("act", lambda e: e.activation(out=gs[:, hf * 256:(hf + 1) * 256], in_=ps[:, 0:256], func=AF.Silu),
                     reads=[psB], writes=[gsB])
            P.op("dve", lambda e: e.tensor_tensor(out=o_tok[:, blk, :], in0=o_tok[:, blk, :], in1=gs[:], op=ALU.mult),
                 reads=[gsB, otB], writes=[otB])
            ps, psB = pspool.next()
            psb = ps[:].bitcast(BF16)

            def ftr(e):
                ins = None
                for c in range(4):
                    ins = e.transpose(psb[:, c * 128:(c + 1) * 128], o_tok[:, blk, c * 128:(c + 1) * 128], identb[:])
                return ins
            P.op("pe", ftr, reads=[otB, cB], writes=[psB])
            copy_op(evac_engine(), oaT[:, :, blk * 128:(blk + 1) * 128], psb[:, 0:512].rearrange("p (c t) -> p c t", c=4),
                    reads=[psB], writes=[oaB])
        if "o_a" in dbg and l == 0:
            P.dma(dr["dbg_o_a"].rearrange("(b p) f -> p b f", p=128), o_tok[:], reads=[otB], q="pool")
        P.barrier()
        merge_branch(l, True, 512, dr["w_br_a"][l], oaT, [oaB], OFF["merge"])
        P.barrier()


    def phase_B(l, first):
        a = [ARENA]

        def al(name, shape, dt):
            nb = (int(np.prod(shape[1:])) * (4 if dt in (F32, I32) else 2) + 31) // 32 * 32
            t = salloc(name, shape, dt, at=a[0])
            a[0] += nb
            return t
        xbcs = al("xbcs", [128, 12, TBW], BF16)
        zs = al("zs", [128, 8, TBW], BF16)
        xpre = [(al(f"xpre{i}", [128, 516], F32), Buf(f"xpre{i}")) for i in range(2)]
        xppool = Rot(xpre)
        carry = al("carry", [128, 12, 4], F32)
        state = al("state", [128, 2, 512], F32)
        stbf = al("stbf", [128, 2, 512], BF16)
        sB_ = al("ssm_small", [128, 128], F32)
        sm4 = al("ssm_small4", [128, 8, 64], F32)
        s4B = Buf("ssm_small4")
        Rhl = al("Rhl", [128, 2, 4, 128], BF16)
        dthl = al("dthl", [128, 32], F32)
        dthb = al("dthb", [128, 16], BF16)
        hlB = Buf("dthl")
        decs = [(al(f"dec{i}", [128, 4, 128], BF16), Buf(f"dec{i}")) for i in range(2)]
        decpool = Rot(decs)
        cbT = al("cbT", [128, 2, 128], BF16)
        xdtz = al("xdtz", [128, 16, 128], BF16)
        xdte = al("xdte", [128, 16, 64], BF16)
        Btok = al("Btok", [128, 2, 128], BF16)
        wdt = al("wdt", [128, KC, 16], BF16)
        bcs = al("bcs", [128, 32], F32)
        dtAx = al("dtAx", [128, 8, 64], F32)
        dxB = Buf("dtAx")
        assert a[0] - ARENA <= ARENA_SZ, (a[0] - ARENA, ARENA_SZ)
        xbB, zsB, caB, stB, sbB, smB, RB, cbB, xzB, xeB, btB, wdB, bcB = [Buf(n) for n in
            ("xbcs", "zs", "carry", "state", "stbf", "ssm_small", "Rhl", "cbT", "xdtz", "xdte", "Btok", "wdt", "bcs")]
        DTX, DT, DTA, CC, DTE, CD, NDA, DTD = [i * 16 for i in range(8)]
        P.barrier()
        src = dr["w_in"][l].rearrange("(k p) n -> p k n", p=128)
        P.dma(wdt[:], src[:, :, OFF["dt"]:OFF["dt"] + 16], writes=[wdB], q="pool")
        P.dma(bcs[:, 0:16], dr["dt_bias"][l:l + 1, :].partition_broadcast(128), writes=[bcB])
        P.dma(bcs[:, 16:32], dr["a_log"][l:l + 1, :].partition_broadcast(128), writes=[bcB])
        P.op("act", lambda e: e.activation(out=bcs[:, 16:32], in_=bcs[:, 16:32], func=AF.Exp), reads=[bcB], writes=[bcB])
        P.op("dve", lambda e: e.tensor_scalar(out=bcs[:, 16:32], in0=bcs[:, 16:32], scalar1=-1.0, scalar2=None, op0=ALU.mult),
             reads=[bcB], writes=[bcB])
        P.op("pool", lambda e: e.memset(state[:], 0.0), writes=[stB])
        P.op("pool", lambda e: e.memset(stbf[:], 0.0), writes=[sbB])
        P.op("pool", lambda e: e.memset(xdtz[:], 0.0), writes=[xzB])
        P.op("pool", lambda e: e.memset(carry[:], 0.0), writes=[caB])
        P.op("pool", lambda e: e.memset(sm4[:], 0.0), writes=[s4B])
        cw = VTC["conv_w"] + l * 48
        cbcol = VTC["conv_b"] + l * 12
        dsk = VTC["d_skip"] + l * 8
        gsn = VTC["ssm_norm"] + l * 8
        for tb in range(NTB):
            ps, psB = pspool.next()
            for ch in range(4):
                gsl4 = slice((tb * 4 + ch) * 128, (tb * 4 + ch + 1) * 128)
                mm_group(ps[:, ch * 16:(ch + 1) * 16], [(hT[:, k, gsl4], wdt[:, k, :]) for k in range(KC)], reads=[wdB, hB[tb]], writes=[psB])
            Q = lambda i: sm4[:, i, :]
            P.op("dve", lambda e: e.tensor_tensor(out=Q(0).rearrange("p (c h) -> p c h", c=4), in0=ps[:, 0:64].rearrange("p (c h) -> p c h", c=4),
                                                  in1=bcs[:, 0:16].unsqueeze(1).broadcast_to([128, 4, 16]), op=ALU.add),
                 reads=[psB, bcB], writes=[s4B])
            P.op("act", lambda e: e.activation(out=Q(0), in_=Q(0), func=AF.Exp), reads=[s4B], writes=[s4B])
            P.op("act", lambda e: e.activation(out=Q(1), in_=Q(0), func=AF.Ln, bias=onec[:, 0:1]), reads=[s4B, cB], writes=[s4B])
            P.op("dve", lambda e: e.tensor_tensor(out=Q(2).rearrange("p (c h) -> p c h", c=4), in0=Q(1).rearrange("p (c h) -> p c h", c=4),
                                                  in1=bcs[:, 16:32].unsqueeze(1).broadcast_to([128, 4, 16]), op=ALU.mult),
                 reads=[s4B, bcB], writes=[s4B])
            psc, pscB = pspool.next()
            P.op("pe", lambda e: e.matmul(psc[:, 0:64], tri32[:], Q(2), start=True, stop=True), reads=[s4B, cB], writes=[pscB])
            pst, pstB = pspool.next()
            P.op("pe", lambda e: e.matmul(pst[:, 0:64], ones32[:], Q(2), start=True, stop=True), reads=[s4B, cB], writes=[pstB])
            copy_op("dve", Q(3), psc[:, 0:64], reads=[pscB], writes=[s4B])
            P.op("dve", lambda e: e.tensor_tensor(out=Q(4), in0=pst[:, 0:64], in1=Q(3), op=ALU.subtract), reads=[pstB, s4B], writes=[s4B])
            P.op("act", lambda e: e.activation(out=Q(4), in_=Q(4), func=AF.Exp), reads=[s4B], writes=[s4B])
            P.op("act", lambda e: e.activation(out=Q(5), in_=pst[:, 0:64], func=AF.Exp), reads=[pstB, s4B], writes=[s4B])
            P.op("dve", lambda e: e.tensor_tensor(out=Q(7), in0=Q(1), in1=Q(4), op=ALU.mult), reads=[s4B], writes=[s4B])
            for c2 in range(4):
                w_, wB_ = load_w(dr["w_in"][l][:, OFF["z"] + c2 * 256:OFF["z"] + (c2 + 1) * 256], D, 256)
                for sub in range(2):
                    c = c2 * 2 + sub
                    ps, psB = pspool.next()
                    mm_group(ps[:], [(w_[:, k, sub * 128:(sub + 1) * 128], hT[:, k, tbs(tb)]) for k in range(KC)],
                             reads=[wB_, hB[tb]], writes=[psB])
                    P.op("act", lambda e: e.activation(out=zs[:, c, :], in_=ps[:], func=AF.Silu), reads=[psB], writes=[zsB])
            def conv_s1(c, w_, wB_, sub):
                ps, psB = pspool.next()
                mm_group(ps[:], [(w_[:, k, sub * 128:(sub + 1) * 128], hT[:, k, tbs(tb)]) for k in range(KC)],
                         reads=[wB_, hB[tb]], writes=[psB])
                xp, xpB = xppool.next()
                P.op("act", lambda e: e.activation(out=xp[:, 3:515], in_=ps[:], func=AF.Copy), reads=[psB], writes=[xpB])
                P.op("act", lambda e: e.activation(out=xp[:, 0:3], in_=carry[:, c, 0:3], func=AF.Copy), reads=[caB], writes=[xpB])
                P.op("act", lambda e: e.activation(out=carry[:, c, 0:3], in_=xp[:, 512:515], func=AF.Copy), reads=[xpB], writes=[caB])
                return (c, xp, xpB)

            def conv_s2(st):
                c, xp, xpB = st
                acc, accB = scrpool.next()
                P.op("dve", lambda e: e.tensor_scalar(out=acc[:], in0=xp[:, 3:515], scalar1=vt[:, cw + 36 + c:cw + 36 + c + 1],
                                                      scalar2=vt[:, cbcol + c:cbcol + c + 1], op0=ALU.mult, op1=ALU.add),
                     reads=[xpB, cB], writes=[accB])
                for j in range(3):
                    P.op("dve", lambda e: e.scalar_tensor_tensor(out=acc[:], in0=xp[:, j:j + 512],
                                                                 scalar=vt[:, cw + j * 12 + c:cw + j * 12 + c + 1], in1=acc[:],
                                                                 op0=ALU.mult, op1=ALU.add), reads=[xpB, accB, cB], writes=[accB])
                P.op("act", lambda e: e.activation(out=xbcs[:, c, :], in_=acc[:], func=AF.Silu), reads=[accB], writes=[xbB])
            pend = None
            for c2 in range(6):
                w_, wB_ = load_w(dr["w_in"][l][:, OFF["xbc"] + c2 * 256:OFF["xbc"] + (c2 + 1) * 256], D, 256)
                for sub in range(2):
                    st = conv_s1(c2 * 2 + sub, w_, wB_, sub)
                    if pend is not None:
                        conv_s2(pend)
                    pend = st
            conv_s2(pend)
            for ch in range(4):
                cg = tb * 4 + ch
                lsl = slice(ch * 128, (ch + 1) * 128)
                gsl = slice(cg * 128, (cg + 1) * 128)
                sm = sB_
                P.op("dve", lambda e: e.tensor_copy(out=sm[:].rearrange("p (q h) -> p q h", q=8), in_=sm4[:, :, ch * 16:(ch + 1) * 16]),
                     reads=[s4B], writes=[smB])
                ps, psB = pspool.next()
                psb = ps[:].bitcast(BF16)

                def ftr(e):
                    ins = None
                    for c in range(8):
                        ins = e.transpose(psb[:, c * 128:(c + 1) * 128], xbcs[:, c, lsl], identb[:])
                    return ins
                P.op("pe", ftr, reads=[xbB, cB], writes=[psB])
                psb4 = psb.rearrange("p (j two d) -> p j two d", two=2, d=64)
                xz4 = xdtz[:].rearrange("p (j two) d -> p j two d", two=2)
                for par in range(2):
                    P.op("dve", lambda e: e.tensor_tensor(
                        out=xz4[:, :, par, par * 64:(par + 1) * 64], in0=psb4[:, :, par, :],
                        in1=sm[:, DT:DT + 16].rearrange("p (j two) -> p j two", two=2)[:, :, par].unsqueeze(2).broadcast_to([128, 8, 64]),
                        op=ALU.mult), reads=[psB, smB], writes=[xzB])
                P.op("dve", lambda e: e.tensor_tensor(out=xdte[:], in0=psb.rearrange("p (h d) -> p h d", d=64),
                                                      in1=sm[:, DTD:DTD + 16].unsqueeze(2).broadcast_to([128, 16, 64]), op=ALU.mult),
                     reads=[psB, smB], writes=[xeB])
                ps, psB = pspool.next()
                psb = ps[:].bitcast(BF16)

                def ftb(e):
                    ins = None
                    for g in range(2):
                        ins = e.transpose(psb[:, g * 128:(g + 1) * 128], xbcs[:, 8 + g, lsl], identb[:])
                    return ins
                P.op("pe", ftb, reads=[xbB, cB], writes=[psB])
                copy_op("act", Btok[:].rearrange("p g n -> p (g n)"), psb[:, 0:256], reads=[psB], writes=[btB])
                ps, psB = pspool.next()

                def fcb(e):
                    ins = None
                    for g in range(2):
                        ins = e.matmul(ps[:, g * 128:(g + 1) * 128], xbcs[:, 8 + g, lsl], xbcs[:, 10 + g, lsl], start=True, stop=True)
                    return ins
                P.op("pe", fcb, reads=[xbB], writes=[psB])
                copy_op("act", cbT[:].rearrange("p g n -> p (g n)"), ps[:, 0:256], reads=[psB], writes=[cbB])
                ETs = []
                for half in range(2):
                    ps, psB = pspool.next()
                    P.op("dve", lambda e: e.tensor_copy(out=dtAx[:], in_=sm[:, DTA + 8 * half:DTA + 8 * half + 8].unsqueeze(2).broadcast_to([128, 8, 64])),
                         reads=[smB], writes=[dxB])

                    def fet(e):
                        ins = None
                        for cc in range(4):
                            ins = e.matmul(ps[:, cc * 128:(cc + 1) * 128],
                                           dtAx[:, 2 * cc:2 * cc + 2, :].rearrange("p h d -> p (h d)"),
                                           tri32[:], start=True, stop=True)
                        return ins
                    P.op("pe", fet, reads=[dxB, cB], writes=[psB])
                    ET, ETB = scrpool.next()
                    P.op("act", lambda e: e.activation(out=ET[:], in_=ps[:], func=AF.Exp), reads=[psB], writes=[ETB])
                    pso, psoB = pspool.next()

                    def fyo(e):
                        ins = None
                        for cc in range(4):
                            c = half * 4 + cc
                            ins = e.matmul(pso[:, cc * 128:(cc + 1) * 128], stbf[:, c // 4, (c % 4) * 128:(c % 4 + 1) * 128],
                                           xbcs[:, 10 + c // 4, lsl], start=True, stop=True)
                        return ins
                    P.op("pe", fyo, reads=[sbB, xbB], writes=[psoB])
                    P.op("dve", lambda e: e.tensor_tensor(out=ET[:], in0=pso[:], in1=ET[:], op=ALU.mult), reads=[psoB, ETB], writes=[ETB])
                    ETs.append((ET, ETB))
                P.op("dve", lambda e: e.tensor_copy(out=dthb[:], in_=sm[:, DTA:DTA + 16]), reads=[smB], writes=[hlB])
                P.op("dve", lambda e: e.tensor_copy(out=dthl[:, 0:16], in_=dthb[:]), reads=[hlB], writes=[hlB])
                P.op("dve", lambda e: e.tensor_tensor(out=dthl[:, 16:32], in0=sm[:, DTA:DTA + 16], in1=dthl[:, 0:16], op=ALU.subtract),
                     reads=[smB, hlB], writes=[hlB])

                def dec_s1(q4):
                    for hl in range(2):
                        P.op("dve", lambda e: e.tensor_tensor(
                            out=Rhl[:, hl, :, :], in0=dthl[:, hl * 16 + 4 * q4:hl * 16 + 4 * q4 + 4].unsqueeze(2).broadcast_to([128, 4, 128]),
                            in1=tri32[:].unsqueeze(1).broadcast_to([128, 4, 128]), op=ALU.mult), reads=[hlB, cB], writes=[RB])
                    psx, psxB = pspool.next()

                    def fdec(e):
                        e.matmul(psx[:], sutb[:], Rhl[:, 0, :, :].rearrange("p j l -> p (j l)"), start=True, stop=False)
                        e.matmul(psx[:], sutb[:], Rhl[:, 1, :, :].rearrange("p j l -> p (j l)"), start=False, stop=False)
                        return e.matmul(psx[:], identb[:], negs[:].unsqueeze(1).broadcast_to([128, 4, 128]), start=False, stop=True)
                    P.op("pe", fdec, reads=[RB, cB], writes=[psxB])
                    dec, decB = decpool.next()
                    P.op("act", lambda e: e.activation(out=dec[:].rearrange("p j l -> p (j l)"), in_=psx[:], func=AF.Exp),
                         reads=[psxB], writes=[decB])
                    return (q4, dec, decB)

                def dec_s2(st, psd, psdB):
                    q4, dec, decB = st
                    g = q4 // 2
                    P.op("dve", lambda e: e.tensor_tensor(out=dec[:], in0=dec[:], in1=cbT[:, g:g + 1, :].broadcast_to([128, 4, 128]),
                                                          op=ALU.mult), reads=[decB, cbB], writes=[decB])

                    def fyd(e):
                        ins = None
                        for cc2 in range(2):
                            c = q4 * 2 + cc2
                            col = (c % 4) * 128
                            e.matmul(psd[:, col:col + 128], xdtz[:, 2 * c, :], dec[:, (2 * c) % 4, :], start=True, stop=False)
                            ins = e.matmul(psd[:, col:col + 128], xdtz[:, 2 * c + 1, :], dec[:, (2 * c + 1) % 4, :], start=False, stop=True)
                        return ins
                    P.op("pe", fyd, reads=[xzB, decB], acc=[psdB])

                def y_tail(half, psd, psdB):
                    ET, ETB = ETs[half]
                    P.op("dve", lambda e: e.tensor_tensor(out=ET[:], in0=psd[:], in1=ET[:], op=ALU.add), reads=[psdB, ETB], writes=[ETB])
                    t2, t2B = scrpool.next()
                    t23 = t2[:].rearrange("p (c t) -> p c t", c=4)
                    P.op("pool", lambda e: e.tensor_tensor(
                        out=t23, in0=xbcs[:, half * 4:(half + 1) * 4, lsl],
                        in1=vt[:, dsk + half * 4:dsk + half * 4 + 4].unsqueeze(2).broadcast_to([128, 4, 128]), op=ALU.mult),
                        reads=[xbB, cB], writes=[t2B])
                    P.op("pool", lambda e: e.tensor_tensor(out=t2[:], in0=t2[:], in1=ET[:], op=ALU.add), reads=[t2B, ETB], writes=[t2B])
                    P.op("pool", lambda e: e.tensor_tensor(out=zs[:, half * 4:(half + 1) * 4, lsl], in0=zs[:, half * 4:(half + 1) * 4, lsl],
                                                           in1=t23, op=ALU.mult), reads=[t2B, zsB], writes=[zsB])
                psds = [pspool.next(), pspool.next()]
                pend = dec_s1(0)
                for q4 in range(4):
                    nx = dec_s1(q4 + 1) if q4 < 3 else None
                    dec_s2(pend, *psds[q4 // 2])
                    if q4 % 2 == 1:
                        y_tail(q4 // 2, *psds[q4 // 2])
                    pend = nx
                for g in range(2):
                    pss, pssB = pspool.next()
                    P.op("pe", lambda e: e.matmul(pss[:], Btok[:, g, :], xdte[:, 8 * g:8 * g + 8, :].rearrange("p h d -> p (h d)"),
                                                  start=True, stop=True), reads=[btB, xeB], writes=[pssB])
                    st3 = state[:, g, :].rearrange("p (h d) -> p h d", d=64)
                    P.op("dve", lambda e: e.tensor_tensor(out=st3, in0=st3,
                                                          in1=sm[:, CD + 8 * g:CD + 8 * g + 8].unsqueeze(2).broadcast_to([128, 8, 64]),
                                                          op=ALU.mult), reads=[stB, smB], writes=[stB])
                    P.op("dve", lambda e: e.tensor_tensor(out=state[:, g, :], in0=state[:, g, :], in1=pss[:], op=ALU.add),
                         reads=[stB, pssB], writes=[stB])
                    copy_op("act", stbf[:, g, :], state[:, g, :], reads=[stB], writes=[sbB])
            pq, pqB = pspool.next()
            for c in range(8):
                sq, sqB = scrpool.next()
                P.op("act", lambda e: e.activation(out=sq[:].bitcast(BF16)[:, 0:512], in_=zs[:, c, :], func=AF.Square), reads=[zsB], writes=[sqB])
                P.op("pe", lambda e: e.matmul(pq[:], onesb[:], sq[:].bitcast(BF16)[:, 0:512], start=(c == 0), stop=(c == 7)), reads=[sqB, cB], acc=[pqB])
            rs, rsB = scrpool.next()
            rms_rstd(pq[:], 1024, rs[:], [pqB], [rsB])
            for c in range(8):
                P.op("dve", lambda e: e.scalar_tensor_tensor(out=zs[:, c, :], in0=zs[:, c, :], scalar=vt[:, gsn + c:gsn + c + 1], in1=rs[:],
                                                             op0=ALU.mult, op1=ALU.mult), reads=[zsB, rsB, cB], writes=[zsB])
            if "o_b" in dbg and l == 0:
                for c in range(8):
                    P.dma(dr["dbg_o_bT"][c * 128:(c + 1) * 128, tbs(tb)], zs[:, c, :], reads=[zsB], q="pool")
            merge_branch(l, first, 1024, dr["w_br_b"][l], zs, [zsB], OFF["merge"] + D, tb_list=[tb], local=True)
        P.barrier()

    SCALE_C = 64.0 ** -0.5
    import os as _os
    NBIS = int(_os.environ.get("DSA_NBIS", "12"))
    NEG = -1.0e30

    def t5_thresholds():
        n = np.arange(0, 400)
        nf = np.maximum(n, 16).astype(np.float32)
        lr = (np.log(nf / np.float32(16.0)) / np.float32(math.log(128 / 16))).astype(np.float32)
        large = np.minimum(16 + (lr * np.float32(16.0)).astype(np.int32), 31)
        bucket = np.where(n < 16, n, large)
        return [int(np.argmax(bucket >= j)) for j in range(1, 32)]

    ebB = Buf("EBr")

    def setup_bias():
        base = ARENA + 16384
        posr_i = salloc("posr_i", [128, 256], I32, at=base)
        posc_i = salloc("posc_i", [128, 1], I32, at=base + 1024)
        posr = salloc("posr", [128, 256], F32, at=base + 2048)
        posc = salloc("posc", [128, 1], F32, at=base + 3072)
        Dd = salloc("Dd", [128, 2, 128], F32, at=base + 4096)
        ind = salloc("ind", [128, 2, 128], F32, at=base + 5120)
        accb = salloc("accb", [128, 2, 8, 128], F32, at=base + 6144)
        rbb = salloc("rbb", [128, 256], F32, at=base + 14336)
        dl = salloc("dl", [128, 248], F32, at=base + 15360)
        bs = salloc("bs", [128, 8], F32, at=base + 16384)
        tB = Buf("biastmp")
        P.dma(posr_i[:], dr["pos"][:, 0:256].partition_broadcast(128), writes=[tB])
        P.dma(posc_i[:], dr["pos"][0, 0:128].rearrange("(p o) -> p o", o=1), writes=[tB])
        P.dma(rbb[:], dr["rel_bias"].partition_broadcast(128), writes=[tB])
        P.dma(negm[:], dr["negm"], writes=[cB])
        P.op("dve", lambda e: e.tensor_copy(out=posr[:], in_=posr_i[:]), reads=[tB], writes=[tB])
        P.op("dve", lambda e: e.tensor_copy(out=posc[:], in_=posc_i[:]), reads=[tB], writes=[tB])
        P.op("dve", lambda e: e.tensor_scalar(out=Dd[:].rearrange("p a t -> p (a t)"), in0=posr[:], scalar1=posc[:, 0:1],
                                              scalar2=None, op0=ALU.subtract), reads=[tB], writes=[tB])
        P.op("dve", lambda e: e.tensor_tensor(out=dl[:], in0=rbb[:, 8:256], in1=rbb[:, 0:248], op=ALU.subtract), reads=[tB], writes=[tB])
        P.op("dve", lambda e: e.tensor_tensor(out=bs[:], in0=rbb[:, 0:8], in1=rbb[:, 248:256], op=ALU.subtract), reads=[tB], writes=[tB])
        P.op("dve", lambda e: e.memset(accb[:], 0.0), writes=[tB])
        for j, T in enumerate(t5_thresholds()):
            P.op("dve", lambda e: e.tensor_scalar(out=ind[:], in0=Dd[:], scalar1=float(T) - 0.5, scalar2=None, op0=ALU.is_ge),
                 reads=[tB], writes=[tB])
            for h in range(8):
                P.op("dve", lambda e: e.scalar_tensor_tensor(out=accb[:, :, h, :], in0=ind[:], scalar=dl[:, j * 8 + h:j * 8 + h + 1],
                                                             in1=accb[:, :, h, :], op0=ALU.mult, op1=ALU.add), reads=[tB], writes=[tB])
        for h in range(8):
            P.op("act", lambda e: e.activation(out=EBr[:, :, (h // 2) + 4 * (h % 2), :], in_=accb[:, :, h, :], func=AF.Exp, bias=bs[:, h:h + 1]),
                 reads=[tB], writes=[ebB])
        P.barrier()

    class _Stop(Exception):
        pass
    DSA_STOP = int(_os.environ.get("DSA_STOP", "0"))

    def stop_at(n):
        if DSA_STOP == n:
            raise _Stop()

    def phase_C(l, first):
        try:
            phase_C_inner(l, first)
        except _Stop:
            pass
        P.barrier()

    def phase_C_inner(l, first):
        a = [ARENA]

        def al(name, shape, dt):
            nb = (int(np.prod(shape[1:])) * (4 if dt in (F32, I32) else 2) + 31) // 32 * 32
            t = salloc(name, shape, dt, at=a[0])
            a[0] += nb
            return t
        kc2 = al("kc2", [128, S], BF16)
        kidx2 = al("kidx2", [128, S], BF16)
        vc = al("vc_aug", [128, NBLK, 66], BF16)
        qcT = al("qcT", [128, 4, TBW], BF16)
        qiT = al("qiT", [128, 4, TBW], BF16)
        Isc = al("Isc", [128, S], F32)
        msk = al("msk", [128, S], BF16)
        mskT2 = [al(f"mskT{i}", [128, NBLK, 128], BF16) for i in range(2)]
        mtB2 = [Buf("mskT0"), Buf("mskT1")]
        recC = al("recC", [128, 8], F32)
        rcB = Buf("recC")
        PTs = [(al(f"PT{i}", [128, 8, 128], BF16), Buf(f"PT{i}")) for i in range(2)]
        PTpool = Rot(PTs)
        ocT = al("ocT", [128, 4, TBW], BF16)
        oblk = al("oblk", [128, 512], BF16)
        gsc = al("gsc", [128, 512], BF16)
        relupool = scrpool
        sm = al("smC", [128, 32], F32)
        widx = al("widx", [128, NBLK, 8], F32)
        assert a[0] - ARENA <= ARENA_SZ, (a[0] - ARENA, ARENA_SZ)
        kcB, kiB, vcB, qcB, qiB, IB, mkB, ocB, obB, gsB, smB, wxB = [Buf(n) for n in
            ("kc2", "kidx2", "vc", "qcT", "qiT", "Isc", "msk", "ocT", "oblk", "gsc", "smC", "widx")]
        WI, LO, W0, MID, CNT, G, REC = 0, 8, 9, 10, 11, 12, 16
        P.barrier()
        src = dr["w_in"][l].rearrange("(k p) n -> p k n", p=128)
        b1, b1B = wpool.next()
        wkc = b1[:, 0:1024].rearrange("p (k n) -> p k n", k=KC)
        wki = b1[:, 1024:2048].rearrange("p (k n) -> p k n", k=KC)
        for hh in range(2):
            P.dma(wkc[:, :, hh * 64:(hh + 1) * 64], src[:, :, OFF["k_c"]:OFF["k_c"] + 64], writes=[b1B], q="pool")
            P.dma(wki[:, :, hh * 64:(hh + 1) * 64], src[:, :, OFF["k_idx"]:OFF["k_idx"] + 64], writes=[b1B], q="pool")
        b2, b2B = wpool.next()
        wvc = b2[:, 0:512].rearrange("p (k n) -> p k n", k=KC)
        wwi = b2[:, 512:576].rearrange("p (k n) -> p k n", k=KC)
        P.dma(wvc, src[:, :, OFF["v_c"]:OFF["v_c"] + 64], writes=[b2B], q="pool")
        P.dma(wwi, src[:, :, OFF["w_idx"]:OFF["w_idx"] + 8], writes=[b2B], q="pool")
        P.op("pool", lambda e: e.memset(vc[:, :, 64:66], 1.0), writes=[vcB])
        P.op("pool", lambda e: e.memset(sm[:], 0.0), writes=[smB])
        for tb in range(NTB):
            for (w_, dst, dB) in ((wkc, kc2, kcB), (wki, kidx2, kiB)):
                ps, psB = pspool.next()
                mm_group(ps[:], [(w_[:, k, :], hT[:, k, tbs(tb)]) for k in range(KC)], reads=[b1B, hB[tb]], writes=[psB])
                copy_op(evac_engine(), dst[:, tbs(tb)], ps[:], reads=[psB], writes=[dB])
            ps, psB = pspool.next()

            def fnv(e):
                ins = None
                for j in range(4):
                    blk = tb * 4 + j
                    for k in range(KC):
                        ins = e.matmul(ps[:, j * 64:(j + 1) * 64], hT[:, k, blk * 128:(blk + 1) * 128], wvc[:, k, :],
                                       start=(k == 0), stop=(k == KC - 1))
                return ins
            P.op("pe", fnv, reads=[b2B, hB[tb]], writes=[psB])
            copy_op(evac_engine(), vc[:, tb * 4:(tb + 1) * 4, 0:64], ps[:, 0:256].rearrange("p (j d) -> p j d", j=4),
                    reads=[psB], writes=[vcB])
            ps, psB = pspool.next()

            def fnw(e):
                ins = None
                for j in range(4):
                    blk = tb * 4 + j
                    for k in range(KC):
                        ins = e.matmul(ps[:, j * 8:(j + 1) * 8], hT[:, k, blk * 128:(blk + 1) * 128], wwi[:, k, :],
                                       start=(k == 0), stop=(k == KC - 1))
                return ins
            P.op("pe", fnw, reads=[b2B, hB[tb]], writes=[psB])
            copy_op("dve", widx[:, tb * 4:(tb + 1) * 4, :], ps[:, 0:32].rearrange("p (j d) -> p j d", j=4), reads=[psB], writes=[wxB])
        stop_at(1)
        for tb in range(NTB):
            for (name, dstT, dB) in (("q_c", qcT, qcB), ("q_idx", qiT, qiB)):
                for c2 in range(2):
                    w_, wB_ = load_w(dr["w_in"][l][:, OFF[name] + c2 * 256:OFF[name] + (c2 + 1) * 256], D, 256)
                    for sub in range(2):
                        ps, psB = pspool.next()
                        mm_group(ps[:], [(w_[:, k, sub * 128:(sub + 1) * 128], hT[:, k, tbs(tb)]) for k in range(KC)],
                                 reads=[wB_, hB[tb]], writes=[psB])
                        copy_op(evac_engine(), dstT[:, c2 * 2 + sub, :], ps[:], reads=[psB], writes=[dB])
            wg0 = load_w(dr["w_in"][l][:, OFF["gate_c"]:OFF["gate_c"] + 256], D, 256)
            wg1 = load_w(dr["w_in"][l][:, OFF["gate_c"] + 256:OFF["gate_c"] + 512], D, 256)
            def qblock(qi):
                mskT = mskT2[qi % 2]
                mtB = mtB2[qi % 2]
                qb = tb * 4 + qi
                nk = (qb + 1) * 128
                tl = slice(qi * 128, (qi + 1) * 128)
                for k0 in range(0, nk, 512):
                    n = min(512, nk - k0)
                    for h in range(8):
                        hp = slice((h % 2) * 64, (h % 2) * 64 + 64)
                        ps, psB = pspool.next()
                        P.op("pe", lambda e: e.matmul(ps[:, 0:n], qiT[hp, h // 2, tl], kidx2[hp, k0:k0 + n], start=True, stop=True),
                             reads=[qiB, kiB], writes=[psB])
                        if h == 0:
                            P.op("dve", lambda e: e.tensor_scalar(out=Isc[:, k0:k0 + n], in0=ps[:, 0:n], scalar1=0.0,
                                                                  scalar2=widx[:, qb, 0:1], op0=ALU.max, op1=ALU.mult),
                                 reads=[psB, wxB], writes=[IB])
                        else:
                            rl, rlB = relupool.next()
                            P.op("act", lambda e: e.activation(out=rl[:, 0:n], in_=ps[:, 0:n], func=AF.Relu), reads=[psB], writes=[rlB])
                            P.op("dve", lambda e: e.scalar_tensor_tensor(out=Isc[:, k0:k0 + n], in0=rl[:, 0:n],
                                                                         scalar=widx[:, qb, h:h + 1], in1=Isc[:, k0:k0 + n],
                                                                         op0=ALU.mult, op1=ALU.add), reads=[rlB, wxB, IB], writes=[IB])
                yield
                if qb >= 2:
                    P.op("dve", lambda e: e.tensor_reduce(out=sm[:, LO:LO + 1], in_=Isc[:, 0:nk], axis=AX.X, op=ALU.min),
                         reads=[IB], writes=[smB])
                    P.op("dve", lambda e: e.tensor_reduce(out=sm[:, W0:W0 + 1], in_=Isc[:, 0:nk - 128], axis=AX.X, op=ALU.max),
                         reads=[IB], writes=[smB])
                P.op("dve", lambda e: e.tensor_tensor(out=Isc[:, nk - 128:nk], in0=Isc[:, nk - 128:nk], in1=negm[:], op=ALU.add),
                     reads=[IB, cB], writes=[IB])
                if qb >= 2:
                    rl, rlB = relupool.next()
                    P.op("dve", lambda e: e.tensor_reduce(out=rl[:, 0:1], in_=Isc[:, nk - 128:nk], axis=AX.X, op=ALU.max),
                         reads=[IB], writes=[rlB])
                    P.op("dve", lambda e: e.tensor_tensor(out=sm[:, W0:W0 + 1], in0=sm[:, W0:W0 + 1], in1=rl[:, 0:1], op=ALU.max),
                         reads=[rlB, smB], writes=[smB])
                    P.op("dve", lambda e: e.tensor_tensor(out=sm[:, W0:W0 + 1], in0=sm[:, W0:W0 + 1], in1=sm[:, LO:LO + 1], op=ALU.subtract),
                         reads=[smB], writes=[smB])
                    midB, midmB, cntB, gB = Buf("mid"), Buf("midm"), Buf("cnt"), Buf("g")
                    MIDM = 13
                    P.op("dve", lambda e: e.scalar_tensor_tensor(out=sm[:, MID:MID + 1], in0=sm[:, W0:W0 + 1], scalar=0.5,
                                                                 in1=sm[:, LO:LO + 1], op0=ALU.mult, op1=ALU.add),
                         reads=[smB], writes=[midB])
                    for it in range(NBIS):
                        q = 0.5 ** (it + 2)
                        P.op("dve", lambda e: e.scalar_tensor_tensor(out=sm[:, MIDM:MIDM + 1], in0=sm[:, W0:W0 + 1], scalar=-q,
                                                                     in1=sm[:, MID:MID + 1], op0=ALU.mult, op1=ALU.add),
                             reads=[smB, midB], writes=[midmB])
                        P.op("dve", lambda e: e.tensor_scalar(out=msk[:, 0:nk], in0=Isc[:, 0:nk], scalar1=sm[:, MID:MID + 1], scalar2=0.0,
                                                              op0=ALU.is_ge, op1=ALU.add, accum_out=sm[:, CNT:CNT + 1]),
                             reads=[IB, midB], writes=[mkB, cntB])
                        P.op("dve", lambda e: e.tensor_scalar(out=sm[:, G:G + 1], in0=sm[:, CNT:CNT + 1], scalar1=255.5, scalar2=2.0 * q,
                                                              op0=ALU.is_ge, op1=ALU.mult), reads=[cntB], writes=[gB])
                        P.op("dve", lambda e: e.scalar_tensor_tensor(out=sm[:, MID:MID + 1], in0=sm[:, G:G + 1], scalar=sm[:, W0:W0 + 1],
                                                                     in1=sm[:, MIDM:MIDM + 1], op0=ALU.mult, op1=ALU.add),
                             reads=[smB, gB, midmB], writes=[midB])
                    P.op("dve", lambda e: e.scalar_tensor_tensor(out=sm[:, LO:LO + 1], in0=sm[:, W0:W0 + 1], scalar=-(0.5 ** (NBIS + 1)),
                                                                 in1=sm[:, MID:MID + 1], op0=ALU.mult, op1=ALU.add),
                         reads=[smB, midB], writes=[smB])
                    P.op("dve", lambda e: e.tensor_scalar(out=msk[:, 0:nk], in0=Isc[:, 0:nk], scalar1=sm[:, LO:LO + 1], scalar2=None,
                                                          op0=ALU.is_ge), reads=[IB, smB], writes=[mkB])
                else:
                    P.op("dve", lambda e: e.tensor_scalar(out=msk[:, 0:nk], in0=Isc[:, 0:nk], scalar1=-1.0e29, scalar2=None,
                                                          op0=ALU.is_ge), reads=[IB], writes=[mkB])
                if "sm" in dbg and l == 0:
                    P.dma(dr["dbg_sm"][qb * 128:(qb + 1) * 128, :], sm[:], reads=[smB])
                    if qb == 2:
                        P.dma(dr["dbg_I"][:, 0:nk], Isc[:, 0:nk], reads=[IB])
                        P.dma(dr["dbg_msk"][:, 0:nk], msk[:, 0:nk], reads=[mkB], q="pool")
                for g0 in range(0, qb + 1, 8):
                    gn = min(8, qb + 1 - g0)
                    ps, psB = pspool.next()
                    psb = ps[:].bitcast(BF16)

                    def ftr(e):
                        ins = None
                        for i in range(gn):
                            ins = e.transpose(psb[:, i * 128:(i + 1) * 128], msk[:, (g0 + i) * 128:(g0 + i + 1) * 128], identb[:])
                        return ins
                    P.op("pe", ftr, reads=[mkB, cB], writes=[psB])
                    P.op("dve", lambda e: e.tensor_scalar(out=mskT[:, g0:g0 + gn, :], in0=psb[:, 0:gn * 128].rearrange("p (g t) -> p g t", g=gn),
                                                          scalar1=-1.0, scalar2=30000.0, op0=ALU.add, op1=ALU.mult),
                         reads=[psB], writes=[mtB])
                yield
                po0, po0B = accpool.next()
                po1, po1B = accpool.next()
                P.op("dve", lambda e: e.memset(po0[:], 0.0), writes=[po0B])
                P.op("dve", lambda e: e.memset(po1[:], 0.0), writes=[po1B])
                pos_ = (po0[:].rearrange("p (j d) -> p j d", j=4), po1[:].rearrange("p (j d) -> p j d", j=4))
                def pv_c(st):
                    kb, PT, PTB = st
                    for half in range(2):
                        def fpv(e):
                            ins = None
                            for hh in range(4):
                                ins = e.matmul(pos_[half][:, hh, 0:65], PT[:, half * 4 + hh, :], vc[:, kb, 0:65],
                                               start=False, stop=(kb == qb), skip_group_check=True)
                            return ins
                        P.op("pe", fpv, reads=[PTB, vcB], acc=[(po0B, po1B)[half]])
                pend = None
                for kb in range(qb + 1):
                    PT, PTB = PTpool.next()
                    for half in range(2):
                        ps, psB = pspool.next()

                        def fqk(e):
                            ins = e.matmul(ps[:], identb[:], mskT[:, kb:kb + 1, :].broadcast_to([128, 4, 128]), start=True, stop=False)
                            for hh in range(4):
                                h = hh * 2 + half
                                hp = slice((h % 2) * 64, (h % 2) * 64 + 64)
                                ins = e.matmul(ps[:, hh * 128:(hh + 1) * 128], kc2[hp, kb * 128:(kb + 1) * 128], qcT[hp, h // 2, tl],
                                               start=False, stop=(hh == 3))
                            return ins
                        P.op("pe", fqk, reads=[kcB, qcB, mtB, cB], writes=[psB])
                        P.op("act", lambda e: e.activation(out=PT[:, half * 4:(half + 1) * 4, :].rearrange("p h t -> p (h t)"), in_=ps[:],
                                                           func=AF.Exp, scale=SCALE_C), reads=[psB], writes=[PTB])
                    if qb - kb <= 1:
                        P.op("pool", lambda e: e.tensor_tensor(out=PT[:], in0=PT[:], in1=EBr[:, qb - kb, :, :], op=ALU.mult),
                             reads=[PTB, ebB], writes=[PTB])
                    if pend is not None:
                        pv_c(pend)
                    pend = (kb, PT, PTB)
                pv_c(pend)
                yield
                for hf, (wg, wgB) in enumerate((wg0, wg1)):
                    ps, psB = pspool.next()
                    mm_group(ps[:, 0:256], [(hT[:, k, qb * 128:(qb + 1) * 128], wg[:, k, :]) for k in range(KC)],
                             reads=[wgB, hB[tb]], writes=[psB])
                    P.op("act", lambda e: e.activation(out=gsc[:, hf * 256:(hf + 1) * 256], in_=ps[:, 0:256], func=AF.Silu),
                         reads=[psB], writes=[gsB])
                for half in range(2):
                    P.op("dve", lambda e: e.reciprocal(out=recC[:, half * 4:half * 4 + 4], in_=pos_[half][:, :, 64]),
                         reads=[(po0B, po1B)[half]], writes=[rcB])
                    P.op("dve", lambda e: e.tensor_tensor(
                        out=oblk[:].rearrange("p (j two d) -> p j two d", two=2, d=64)[:, :, half, :], in0=pos_[half][:, :, 0:64],
                        in1=recC[:, half * 4:half * 4 + 4].unsqueeze(2).broadcast_to([128, 4, 64]), op=ALU.mult),
                        reads=[(po0B, po1B)[half], rcB], writes=[obB])
                if "o_c" in dbg and l == 0:
                    P.op("pool", lambda e: e.tensor_tensor(out=gsc[:], in0=oblk[:], in1=gsc[:], op=ALU.mult), reads=[gsB, obB], writes=[gsB])
                    P.dma(dr["dbg_o_c"][qb * 128:(qb + 1) * 128, :], gsc[:], reads=[gsB], q="pool")
                    P.op("dve", lambda e: e.tensor_copy(out=oblk[:], in_=gsc[:]), reads=[gsB, obB], writes=[obB])
                else:
                    P.op("dve", lambda e: e.tensor_tensor(out=oblk[:], in0=oblk[:], in1=gsc[:], op=ALU.mult), reads=[gsB, obB], writes=[obB])
                ps, psB = pspool.next()
                psb = ps[:].bitcast(BF16)

                def ftr2(e):
                    ins = None
                    for c in range(4):
                        ins = e.transpose(psb[:, c * 128:(c + 1) * 128], oblk[:, c * 128:(c + 1) * 128], identb[:])
                    return ins
                P.op("pe", ftr2, reads=[obB, cB], writes=[psB])
                copy_op(evac_engine(), ocT[:, :, tl], psb[:, 0:512].rearrange("p (c t) -> p c t", c=4), reads=[psB], writes=[ocB])
                yield
            gens = [qblock(qi) for qi in range(4)]
            next(gens[0]); next(gens[0])
            for qi in range(1, 4):
                next(gens[qi])
                next(gens[qi - 1])
                next(gens[qi])
                next(gens[qi - 1])
            next(gens[3]); next(gens[3])
            stop_at(7)
            merge_branch(l, first, 512, dr["w_br_c"][l], ocT, [ocB], OFF["merge"] + 2 * D, tb_list=[tb], local=True)

    if use_c:
        setup_bias()
    for l in range(depth):
        gcol = VTC["norm_g"] + l * 8
        for tb in range(NTB):
            ps, psB = pspool.next()
            for c in range(KC):
                sq, sqB = scrpool.next()
                P.op("act", lambda e, sq=sq, c=c, tb=tb: e.activation(out=sq[:].bitcast(BF16)[:, 0:512], in_=xT[:, c, tbs(tb)], func=AF.Square),
                     reads=[xB[tb]], writes=[sqB])
                P.op("pe", lambda e, sq=sq, c=c, ps=ps: e.matmul(ps[:], onesb[:], sq[:].bitcast(BF16)[:, 0:512], start=(c == 0), stop=(c == KC - 1)),
                     reads=[sqB, cB], writes=[psB])
            rs, rsB = scrpool.next()
            rms_rstd(ps[:], D, rs[:], [psB], [rsB])
            for c in range(KC):
                P.op("dve", lambda e, c=c, tb=tb, rs=rs: e.scalar_tensor_tensor(
                    out=hT[:, c, tbs(tb)], in0=xT[:, c, tbs(tb)], scalar=vt[:, gcol + c:gcol + c + 1], in1=rs[:],
                    op0=ALU.mult, op1=ALU.mult), reads=[xB[tb], rsB, cB], writes=[hB[tb]])

        any_branch = use_a or use_b or use_c
        if use_a:
            phase_A(l)
        if use_b:
            phase_B(l, first=not use_a)
        if use_c:
            phase_C(l, first=not (use_a or use_b))

        if any_branch:
            for oc2 in range(4):
                w, wB = load_w(dr["w_out"][l][:, oc2 * 256:(oc2 + 1) * 256], D, 256)
                for sub in range(2):
                    oc = oc2 * 2 + sub
                    for tb in range(NTB):
                        ps, psB = pspool.next()
                        mm_group(ps[:], [(w[:, k, sub * 128:(sub + 1) * 128], mT[:, k, tbs(tb)]) for k in range(KC)],
                                 reads=[wB, mB[tb]], writes=[psB])
                        P.op("dve", lambda e, oc=oc, tb=tb, ps=ps: e.tensor_tensor(
                            out=xT[:, oc, tbs(tb)], in0=xT[:, oc, tbs(tb)], in1=ps[:], op=ALU.add),
                            reads=[psB, xB[tb]], writes=[xB[tb]])
        for tb in range(NTB):
            copy_op("dve", hT[:, :, tbs(tb)], xT[:, :, tbs(tb)], reads=[xB[tb]], writes=[hB[tb]])
            pt, ptB = tokpool.next()
            P.dma(pt[:].rearrange("p (j f) -> p j f", j=4),
                  dr["p"][l][tb * TBW:(tb + 1) * TBW, :].rearrange("(j p) f -> p j f", p=128), writes=[ptB])
            for c in range(2):
                ps, psB = pspool.next()

                def fn(e, pt=pt, ps=ps, c=c):
                    ins = None
                    for j in range(4):
                        ins = e.transpose(ps[:, j * 128:(j + 1) * 128], pt[:, j * 256 + c * 128:j * 256 + (c + 1) * 128], ident[:])
                    return ins
                P.op("pe", fn, reads=[ptB, cB], writes=[psB])
                copy_op(evac_engine(), mT[:, c, tbs(tb)], ps[:], reads=[psB], writes=[mB[tb]])
        for oc2 in range(4):
            wg, wgB = load_w(dr["w_ple_gate"][l][:, oc2 * 256:(oc2 + 1) * 256], D, 256)
            wp, wpB = load_w(dr["w_ple"][l][:, oc2 * 256:(oc2 + 1) * 256], 256, 256)
            for sub in range(2):
                oc = oc2 * 2 + sub
                for tb in range(NTB):
                    ps, psB = pspool.next()
                    mm_group(ps[:], [(wg[:, k, sub * 128:(sub + 1) * 128], hT[:, k, tbs(tb)]) for k in range(KC)],
                             reads=[wgB, hB[tb]], writes=[psB])
                    sg, sgB = scrpool.next()
                    P.op("act", lambda e, sg=sg, ps=ps: e.activation(out=sg[:], in_=ps[:], func=AF.Sigmoid),
                         reads=[psB], writes=[sgB])
                    ps2, ps2B = pspool.next()
                    mm_group(ps2[:], [(wp[:, k, sub * 128:(sub + 1) * 128], mT[:, k, tbs(tb)]) for k in range(2)],
                             reads=[wpB, mB[tb]], writes=[ps2B])
                    P.op("dve", lambda e, sg=sg, ps2=ps2: e.tensor_tensor(out=sg[:], in0=sg[:], in1=ps2[:], op=ALU.mult),
                         reads=[ps2B, sgB], writes=[sgB])
                    P.op("dve", lambda e, sg=sg, oc=oc, tb=tb: e.tensor_tensor(
                        out=xT[:, oc, tbs(tb)], in0=xT[:, oc, tbs(tb)], in1=sg[:], op=ALU.add),
                        reads=[sgB, xB[tb]], writes=[xB[tb]])

    yB = Buf("y")
    P.barrier()
    fnbB = Buf("fnb")
    P.dma(fnb[:], dr["fnb"], writes=[fnbB])
    for blk in range(NBLK):
        ot, otB = tokpool.next()
        P.op("dve", lambda e: e.memset(small[:, 0:2], 0.0), writes=[smallB])
        pss = []
        for half in range(2):
            ps, psB = pspool.next()

            def fn(e, ps=ps, half=half, blk=blk):
                ins = None
                for j in range(4):
                    c = half * 4 + j
                    ins = e.transpose(ps[:, j * 128:(j + 1) * 128], xT[:, c, blk * 128:(blk + 1) * 128], ident[:])
                return ins
            P.op("pe", fn, reads=[xB[blk // 4], cB], writes=[psB])
            sq, sqB = scrpool.next()
            P.op("act", lambda e, sq=sq, ps=ps, half=half: e.activation(out=sq[:], in_=ps[:], func=AF.Square,
                                                                       accum_out=small[:, half:half + 1]),
                 reads=[psB], writes=[sqB, smallB])
            pss.append((ps, psB))
        P.op("dve", lambda e: e.tensor_tensor(out=small[:, 2:3], in0=small[:, 0:1], in1=small[:, 1:2], op=ALU.add),
             reads=[smallB], writes=[smallB])
        rms_rstd(small[:, 2:3], D, small[:, 3:4], [smallB], [smallB])
        for half in range(2):
            ps, psB = pss[half]
            P.op("dve", lambda e, ps=ps, half=half, ot=ot: e.scalar_tensor_tensor(
                out=ot[:, half * 512:(half + 1) * 512], in0=ps[:], scalar=small[:, 3:4],
                in1=fnb[:, half * 512:(half + 1) * 512], op0=ALU.mult, op1=ALU.mult),
                reads=[psB, smallB, fnbB], writes=[otB])
        P.dma(dr["y"][blk * 128:(blk + 1) * 128, :], ot[:], reads=[otB], writes=[yB])
    P.barrier()
    return nc, P


def rope_consts():
    c = np.zeros((128, 4), np.float32)
    inv_freq = (1.0 / (10000.0 ** (np.arange(0, 32, 2, dtype=np.float32) / 32.0))).astype(np.float32)
    for p in range(64, 96):
        c[p, 0] = inv_freq[(p - 64) % 16]
        c[p, 1] = -1.0 if p < 80 else 1.0
    return c


def host_layout(inputs, b):
    VTC, NV = vt_layout()
    vt = np.zeros((128, NV), np.float32)

    def put(name, arr2d):
        r, f = arr2d.shape
        ch = f // 128
        vt[:, VTC[name]:VTC[name] + r * ch] = arr2d.reshape(r, ch, 128).transpose(2, 0, 1).reshape(128, r * ch)
    put("norm_g", inputs["norm_g"])
    put("q_norm", inputs["mla_q_norm"])
    put("kv_norm", inputs["mla_kv_norm"])
    put("conv_w", inputs["conv_w"].reshape(DEPTH * 4, 1536))
    put("conv_b", inputs["conv_b"])
    put("ssm_norm", inputs["ssm_norm"])
    put("d_skip", np.repeat(inputs["d_skip"], 64, axis=1))
    m = {
        "x": np.ascontiguousarray(inputs["x"][b]),
        "p": np.ascontiguousarray(inputs["p"][:, b]),
        "pos": np.ascontiguousarray(inputs["positions"][b:b + 1]).astype(np.int32),
        "vt": vt,
        "fnb": np.ascontiguousarray(np.broadcast_to(inputs["final_norm"][None, :], (128, D))).astype(np.float32),
        "ident": np.eye(128, dtype=np.float32),
        "tri": np.triu(np.ones((128, 128), np.float32)),
        "ropec": rope_consts(),
        "negm": np.where(np.arange(128)[None, :] > np.arange(128)[:, None], np.float32(-1.0e30), np.float32(0.0)).astype(np.float32),
        "rel_bias": np.ascontiguousarray(inputs["rel_bias"], dtype=np.float32).reshape(1, 256),
        "negs": np.where(np.arange(128)[None, :] < np.arange(128)[:, None], np.float32(-30000.0), np.float32(0.0)).astype(np.float32),
        "sut": np.tril(np.ones((128, 128), np.float32), -1),
        "dt_bias": np.ascontiguousarray(inputs["dt_bias"], dtype=np.float32),
        "a_log": np.ascontiguousarray(inputs["a_log"], dtype=np.float32),
    }
    for k in ("w_in", "w_uq", "w_ukv", "w_br_a", "w_br_b", "w_br_c", "w_out", "w_ple", "w_ple_gate"):
        m[k] = np.ascontiguousarray(inputs[k], dtype=np.float32)
    return m


def kernel(**inputs):
    inputs = {k: np.asarray(v) for k, v in inputs.items()}
    nc, P = build_program()
    in_maps = [host_layout(inputs, b) for b in range(8)]
    res = run_bass_kernel_spmd(nc, in_maps, core_ids=list(range(8)))
    return np.stack([np.asarray(r["y"]) for r in res.results], axis=0).astype(np.float32)
```

```python
import math
import numpy as np
import concourse.bass as bass
import concourse.mybir as mybir
from concourse.bass_utils import run_bass_kernel_spmd

F32 = mybir.dt.float32
BF16 = mybir.dt.bfloat16
I32 = mybir.dt.int32
AF = mybir.ActivationFunctionType
ALU = mybir.AluOpType
AX = mybir.AxisListType

S = 2048
D = 1024
KC = 8
NTB = 4
TBW = 512
NBLK = 16
DEPTH = 4
EPS = 1e-6
IN_TOTAL = 8568
OFF = {}
_o = 0
for _n, _s in [("c_q", 384), ("c_kv", 256), ("k_rope", 32), ("gate_a", 512), ("z", 1024), ("xbc", 1536),
               ("dt", 16), ("q_c", 512), ("k_c", 64), ("v_c", 64), ("q_idx", 512), ("k_idx", 64), ("w_idx", 8),
               ("gate_c", 512), ("merge", 3072)]:
    OFF[_n] = _o
    _o += _s
assert _o == IN_TOTAL


class Buf:
    __slots__ = ("name", "w", "r")

    def __init__(self, name):
        self.name = name
        self.w = None
        self.r = []


class Prog:
    NDMA = 24

    def __init__(self, nc):
        self.nc = nc
        self.e = dict(pe=nc.tensor, act=nc.scalar, dve=nc.vector, pool=nc.gpsimd, sp=nc.sync)
        self.sem = {}
        for k in ("pe", "act", "dve", "pool"):
            self.sem[k] = nc.alloc_semaphore(name=f"s_{k}")
        for j in range(self.NDMA):
            self.sem[f"d{j}"] = nc.alloc_semaphore(name=f"s_d{j}")
            self.sem[f"g{j}"] = nc.alloc_semaphore(name=f"s_g{j}")
        self.cnt = {k: 0 for k in self.sem}
        self.seen = {k: {} for k in self.e}
        self.dma_rr = {"sp": 0, "pool": 0}
        self.swq = []
        self.SW_DESC_BUDGET = 3072
        self.ninst = {k: 0 for k in self.e}

    def _wait(self, eng, toks):
        need = {}
        for t in toks:
            if t is None:
                continue
            k, v = t
            if v > need.get(k, 0):
                need[k] = v
        for k, v in need.items():
            if self.seen[eng].get(k, 0) < v:
                self.e[eng].wait_ge(self.sem[k], v)
                self.seen[eng][k] = v
                self.ninst[eng] += 1

    @staticmethod
    def _flat(bs):
        out = []
        for b in bs:
            if isinstance(b, (list, tuple)):
                out.extend(Prog._flat(b))
            else:
                out.append(b)
        return out

    @staticmethod
    def _deps(reads, writes):
        reads = Prog._flat(reads)
        writes = Prog._flat(writes)
        toks = []
        for b in reads:
            toks.append(b.w)
        for b in writes:
            toks.append(b.w)
            toks.extend(b.r)
        return toks

    def _mark(self, tok, reads, writes):
        reads = Prog._flat(reads)
        writes = Prog._flat(writes)
        for b in reads:
            b.r.append(tok)
        for b in writes:
            b.w = tok
            b.r = []

    def op(self, eng, fn, reads=(), writes=(), acc=()):
        toks = self._deps(reads, writes)
        acc = Prog._flat(acc)
        for b in acc:
            if b.w is not None and b.w[0] != eng:
                toks.append(b.w)
            toks.extend(b.r)
        self._wait(eng, toks)
        ins = fn(self.e[eng])
        self.cnt[eng] += 1
        ins.then_inc(self.sem[eng], 1)
        self.ninst[eng] += 1
        tok = (eng, self.cnt[eng])
        self._mark(tok, reads, list(writes) + list(acc))
        return tok

    def dma(self, out, in_, reads=(), writes=(), q="sp"):
        j = self.dma_rr[q]
        self.dma_rr[q] = (j + 1) % self.NDMA
        k = ("d" if q == "sp" else "g") + str(j)
        toks = self._deps(reads, writes)
        if self.cnt[k] > 0:
            toks.append((k, self.cnt[k]))
        if q == "pool":
            nd = 1
            for d in tuple(out.shape)[:-1]:
                nd *= int(d)
            nd = max(nd, 128)
            while self.swq and sum(x[1] for x in self.swq) + nd > self.SW_DESC_BUDGET:
                toks.append(self.swq.pop(0)[0])
        self._wait(q, toks)
        self.cnt[k] += 16
        self.e[q].dma_start(out=out, in_=in_).then_inc(self.sem[k], 16)
        self.ninst[q] += 1
        tok = (k, self.cnt[k])
        if q == "pool":
            self.swq.append((tok, nd))
        self._mark(tok, reads, writes)
        return tok

    def barrier(self):
        toks = [(k, v) for k, v in self.cnt.items() if v > 0]
        for eng in self.e:
            self._wait(eng, toks)


class Rot:
    def __init__(self, items):
        self.items = items
        self.i = 0

    def next(self):
        it = self.items[self.i]
        self.i = (self.i + 1) % len(self.items)
        return it


def vt_layout():
    cols = {}
    o = 0
    for name, n in [("norm_g", DEPTH * 8), ("q_norm", DEPTH * 3), ("kv_norm", DEPTH * 2),
                    ("conv_w", DEPTH * 4 * 12), ("conv_b", DEPTH * 12), ("ssm_norm", DEPTH * 8),
                    ("d_skip", DEPTH * 8)]:
        cols[name] = o
        o += n
    return cols, o


def build_program(depth=DEPTH, use_a=True, use_b=True, use_c=True, dbg=()):
    nc = bass.Bass("TRN2", target_bir_lowering=False)
    P = Prog(nc)
    dr = {}

    def din(name, shape, dt=F32):
        dr[name] = nc.dram_tensor(name, shape, dt, kind="ExternalInput").ap()

    def dout(name, shape, dt=F32):
        dr[name] = nc.dram_tensor(name, shape, dt, kind="ExternalOutput").ap()

    VTC, NV = vt_layout()
    din("x", [S, D])
    din("p", [DEPTH, S, 256])
    din("pos", [1, S], I32)
    din("w_in", [DEPTH, D, IN_TOTAL])
    din("w_uq", [DEPTH, 384, 768])
    din("w_ukv", [DEPTH, 256, 1024])
    din("w_br_a", [DEPTH, 512, D])
    din("w_br_b", [DEPTH, 1024, D])
    din("w_br_c", [DEPTH, 512, D])
    din("w_out", [DEPTH, D, D])
    din("w_ple", [DEPTH, 256, D])
    din("w_ple_gate", [DEPTH, D, D])
    din("vt", [128, NV])
    din("fnb", [128, D])
    din("ident", [128, 128])
    din("tri", [128, 128])
    din("ropec", [128, 4])
    din("negm", [128, 128])
    din("negs", [128, 128])
    din("sut", [128, 128])
    din("dt_bias", [DEPTH, 16])
    din("a_log", [DEPTH, 16])
    din("rel_bias", [1, 256])
    dout("y", [S, D])
    if "o_a" in dbg:
        dout("dbg_o_a", [S, 512])
    if "o_c" in dbg:
        dout("dbg_o_c", [S, 512])
    if "o_b" in dbg:
        dout("dbg_o_bT", [1024, S])
    if "sm" in dbg:
        dout("dbg_sm", [S, 32])
        dout("dbg_I", [128, 512])
        dout("dbg_msk", [128, 512])

    base0 = (nc.sbuf_base + 31) // 32 * 32
    slab = nc.alloc_sbuf_tensor("slab", [128, 212800 // 2], BF16)
    cur = [base0]

    def salloc(name, shape, dt, at=None):
        nbytes = int(np.prod(shape[1:])) * (4 if dt in (F32, I32) else 2)
        nbytes = (nbytes + 31) // 32 * 32
        if at is None:
            at = cur[0]
            cur[0] += nbytes
        return nc.alloc_sbuf_tensor_at(name, shape, dt, offset=at)

    xT = salloc("xT", [128, KC, S], F32)
    hT = salloc("hT", [128, KC, S], BF16)
    MT_OFF = cur[0]
    mT = salloc("mT", [128, KC, S], BF16)
    vt = salloc("vt_sb", [128, NV], F32)
    ident = salloc("ident_sb", [128, 128], F32)
    identb = salloc("identb_sb", [128, 128], BF16)
    trib = salloc("trib_sb", [128, 128], BF16)
    ropec = salloc("ropec_sb", [128, 4], F32)
    ones32 = salloc("ones32", [128, 128], F32)
    epsc = salloc("epsc", [128, 1], F32)
    xB = [Buf(f"x{t}") for t in range(NTB)]
    hB = [Buf(f"h{t}") for t in range(NTB)]
    mB = [Buf(f"m{t}") for t in range(NTB)]
    cB = Buf("consts")

    wbufs = []
    for i in range(3):
        t = salloc(f"wbuf{i}", [128, 2048], BF16)
        wbufs.append((t, [Buf(f"wbuf{i}a"), Buf(f"wbuf{i}b")]))
    wpool = Rot(wbufs)
    psl = []
    for i in range(8):
        t = nc.alloc_psum_tensor(f"ps{i}", [128, 512], F32)
        psl.append((t, Buf(f"ps{i}")))
    pspool = Rot(psl[:6])
    accpool = Rot(psl[6:])
    scr = []
    for i in range(3):
        t = salloc(f"scr{i}", [128, 512], F32)
        scr.append((t, Buf(f"scr{i}")))
    scrpool = Rot(scr)
    small = salloc("small", [128, 64], F32)
    smallB = Buf("small")
    EBr = salloc("EBr", [128, 2, 8, 128], BF16)
    negm = salloc("negm", [128, 128], F32)
    tri32 = salloc("tri32", [128, 128], F32)
    negs = salloc("negs", [128, 128], BF16)
    onec = salloc("onec", [128, 1], F32)
    onesb = salloc("onesb", [128, 128], BF16)
    sutb = salloc("sutb", [128, 128], BF16)
    ARENA = cur[0]
    ARENA_SZ = 212800 - (ARENA - base0) - 64
    assert ARENA_SZ >= 53000, ARENA_SZ
    tok = []
    for i in range(2):
        t = salloc(f"tok{i}", [128, 1024], F32, at=ARENA + i * 4096)
        tok.append((t, Buf(f"tok{i}")))
    tokpool = Rot(tok)
    fnb = salloc("fnb_sb", [128, D], F32, at=ARENA + 8192)

    def tbs(tb):
        return slice(tb * TBW, (tb + 1) * TBW)

    def load_w(src2d, K, N):
        buf, B = wpool.next()
        kc = K // 128
        dst = buf[:, :kc * N].rearrange("p (k n) -> p k n", k=kc)
        P.dma(dst, src2d.rearrange("(k p) n -> p k n", p=128), writes=[B], q="pool")
        return dst, B

    def mm_group(out, pairs, reads, writes):
        n = len(pairs)

        def fn(e):
            ins = None
            for i, (l, r) in enumerate(pairs):
                ins = e.matmul(out, l, r, start=(i == 0), stop=(i == n - 1))
            return ins
        return P.op("pe", fn, reads, writes)

    evac_flip = [0]

    def evac_engine():
        evac_flip[0] ^= 1
        return "act" if evac_flip[0] else "dve"

    def copy_op(eng, out, in_, reads, writes):
        if eng == "act":
            return P.op("act", lambda e: e.activation(out=out, in_=in_, func=AF.Copy), reads, writes)
        return P.op(eng, lambda e: e.tensor_copy(out=out, in_=in_), reads, writes)

    P.dma(vt[:], dr["vt"], writes=[cB])
    P.dma(ident[:], dr["ident"], writes=[cB])
    P.dma(identb[:], dr["ident"], writes=[cB], q="pool")
    P.dma(trib[:], dr["tri"], writes=[cB], q="pool")
    P.dma(ropec[:], dr["ropec"], writes=[cB])
    P.dma(tri32[:], dr["tri"], writes=[cB])
    P.dma(sutb[:], dr["sut"], writes=[cB], q="pool")
    P.dma(negs[:], dr["negs"], writes=[cB], q="pool")
    P.op("dve", lambda e: e.memset(onec[:], 1.0), writes=[cB])
    P.op("dve", lambda e: e.memset(onesb[:], 1.0), writes=[cB])
    P.op("dve", lambda e: e.memset(ones32[:], 1.0), writes=[cB])
    P.op("dve", lambda e: e.memset(epsc[:], EPS), writes=[cB])

    for blk in range(NBLK):
        xt, xtB = tokpool.next()
        P.dma(xt[:], dr["x"][blk * 128:(blk + 1) * 128, :], writes=[xtB])
        for half in range(2):
            ps, psB = pspool.next()

            def fn(e, xt=xt, ps=ps, half=half):
                ins = None
                for j in range(4):
                    c = half * 4 + j
                    ins = e.transpose(ps[:, j * 128:(j + 1) * 128], xt[:, c * 128:(c + 1) * 128], ident[:])
                return ins
            P.op("pe", fn, reads=[xtB, cB], writes=[psB])
            copy_op(evac_engine(), xT[:, half * 4:(half + 1) * 4, blk * 128:(blk + 1) * 128],
                    ps[:].rearrange("p (j t) -> p j t", j=4), reads=[psB], writes=[xB[blk // 4]])

    Ct = salloc("Ct", [128, S], BF16, at=ARENA + 40960)
    Sgt = salloc("Sgt", [128, S], BF16, at=ARENA + 45056)
    ropeB = Buf("ropetab")
    dr["ropeC"] = nc.dram_tensor("ropeC", [128, S], BF16, kind="Internal").ap()
    dr["ropeS"] = nc.dram_tensor("ropeS", [128, S], BF16, kind="Internal").ap()
    posi = salloc("posi", [128, S], I32, at=ARENA + 8192)
    angf = salloc("angf", [128, S], F32, at=ARENA + 16384)
    kff = salloc("kff", [128, S], F32, at=ARENA + 24576)
    kii = salloc("kii", [128, S], I32, at=ARENA + 8192)
    rB = Buf("ropetmp")
    P.dma(posi[:], dr["pos"].partition_broadcast(128), writes=[rB])
    P.op("dve", lambda e: e.tensor_copy(out=angf[:], in_=posi[:]), reads=[rB], writes=[rB])
    P.op("dve", lambda e: e.tensor_scalar(out=angf[:], in0=angf[:], scalar1=ropec[:, 0:1], scalar2=None, op0=ALU.mult),
         reads=[rB, cB], writes=[rB])
    TWO_PI = 2.0 * math.pi

    def reduced_sin(out_bf, shift, post_scale_col):
        P.op("dve", lambda e: e.tensor_scalar(out=kff[:], in0=angf[:], scalar1=shift, scalar2=1.0 / TWO_PI,
                                              op0=ALU.add, op1=ALU.mult), reads=[rB], writes=[rB])
        P.op("dve", lambda e: e.tensor_copy(out=kii[:], in_=kff[:]), reads=[rB], writes=[rB])
        P.op("dve", lambda e: e.tensor_copy(out=kff[:], in_=kii[:]), reads=[rB], writes=[rB])
        P.op("dve", lambda e: e.scalar_tensor_tensor(out=kff[:], in0=kff[:], scalar=-TWO_PI, in1=angf[:],
                                                     op0=ALU.mult, op1=ALU.add), reads=[rB], writes=[rB])
        P.op("dve", lambda e: e.tensor_scalar(out=kff[:], in0=kff[:], scalar1=shift, scalar2=None, op0=ALU.add),
             reads=[rB], writes=[rB])
        tmpf = angf2
        P.op("dve", lambda e: e.tensor_scalar(out=tmpf[:], in0=kff[:], scalar1=math.pi, scalar2=TWO_PI,
                                              op0=ALU.is_gt, op1=ALU.mult), reads=[rB], writes=[rB])
        P.op("dve", lambda e: e.tensor_tensor(out=kff[:], in0=kff[:], in1=tmpf[:], op=ALU.subtract), reads=[rB], writes=[rB])
        P.op("dve", lambda e: e.tensor_scalar(out=tmpf[:], in0=kff[:], scalar1=-math.pi, scalar2=TWO_PI,
                                              op0=ALU.is_lt, op1=ALU.mult), reads=[rB], writes=[rB])
        P.op("dve", lambda e: e.tensor_tensor(out=kff[:], in0=kff[:], in1=tmpf[:], op=ALU.add), reads=[rB], writes=[rB])
        P.op("act", lambda e: e.activation(out=kff[:], in_=kff[:], func=AF.Sin), reads=[rB], writes=[rB])
        if post_scale_col is None:
            P.op("dve", lambda e: e.tensor_copy(out=out_bf[:], in_=kff[:]), reads=[rB], writes=[cB])
        else:
            P.op("dve", lambda e: e.tensor_scalar(out=out_bf[:], in0=kff[:], scalar1=post_scale_col, scalar2=None,
                                                  op0=ALU.mult), reads=[rB, cB], writes=[cB])
    angf2 = salloc("angf2", [128, S], F32, at=ARENA + 32768)
    reduced_sin(Ct, math.pi / 2.0, None)
    reduced_sin(Sgt, 0.0, ropec[:, 1:2])
    sB1, sB2 = Buf("ropeCd"), Buf("ropeSd")
    P.dma(dr["ropeC"], Ct[:], reads=[cB], writes=[sB1])
    P.dma(dr["ropeS"], Sgt[:], reads=[cB], writes=[sB2])
    P.barrier()

    def rms_rstd(ps, n_feat, out_rstd, reads, writes):
        P.op("act", lambda e: e.activation(out=out_rstd, in_=ps, func=AF.Sqrt, scale=1.0 / n_feat, bias=epsc[:, 0:1]),
             reads=list(reads) + [cB], writes=writes)
        P.op("dve", lambda e: e.reciprocal(out=out_rstd, in_=out_rstd), reads=writes, writes=writes)


    SCALE_A = 96.0 ** -0.5

    def merge_branch(l, first, K, w_br, oT, oBufs, gate_col0, tb_list=None, local=False):
        kc = K // 128
        for oc in range(8):
            if kc == 8:
                wb, WB0 = load_w(w_br[:, oc * 128:(oc + 1) * 128], K, 128)
                wm, WB1 = load_w(dr["w_in"][l][:, gate_col0 + oc * 128:gate_col0 + (oc + 1) * 128], D, 128)
                WB = [WB0, WB1]
            else:
                buf, WB = wpool.next()
                wb = buf[:, 0:kc * 128].rearrange("p (k n) -> p k n", k=kc)
                wm = buf[:, kc * 128:(kc + 8) * 128].rearrange("p (k n) -> p k n", k=KC)
                P.dma(wb, w_br[:, oc * 128:(oc + 1) * 128].rearrange("(k p) n -> p k n", p=128), writes=[WB[0]], q="pool")
                P.dma(wm, dr["w_in"][l][:, gate_col0 + oc * 128:gate_col0 + (oc + 1) * 128].rearrange("(k p) n -> p k n", p=128),
                      writes=[WB[1]], q="pool")
            for tb in (tb_list if tb_list is not None else range(NTB)):
                osl = slice(0, TBW) if local else tbs(tb)
                psg, psgB = pspool.next()
                mm_group(psg[:], [(wm[:, k, :], hT[:, k, tbs(tb)]) for k in range(KC)], reads=[WB[1], hB[tb]], writes=[psgB])
                sg, sgB = scrpool.next()
                P.op("act", lambda e: e.activation(out=sg[:], in_=psg[:], func=AF.Sigmoid), reads=[psgB], writes=[sgB])
                psy, psyB = pspool.next()
                mm_group(psy[:], [(wb[:, k, :], oT[:, k, osl]) for k in range(kc)], reads=[WB[0]] + oBufs, writes=[psyB])
                if first:
                    P.op("dve", lambda e: e.tensor_tensor(out=mT[:, oc, tbs(tb)], in0=sg[:], in1=psy[:], op=ALU.mult),
                         reads=[sgB, psyB], writes=[mB[tb]])
                else:
                    P.op("dve", lambda e: e.tensor_tensor(out=sg[:], in0=sg[:], in1=psy[:], op=ALU.mult),
                         reads=[sgB, psyB], writes=[sgB])
                    P.op("dve", lambda e: e.tensor_tensor(out=mT[:, oc, tbs(tb)], in0=mT[:, oc, tbs(tb)], in1=sg[:], op=ALU.add),
                         reads=[sgB, mB[tb]], writes=[mB[tb]])

    def norm_proj(l, col0, nfeat, gname, outT, outB):
        nch = nfeat // 128
        gcol = VTC[gname] + l * nch
        ws = []
        for c0 in range(0, nch, 2):
            n = min(2, nch - c0) * 128
            ws.append(load_w(dr["w_in"][l][:, col0 + c0 * 128:col0 + c0 * 128 + n], D, n))
        for tb in range(NTB):
            pss = []
            pq, pqB = pspool.next()
            for c in range(nch):
                w, wB = ws[c // 2]
                ps, psB = pspool.next()
                mm_group(ps[:], [(w[:, k, (c % 2) * 128:(c % 2 + 1) * 128], hT[:, k, tbs(tb)]) for k in range(KC)],
                         reads=[wB, hB[tb]], writes=[psB])
                sq, sqB = scrpool.next()
                P.op("act", lambda e: e.activation(out=sq[:].bitcast(BF16)[:, 0:512], in_=ps[:], func=AF.Square), reads=[psB], writes=[sqB])
                P.op("pe", lambda e: e.matmul(pq[:], onesb[:], sq[:].bitcast(BF16)[:, 0:512], start=(c == 0), stop=(c == nch - 1)),
                     reads=[sqB, cB], acc=[pqB])
                pss.append((ps, psB))
            rs, rsB = scrpool.next()
            rms_rstd(pq[:], nfeat, rs[:], [pqB], [rsB])
            for c in range(nch):
                ps, psB = pss[c]
                P.op("dve", lambda e: e.scalar_tensor_tensor(
                    out=outT[:, c, tbs(tb)], in0=ps[:], scalar=vt[:, gcol + c:gcol + c + 1], in1=rs[:],
                    op0=ALU.mult, op1=ALU.mult), reads=[psB, rsB, cB], writes=[outB])

    def phase_A(l):
        o_tok = salloc("o_tok", [128, NBLK, 512], BF16, at=ARENA)
        oaT = salloc("oaT", [128, 4, S], BF16, at=ARENA + 16384)
        v_aug = salloc("v_aug", [128, NBLK, 66], BF16, at=ARENA + 32768)
        pTs = [(salloc(f"pT{i}", [128, 512], BF16, at=ARENA + 34944 + i * 1024), Buf(f"pT{i}")) for i in range(3)]
        ptpool = Rot(pTs)
        gsl = [(salloc(f"gsil{i}", [128, 512], BF16, at=ARENA + 38016 + i * 1024), Buf(f"gsil{i}")) for i in range(2)]
        gspool = Rot(gsl)
        recs = salloc("recs", [128, 8], F32, at=ARENA + 40064)
        ckvn = salloc("ckvn", [128, 2, S], BF16, at=MT_OFF)
        cqn = salloc("cqn", [128, 3, S], BF16, at=MT_OFF + 8192)
        krT = salloc("krT", [128, S], BF16, at=MT_OFF + 20480)
        kfull = salloc("kfull", [128, S], BF16, at=MT_OFF + 24576)
        qfull = salloc("qfull", [128, S], BF16, at=MT_OFF + 28672)
        otB, oaB, vB, ckvB, cqB, krB, kfB, qfB, recB = [Buf(n) for n in
                                                     ("o_tok", "oaT", "v_aug", "ckvn", "cqn", "krT", "kfull", "qfull", "recs")]
        P.barrier()
        P.dma(Ct[:], dr["ropeC"], writes=[ropeB])
        P.dma(Sgt[:], dr["ropeS"], writes=[ropeB])
        P.op("pool", lambda e: e.memset(v_aug[:, :, 64:66], 1.0), writes=[vB])
        norm_proj(l, OFF["c_kv"], 256, "kv_norm", ckvn, ckvB)
        norm_proj(l, OFF["c_q"], 384, "q_norm", cqn, cqB)
        wbuf, wB = wpool.next()
        wk = wbuf[:, :KC * 96].rearrange("p (k n) -> p k n", k=KC)
        wbuf2, wB2 = wpool.next()
        wk2 = wbuf2[:, :KC * 96].rearrange("p (k n) -> p k n", k=KC)
        src = dr["w_in"][l].rearrange("(k p) n -> p k n", p=128)
        c0 = OFF["k_rope"]
        P.dma(wk[:, :, 64:96], src[:, :, c0:c0 + 32], writes=[wB], q="pool")
        P.dma(wk2[:, :, 64:80], src[:, :, c0 + 16:c0 + 32], writes=[wB2], q="pool")
        P.dma(wk2[:, :, 80:96], src[:, :, c0:c0 + 16], writes=[wB2], q="pool")

        def rope_rows(psA, psAB, psSw, psSwB, out_rows, outB_, tb):
            t1, t1B = scrpool.next()
            t2, t2B = scrpool.next()
            P.op("dve", lambda e: e.tensor_tensor(out=t1[64:96, :], in0=psA[64:96, :], in1=Ct[64:96, tbs(tb)], op=ALU.mult),
                 reads=[psAB, ropeB], writes=[t1B])
            P.op("dve", lambda e: e.tensor_tensor(out=t2[64:96, :], in0=psSw[64:96, :], in1=Sgt[64:96, tbs(tb)], op=ALU.mult),
                 reads=[psSwB, ropeB], writes=[t2B])
            P.op("dve", lambda e: e.tensor_tensor(out=out_rows, in0=t1[64:96, :], in1=t2[64:96, :], op=ALU.add),
                 reads=[t1B, t2B], writes=[outB_])

        for tb in range(NTB):
            psA, psAB = pspool.next()
            mm_group(psA[0:96, :], [(wk[:, k, :], hT[:, k, tbs(tb)]) for k in range(KC)], reads=[wB, hB[tb]], writes=[psAB])
            psS, psSB = pspool.next()
            mm_group(psS[0:96, :], [(wk2[:, k, :], hT[:, k, tbs(tb)]) for k in range(KC)], reads=[wB2, hB[tb]], writes=[psSB])
            rope_rows(psA, psAB, psS, psSB, krT[64:96, tbs(tb)], krB, tb)

        uq = dr["w_uq"][l].rearrange("(k p) n -> p k n", p=128)
        ukv = dr["w_ukv"][l].rearrange("(k p) n -> p k n", p=128)
        def load_head(h):
            b1, b1B = wpool.next()
            wq = b1[:, 0:288].rearrange("p (k n) -> p k n", k=3)
            wqs = b1[:, 288:576].rearrange("p (k n) -> p k n", k=3)
            wkn = b1[:, 576:704].rearrange("p (k n) -> p k n", k=2)
            wv = b1[:, 704:832].rearrange("p (k n) -> p k n", k=2)
            hb = [Buf(f"hw{h}_{i}") for i in range(5)]
            P.dma(wq, uq[:, :, h * 96:(h + 1) * 96], reads=[], writes=[b1B, hb[0]], q="pool")
            P.dma(wqs[:, :, 64:80], uq[:, :, h * 96 + 80:h * 96 + 96], writes=[hb[1]], q="pool")
            P.dma(wqs[:, :, 80:96], uq[:, :, h * 96 + 64:h * 96 + 80], writes=[hb[2]], q="pool")
            P.dma(wkn, ukv[:, :, h * 128:h * 128 + 64], writes=[hb[3]], q="pool")
            P.dma(wv, ukv[:, :, h * 128 + 64:h * 128 + 128], writes=[hb[4]], q="pool")
            return wq, wqs, wkn, wv, [b1B, hb]
        nxt = load_head(0)
        for h in range(8):
            wq, wqs, wkn, wv, b1B = nxt
            if h < 7:
                nxt = load_head(h + 1)
            for tb in range(NTB):
                ps, psB = pspool.next()
                mm_group(ps[0:64, :], [(wkn[:, k, :], ckvn[:, k, tbs(tb)]) for k in range(2)], reads=[b1B, ckvB], writes=[psB])
                copy_op(evac_engine(), kfull[0:64, tbs(tb)], ps[0:64, :], reads=[psB], writes=[kfB])
                copy_op("dve", kfull[64:96, tbs(tb)], krT[64:96, tbs(tb)], reads=[krB], writes=[kfB])
                psA, psAB = pspool.next()
                mm_group(psA[0:96, :], [(wq[:, k, :], cqn[:, k, tbs(tb)]) for k in range(3)], reads=[b1B, cqB], writes=[psAB])
                psS, psSB = pspool.next()
                mm_group(psS[0:96, :], [(wqs[:, k, :], cqn[:, k, tbs(tb)]) for k in range(3)], reads=[b1B, cqB], writes=[psSB])
                copy_op(evac_engine(), qfull[0:64, tbs(tb)], psA[0:64, :], reads=[psAB], writes=[qfB])
                rope_rows(psA, psAB, psS, psSB, qfull[64:96, tbs(tb)], qfB, tb)
            for g4 in range(4):
                ps, psB = pspool.next()

                def fnv(e):
                    ins = None
                    for j in range(4):
                        blk = g4 * 4 + j
                        for k in range(2):
                            ins = e.matmul(ps[:, j * 64:(j + 1) * 64], ckvn[:, k, blk * 128:(blk + 1) * 128], wv[:, k, :],
                                           start=(k == 0), stop=(k == 1))
                    return ins
                P.op("pe", fnv, reads=[b1B, ckvB], writes=[psB])
                copy_op(evac_engine(), v_aug[:, g4 * 4:(g4 + 1) * 4, 0:64], ps[:, 0:256].rearrange("p (j d) -> p j d", j=4),
                        reads=[psB], writes=[vB])
            for Qb in range(4):
                po, poB = accpool.next()
                po3 = po[:].rearrange("p (j d) -> p j d", j=4)
                P.op("dve", lambda e: e.memset(po[:], 0.0), writes=[poB])
                nkb = 4 * Qb + 4
                def pv_a(st):
                    kb, j0, pT, pTB = st

                    def fpv(e):
                        ins = None
                        for j in range(j0, 4):
                            ins = e.matmul(po3[:, j, 0:65], pT[:, (j - j0) * 128:(j - j0 + 1) * 128], v_aug[:, kb, 0:65],
                                           start=False, stop=(kb == 4 * Qb + j), skip_group_check=True)
                        return ins
                    P.op("pe", fpv, reads=[pTB, vB], acc=[poB])
                pendq = []
                for kb in range(nkb):
                    j0 = max(0, kb - 4 * Qb)
                    n = 512 - j0 * 128
                    qlo = Qb * 512 + j0 * 128
                    ps, psB = pspool.next()
                    P.op("pe", lambda e: e.matmul(ps[:, 0:n], kfull[0:96, kb * 128:(kb + 1) * 128], qfull[0:96, qlo:qlo + n],
                                                  start=True, stop=True), reads=[kfB, qfB], writes=[psB])
                    pT, pTB = ptpool.next()
                    P.op("act", lambda e: e.activation(out=pT[:, 0:n], in_=ps[:, 0:n], func=AF.Exp, scale=SCALE_A),
                         reads=[psB], writes=[pTB])
                    if kb >= 4 * Qb:
                        P.op("pool", lambda e: e.tensor_tensor(out=pT[:, 0:128], in0=pT[:, 0:128], in1=trib[:], op=ALU.mult),
                             reads=[pTB, cB], writes=[pTB])
                    pendq.append((kb, j0, pT, pTB))
                    if len(pendq) > 2:
                        pv_a(pendq.pop(0))
                for st_ in pendq:
                    pv_a(st_)
                P.op("dve", lambda e: e.reciprocal(out=recs[:, 0:4], in_=po3[:, :, 64]), reads=[poB], writes=[recB])
                for j in range(4):
                    P.op("dve" if j % 2 else "act", (lambda e: e.tensor_scalar(
                        out=o_tok[:, Qb * 4 + j, h * 64:(h + 1) * 64], in0=po3[:, j, 0:64], scalar1=recs[:, j:j + 1],
                        scalar2=None, op0=ALU.mult)) if j % 2 else (lambda e: e.activation(
                            out=o_tok[:, Qb * 4 + j, h * 64:(h + 1) * 64], in_=po3[:, j, 0:64], func=AF.Copy,
                            scale=recs[:, j:j + 1])), reads=[poB, recB], writes=[otB])
        wg0 = load_w(dr["w_in"][l][:, OFF["gate_a"]:OFF["gate_a"] + 256], D, 256)
        wg1 = load_w(dr["w_in"][l][:, OFF["gate_a"] + 256:OFF["gate_a"] + 512], D, 256)
        if "o_a" in dbg and l == 0:
            pass
        for blk in range(NBLK):
            gs, gsB = gspool.next()
            for hf, (wg, wgB) in enumerate((wg0, wg1)):
                ps, psB = pspool.next()
                mm_group(ps[:, 0:256], [(hT[:, k, blk * 128:(blk + 1) * 128], wg[:, k, :]) for k in range(KC)],
                         reads=[wgB, hB[blk // 4]], writes=[psB])
                P.op("act", lambda e: e.activation(out=gs[:, hf * 256:(hf + 1) * 256], in_=ps[:, 0:256], func=AF.Silu),
                     reads=[psB], writes=[gsB])
            P.op("dve", lambda e: e.tensor_tensor(out=o_tok[:, blk, :], in0=o_tok[:, blk, :], in1=gs[:], op=ALU.mult),
                 reads=[gsB, otB], writes=[otB])
            ps, psB = pspool.next()
            psb = ps[:].bitcast(BF16)

            def ftr(e):
                ins = None
                for c in range(4):
                    ins = e.transpose(psb[:, c * 128:(c + 1) * 128], o_tok[:, blk, c * 128:(c + 1) * 128], identb[:])
                return ins
            P.op("pe", ftr, reads=[otB, cB], writes=[psB])
            copy_op(evac_engine(), oaT[:, :, blk * 128:(blk + 1) * 128], psb[:, 0:512].rearrange("p (c t) -> p c t", c=4),
                    reads=[psB], writes=[oaB])
        if "o_a" in dbg and l == 0:
            P.dma(dr["dbg_o_a"].rearrange("(b p) f -> p b f", p=128), o_tok[:], reads=[otB], q="pool")
        P.barrier()
        merge_branch(l, True, 512, dr["w_br_a"][l], oaT, [oaB], OFF["merge"])
        P.barrier()


    def phase_B(l, first):
        a = [ARENA]

        def al(name, shape, dt):
            nb = (int(np.prod(shape[1:])) * (4 if dt in (F32, I32) else 2) + 31) // 32 * 32
            t = salloc(name, shape, dt, at=a[0])
            a[0] += nb
            return t
        xbcs = al("xbcs", [128, 12, TBW], BF16)
        zs = al("zs", [128, 8, TBW], BF16)
        xpre = [(al(f"xpre{i}", [128, 516], F32), Buf(f"xpre{i}")) for i in range(2)]
        xppool = Rot(xpre)
        carry = al("carry", [128, 12, 4], F32)
        state = al("state", [128, 2, 512], F32)
        stbf = al("stbf", [128, 2, 512], BF16)
        sB_ = al("ssm_small", [128, 128], F32)
        sm4 = al("ssm_small4", [128, 8, 64], F32)
        s4B = Buf("ssm_small4")
        Rhl = al("Rhl", [128, 2, 4, 128], BF16)
        dthl = al("dthl", [128, 32], F32)
        dthb = al("dthb", [128, 16], BF16)
        hlB = Buf("dthl")
        decs = [(al(f"dec{i}", [128, 4, 128], BF16), Buf(f"dec{i}")) for i in range(2)]
        decpool = Rot(decs)
        cbT = al("cbT", [128, 2, 128], BF16)
        xdtz = al("xdtz", [128, 16, 128], BF16)
        xdte = al("xdte", [128, 16, 64], BF16)
        Btok = al("Btok", [128, 2, 128], BF16)
        wdt = al("wdt", [128, KC, 16], BF16)
        bcs = al("bcs", [128, 32], F32)
        dtAx = al("dtAx", [128, 8, 64], F32)
        dxB = Buf("dtAx")
        assert a[0] - ARENA <= ARENA_SZ, (a[0] - ARENA, ARENA_SZ)
        xbB, zsB, caB, stB, sbB, smB, RB, cbB, xzB, xeB, btB, wdB, bcB = [Buf(n) for n in
            ("xbcs", "zs", "carry", "state", "stbf", "ssm_small", "Rhl", "cbT", "xdtz", "xdte", "Btok", "wdt", "bcs")]
        DTX, DT, DTA, CC, DTE, CD, NDA, DTD = [i * 16 for i in range(8)]
        P.barrier()
        src = dr["w_in"][l].rearrange("(k p) n -> p k n", p=128)
        P.dma(wdt[:], src[:, :, OFF["dt"]:OFF["dt"] + 16], writes=[wdB], q="pool")
        P.dma(bcs[:, 0:16], dr["dt_bias"][l:l + 1, :].partition_broadcast(128), writes=[bcB])
        P.dma(bcs[:, 16:32], dr["a_log"][l:l + 1, :].partition_broadcast(128), writes=[bcB])
        P.op("act", lambda e: e.activation(out=bcs[:, 16:32], in_=bcs[:, 16:32], func=AF.Exp), reads=[bcB], writes=[bcB])
        P.op("dve", lambda e: e.tensor_scalar(out=bcs[:, 16:32], in0=bcs[:, 16:32], scalar1=-1.0, scalar2=None, op0=ALU.mult),
             reads=[bcB], writes=[bcB])
        P.op("pool", lambda e: e.memset(state[:], 0.0), writes=[stB])
        P.op("pool", lambda e: e.memset(stbf[:], 0.0), writes=[sbB])
        P.op("pool", lambda e: e.memset(xdtz[:], 0.0), writes=[xzB])
        P.op("pool", lambda e: e.memset(carry[:], 0.0), writes=[caB])
        P.op("pool", lambda e: e.memset(sm4[:], 0.0), writes=[s4B])
        cw = VTC["conv_w"] + l * 48
        cbcol = VTC["conv_b"] + l * 12
        dsk = VTC["d_skip"] + l * 8
        gsn = VTC["ssm_norm"] + l * 8
        for tb in range(NTB):
            ps, psB = pspool.next()
            for ch in range(4):
                gsl4 = slice((tb * 4 + ch) * 128, (tb * 4 + ch + 1) * 128)
                mm_group(ps[:, ch * 16:(ch + 1) * 16], [(hT[:, k, gsl4], wdt[:, k, :]) for k in range(KC)], reads=[wdB, hB[tb]], writes=[psB])
            Q = lambda i: sm4[:, i, :]
            P.op("dve", lambda e: e.tensor_tensor(out=Q(0).rearrange("p (c h) -> p c h", c=4), in0=ps[:, 0:64].rearrange("p (c h) -> p c h", c=4),
                                                  in1=bcs[:, 0:16].unsqueeze(1).broadcast_to([128, 4, 16]), op=ALU.add),
                 reads=[psB, bcB], writes=[s4B])
            P.op("act", lambda e: e.activation(out=Q(0), in_=Q(0), func=AF.Exp), reads=[s4B], writes=[s4B])
            P.op("act", lambda e: e.activation(out=Q(1), in_=Q(0), func=AF.Ln, bias=onec[:, 0:1]), reads=[s4B, cB], writes=[s4B])
            P.op("dve", lambda e: e.tensor_tensor(out=Q(2).rearrange("p (c h) -> p c h", c=4), in0=Q(1).rearrange("p (c h) -> p c h", c=4),
                                                  in1=bcs[:, 16:32].unsqueeze(1).broadcast_to([128, 4, 16]), op=ALU.mult),
                 reads=[s4B, bcB], writes=[s4B])
            psc, pscB = pspool.next()
            P.op("pe", lambda e: e.matmul(psc[:, 0:64], tri32[:], Q(2), start=True, stop=True), reads=[s4B, cB], writes=[pscB])
            pst, pstB = pspool.next()
            P.op("pe", lambda e: e.matmul(pst[:, 0:64], ones32[:], Q(2), start=True, stop=True), reads=[s4B, cB], writes=[pstB])
            copy_op("dve", Q(3), psc[:, 0:64], reads=[pscB], writes=[s4B])
            P.op("dve", lambda e: e.tensor_tensor(out=Q(4), in0=pst[:, 0:64], in1=Q(3), op=ALU.subtract), reads=[pstB, s4B], writes=[s4B])
            P.op("act", lambda e: e.activation(out=Q(4), in_=Q(4), func=AF.Exp), reads=[s4B], writes=[s4B])
            P.op("act", lambda e: e.activation(out=Q(5), in_=pst[:, 0:64], func=AF.Exp), reads=[pstB, s4B], writes=[s4B])
            P.op("dve", lambda e: e.tensor_tensor(out=Q(7), in0=Q(1), in1=Q(4), op=ALU.mult), reads=[s4B], writes=[s4B])
            for c2 in range(4):
                w_, wB_ = load_w(dr["w_in"][l][:, OFF["z"] + c2 * 256:OFF["z"] + (c2 + 1) * 256], D, 256)
                for sub in range(2):
                    c = c2 * 2 + sub
                    ps, psB = pspool.next()
                    mm_group(ps[:], [(w_[:, k, sub * 128:(sub + 1) * 128], hT[:, k, tbs(tb)]) for k in range(KC)],
                             reads=[wB_, hB[tb]], writes=[psB])
                    P.op("act", lambda e: e.activation(out=zs[:, c, :], in_=ps[:], func=AF.Silu), reads=[psB], writes=[zsB])
            def conv_s1(c, w_, wB_, sub):
                ps, psB = pspool.next()
                mm_group(ps[:], [(w_[:, k, sub * 128:(sub + 1) * 128], hT[:, k, tbs(tb)]) for k in range(KC)],
                         reads=[wB_, hB[tb]], writes=[psB])
                xp, xpB = xppool.next()
                P.op("act", lambda e: e.activation(out=xp[:, 3:515], in_=ps[:], func=AF.Copy), reads=[psB], writes=[xpB])
                P.op("act", lambda e: e.activation(out=xp[:, 0:3], in_=carry[:, c, 0:3], func=AF.Copy), reads=[caB], writes=[xpB])
                P.op("act", lambda e: e.activation(out=carry[:, c, 0:3], in_=xp[:, 512:515], func=AF.Copy), reads=[xpB], writes=[caB])
                return (c, xp, xpB)

            def conv_s2(st):
                c, xp, xpB = st
                acc, accB = scrpool.next()
                P.op("dve", lambda e: e.tensor_scalar(out=acc[:], in0=xp[:, 3:515], scalar1=vt[:, cw + 36 + c:cw + 36 + c + 1],
                                                      scalar2=vt[:, cbcol + c:cbcol + c + 1], op0=ALU.mult, op1=ALU.add),
                     reads=[xpB, cB], writes=[accB])
                for j in range(3):
                    P.op("dve", lambda e: e.scalar_tensor_tensor(out=acc[:], in0=xp[:, j:j + 512],
                                                                 scalar=vt[:, cw + j * 12 + c:cw + j * 12 + c + 1], in1=acc[:],
                                                                 op0=ALU.mult, op1=ALU.add), reads=[xpB, accB, cB], writes=[accB])
                P.op("act", lambda e: e.activation(out=xbcs[:, c, :], in_=acc[:], func=AF.Silu), reads=[accB], writes=[xbB])
            pend = None
            for c2 in range(6):
                w_, wB_ = load_w(dr["w_in"][l][:, OFF["xbc"] + c2 * 256:OFF["xbc"] + (c2 + 1) * 256], D, 256)
                for sub in range(2):
                    st = conv_s1(c2 * 2 + sub, w_, wB_, sub)
                    if pend is not None:
                        conv_s2(pend)
                    pend = st
            conv_s2(pend)
            for ch in range(4):
                cg = tb * 4 + ch
                lsl = slice(ch * 128, (ch + 1) * 128)
                gsl = slice(cg * 128, (cg + 1) * 128)
                sm = sB_
                P.op("dve", lambda e: e.tensor_copy(out=sm[:].rearrange("p (q h) -> p q h", q=8), in_=sm4[:, :, ch * 16:(ch + 1) * 16]),
                     reads=[s4B], writes=[smB])
                ps, psB = pspool.next()
                psb = ps[:].bitcast(BF16)

                def ftr(e):
                    ins = None
                    for c in range(8):
                        ins = e.transpose(psb[:, c * 128:(c + 1) * 128], xbcs[:, c, lsl], identb[:])
                    return ins
                P.op("pe", ftr, reads=[xbB, cB], writes=[psB])
                psb4 = psb.rearrange("p (j two d) -> p j two d", two=2, d=64)
                xz4 = xdtz[:].rearrange("p (j two) d -> p j two d", two=2)
                for par in range(2):
                    P.op("dve", lambda e: e.tensor_tensor(
                        out=xz4[:, :, par, par * 64:(par + 1) * 64], in0=psb4[:, :, par, :],
                        in1=sm[:, DT:DT + 16].rearrange("p (j two) -> p j two", two=2)[:, :, par].unsqueeze(2).broadcast_to([128, 8, 64]),
                        op=ALU.mult), reads=[psB, smB], writes=[xzB])
                P.op("dve", lambda e: e.tensor_tensor(out=xdte[:], in0=psb.rearrange("p (h d) -> p h d", d=64),
                                                      in1=sm[:, DTD:DTD + 16].unsqueeze(2).broadcast_to([128, 16, 64]), op=ALU.mult),
                     reads=[psB, smB], writes=[xeB])
                ps, psB = pspool.next()
                psb = ps[:].bitcast(BF16)

                def ftb(e):
                    ins = None
                    for g in range(2):
                        ins = e.transpose(psb[:, g * 128:(g + 1) * 128], xbcs[:, 8 + g, lsl], identb[:])
                    return ins
                P.op("pe", ftb, reads=[xbB, cB], writes=[psB])
                copy_op("act", Btok[:].rearrange("p g n -> p (g n)"), psb[:, 0:256], reads=[psB], writes=[btB])
                ps, psB = pspool.next()

                def fcb(e):
                    ins = None
                    for g in range(2):
                        ins = e.matmul(ps[:, g * 128:(g + 1) * 128], xbcs[:, 8 + g, lsl], xbcs[:, 10 + g, lsl], start=True, stop=True)
                    return ins
                P.op("pe", fcb, reads=[xbB], writes=[psB])
                copy_op("act", cbT[:].rearrange("p g n -> p (g n)"), ps[:, 0:256], reads=[psB], writes=[cbB])
                ETs = []
                for half in range(2):
                    ps, psB = pspool.next()
                    P.op("dve", lambda e: e.tensor_copy(out=dtAx[:], in_=sm[:, DTA + 8 * half:DTA + 8 * half + 8].unsqueeze(2).broadcast_to([128, 8, 64])),
                         reads=[smB], writes=[dxB])

                    def fet(e):
                        ins = None
                        for cc in range(4):
                            ins = e.matmul(ps[:, cc * 128:(cc + 1) * 128],
                                           dtAx[:, 2 * cc:2 * cc + 2, :].rearrange("p h d -> p (h d)"),
                                           tri32[:], start=True, stop=True)
                        return ins
                    P.op("pe", fet, reads=[dxB, cB], writes=[psB])
                    ET, ETB = scrpool.next()
                    P.op("act", lambda e: e.activation(out=ET[:], in_=ps[:], func=AF.Exp), reads=[psB], writes=[ETB])
                    pso, psoB = pspool.next()

                    def fyo(e):
                        ins = None
                        for cc in range(4):
                            c = half * 4 + cc
                            ins = e.matmul(pso[:, cc * 128:(cc + 1) * 128], stbf[:, c // 4, (c % 4) * 128:(c % 4 + 1) * 128],
                                           xbcs[:, 10 + c // 4, lsl], start=True, stop=True)
                        return ins
                    P.op("pe", fyo, reads=[sbB, xbB], writes=[psoB])
                    P.op("dve", lambda e: e.tensor_tensor(out=ET[:], in0=pso[:], in1=ET[:], op=ALU.mult), reads=[psoB, ETB], writes=[ETB])
                    ETs.append((ET, ETB))
                P.op("dve", lambda e: e.tensor_copy(out=dthb[:], in_=sm[:, DTA:DTA + 16]), reads=[smB], writes=[hlB])
                P.op("dve", lambda e: e.tensor_copy(out=dthl[:, 0:16], in_=dthb[:]), reads=[hlB], writes=[hlB])
                P.op("dve", lambda e: e.tensor_tensor(out=dthl[:, 16:32], in0=sm[:, DTA:DTA + 16], in1=dthl[:, 0:16], op=ALU.subtract),
                     reads=[smB, hlB], writes=[hlB])

                def dec_s1(q4):
                    for hl in range(2):
                        P.op("dve", lambda e: e.tensor_tensor(
                            out=Rhl[:, hl, :, :], in0=dthl[:, hl * 16 + 4 * q4:hl * 16 + 4 * q4 + 4].unsqueeze(2).broadcast_to([128, 4, 128]),
                            in1=tri32[:].unsqueeze(1).broadcast_to([128, 4, 128]), op=ALU.mult), reads=[hlB, cB], writes=[RB])
                    psx, psxB = pspool.next()

                    def fdec(e):
                        e.matmul(psx[:], sutb[:], Rhl[:, 0, :, :].rearrange("p j l -> p (j l)"), start=True, stop=False)
                        e.matmul(psx[:], sutb[:], Rhl[:, 1, :, :].rearrange("p j l -> p (j l)"), start=False, stop=False)
                        return e.matmul(psx[:], identb[:], negs[:].unsqueeze(1).broadcast_to([128, 4, 128]), start=False, stop=True)
                    P.op("pe", fdec, reads=[RB, cB], writes=[psxB])
                    dec, decB = decpool.next()
                    P.op("act", lambda e: e.activation(out=dec[:].rearrange("p j l -> p (j l)"), in_=psx[:], func=AF.Exp),
                         reads=[psxB], writes=[decB])
                    return (q4, dec, decB)

                def dec_s2(st, psd, psdB):
                    q4, dec, decB = st
                    g = q4 // 2
                    P.op("dve", lambda e: e.tensor_tensor(out=dec[:], in0=dec[:], in1=cbT[:, g:g + 1, :].broadcast_to([128, 4, 128]),
                                                          op=ALU.mult), reads=[decB, cbB], writes=[decB])

                    def fyd(e):
                        ins = None
                        for cc2 in range(2):
                            c = q4 * 2 + cc2
                            col = (c % 4) * 128
                            e.matmul(psd[:, col:col + 128], xdtz[:, 2 * c, :], dec[:, (2 * c) % 4, :], start=True, stop=False)
                            ins = e.matmul(psd[:, col:col + 128], xdtz[:, 2 * c + 1, :], dec[:, (2 * c + 1) % 4, :], start=False, stop=True)
                        return ins
                    P.op("pe", fyd, reads=[xzB, decB], acc=[psdB])

                def y_tail(half, psd, psdB):
                    ET, ETB = ETs[half]
                    P.op("dve", lambda e: e.tensor_tensor(out=ET[:], in0=psd[:], in1=ET[:], op=ALU.add), reads=[psdB, ETB], writes=[ETB])
                    t2, t2B = scrpool.next()
                    t23 = t2[:].rearrange("p (c t) -> p c t", c=4)
                    P.op("pool", lambda e: e.tensor_tensor(
                        out=t23, in0=xbcs[:, half * 4:(half + 1) * 4, lsl],
                        in1=vt[:, dsk + half * 4:dsk + half * 4 + 4].unsqueeze(2).broadcast_to([128, 4, 128]), op=ALU.mult),
                        reads=[xbB, cB], writes=[t2B])
                    P.op("pool", lambda e: e.tensor_tensor(out=t2[:], in0=t2[:], in1=ET[:], op=ALU.add), reads=[t2B, ETB], writes=[t2B])
                    P.op("pool", lambda e: e.tensor_tensor(out=zs[:, half * 4:(half + 1) * 4, lsl], in0=zs[:, half * 4:(half + 1) * 4, lsl],
                                                           in1=t23, op=ALU.mult), reads=[t2B, zsB], writes=[zsB])
                psds = [pspool.next(), pspool.next()]
                pend = dec_s1(0)
                for q4 in range(4):
                    nx = dec_s1(q4 + 1) if q4 < 3 else None
                    dec_s2(pend, *psds[q4 // 2])
                    if q4 % 2 == 1:
                        y_tail(q4 // 2, *psds[q4 // 2])
                    pend = nx
                for g in range(2):
                    pss, pssB = pspool.next()
                    P.op("pe", lambda e: e.matmul(pss[:], Btok[:, g, :], xdte[:, 8 * g:8 * g + 8, :].rearrange("p h d -> p (h d)"),
                                                  start=True, stop=True), reads=[btB, xeB], writes=[pssB])
                    st3 = state[:, g, :].rearrange("p (h d) -> p h d", d=64)
                    P.op("dve", lambda e: e.tensor_tensor(out=st3, in0=st3,
                                                          in1=sm[:, CD + 8 * g:CD + 8 * g + 8].unsqueeze(2).broadcast_to([128, 8, 64]),
                                                          op=ALU.mult), reads=[stB, smB], writes=[stB])
                    P.op("dve", lambda e: e.tensor_tensor(out=state[:, g, :], in0=state[:, g, :], in1=pss[:], op=ALU.add),
                         reads=[stB, pssB], writes=[stB])
                    copy_op("act", stbf[:, g, :], state[:, g, :], reads=[stB], writes=[sbB])
            pq, pqB = pspool.next()
            for c in range(8):
                sq, sqB = scrpool.next()
                P.op("act", lambda e: e.activation(out=sq[:].bitcast(BF16)[:, 0:512], in_=zs[:, c, :], func=AF.Square), reads=[zsB], writes=[sqB])
                P.op("pe", lambda e: e.matmul(pq[:], onesb[:], sq[:].bitcast(BF16)[:, 0:512], start=(c == 0), stop=(c == 7)), reads=[sqB, cB], acc=[pqB])
            rs, rsB = scrpool.next()
            rms_rstd(pq[:], 1024, rs[:], [pqB], [rsB])
            for c in range(8):
                P.op("dve", lambda e: e.scalar_tensor_tensor(out=zs[:, c, :], in0=zs[:, c, :], scalar=vt[:, gsn + c:gsn + c + 1], in1=rs[:],
                                                             op0=ALU.mult, op1=ALU.mult), reads=[zsB, rsB, cB], writes=[zsB])
            if "o_b" in dbg and l == 0:
                for c in range(8):
                    P.dma(dr["dbg_o_bT"][c * 128:(c + 1) * 128, tbs(tb)], zs[:, c, :], reads=[zsB], q="pool")
            merge_branch(l, first, 1024, dr["w_br_b"][l], zs, [zsB], OFF["merge"] + D, tb_list=[tb], local=True)
        P.barrier()

    SCALE_C = 64.0 ** -0.5
    import os as _os
    NBIS = int(_os.environ.get("DSA_NBIS", "12"))
    NEG = -1.0e30

    def t5_thresholds():
        n = np.arange(0, 400)
        nf = np.maximum(n, 16).astype(np.float32)
        lr = (np.log(nf / np.float32(16.0)) / np.float32(math.log(128 / 16))).astype(np.float32)
        large = np.minimum(16 + (lr * np.float32(16.0)).astype(np.int32), 31)
        bucket = np.where(n < 16, n, large)
        return [int(np.argmax(bucket >= j)) for j in range(1, 32)]

    ebB = Buf("EBr")

    def setup_bias():
        base = ARENA + 16384
        posr_i = salloc("posr_i", [128, 256], I32, at=base)
        posc_i = salloc("posc_i", [128, 1], I32, at=base + 1024)
        posr = salloc("posr", [128, 256], F32, at=base + 2048)
        posc = salloc("posc", [128, 1], F32, at=base + 3072)
        Dd = salloc("Dd", [128, 2, 128], F32, at=base + 4096)
        ind = salloc("ind", [128, 2, 128], F32, at=base + 5120)
        accb = salloc("accb", [128, 2, 8, 128], F32, at=base + 6144)
        rbb = salloc("rbb", [128, 256], F32, at=base + 14336)
        dl = salloc("dl", [128, 248], F32, at=base + 15360)
        bs = salloc("bs", [128, 8], F32, at=base + 16384)
        tB = Buf("biastmp")
        P.dma(posr_i[:], dr["pos"][:, 0:256].partition_broadcast(128), writes=[tB])
        P.dma(posc_i[:], dr["pos"][0, 0:128].rearrange("(p o) -> p o", o=1), writes=[tB])
        P.dma(rbb[:], dr["rel_bias"].partition_broadcast(128), writes=[tB])
        P.dma(negm[:], dr["negm"], writes=[cB])
        P.op("dve", lambda e: e.tensor_copy(out=posr[:], in_=posr_i[:]), reads=[tB], writes=[tB])
        P.op("dve", lambda e: e.tensor_copy(out=posc[:], in_=posc_i[:]), reads=[tB], writes=[tB])
        P.op("dve", lambda e: e.tensor_scalar(out=Dd[:].rearrange("p a t -> p (a t)"), in0=posr[:], scalar1=posc[:, 0:1],
                                              scalar2=None, op0=ALU.subtract), reads=[tB], writes=[tB])
        P.op("dve", lambda e: e.tensor_tensor(out=dl[:], in0=rbb[:, 8:256], in1=rbb[:, 0:248], op=ALU.subtract), reads=[tB], writes=[tB])
        P.op("dve", lambda e: e.tensor_tensor(out=bs[:], in0=rbb[:, 0:8], in1=rbb[:, 248:256], op=ALU.subtract), reads=[tB], writes=[tB])
        P.op("dve", lambda e: e.memset(accb[:], 0.0), writes=[tB])
        for j, T in enumerate(t5_thresholds()):
            P.op("dve", lambda e: e.tensor_scalar(out=ind[:], in0=Dd[:], scalar1=float(T) - 0.5, scalar2=None, op0=ALU.is_ge),
                 reads=[tB], writes=[tB])
            for h in range(8):
                P.op("dve", lambda e: e.scalar_tensor_tensor(out=accb[:, :, h, :], in0=ind[:], scalar=dl[:, j * 8 + h:j * 8 + h + 1],
                                                             in1=accb[:, :, h, :], op0=ALU.mult, op1=ALU.add), reads=[tB], writes=[tB])
        for h in range(8):
            P.op("act", lambda e: e.activation(out=EBr[:, :, (h // 2) + 4 * (h % 2), :], in_=accb[:, :, h, :], func=AF.Exp, bias=bs[:, h:h + 1]),
                 reads=[tB], writes=[ebB])
        P.barrier()

    class _Stop(Exception):
        pass
    DSA_STOP = int(_os.environ.get("DSA_STOP", "0"))

    def stop_at(n):
        if DSA_STOP == n:
            raise _Stop()

    def phase_C(l, first):
        try:
            phase_C_inner(l, first)
        except _Stop:
            pass
        P.barrier()

    def phase_C_inner(l, first):
        a = [ARENA]

        def al(name, shape, dt):
            nb = (int(np.prod(shape[1:])) * (4 if dt in (F32, I32) else 2) + 31) // 32 * 32
            t = salloc(name, shape, dt, at=a[0])
            a[0] += nb
            return t
        kc2 = al("kc2", [128, S], BF16)
        kidx2 = al("kidx2", [128, S], BF16)
        vc = al("vc_aug", [128, NBLK, 66], BF16)
        qcT = al("qcT", [128, 4, TBW], BF16)
        qiT = al("qiT", [128, 4, TBW], BF16)
        Isc = al("Isc", [128, S], F32)
        msk = al("msk", [128, S], BF16)
        mskT2 = [al(f"mskT{i}", [128, NBLK, 128], BF16) for i in range(2)]
        mtB2 = [Buf("mskT0"), Buf("mskT1")]
        recC = al("recC", [128, 8], F32)
        rcB = Buf("recC")
        PTs = [(al(f"PT{i}", [128, 8, 128], BF16), Buf(f"PT{i}")) for i in range(3)]
        PTpool = Rot(PTs)
        ocT = al("ocT", [128, 4, TBW], BF16)
        oblk = al("oblk", [128, 512], BF16)
        gsc = al("gsc", [128, 512], BF16)
        relupool = scrpool
        sm = al("smC", [128, 32], F32)
        widx = al("widx", [128, NBLK, 8], F32)
        assert a[0] - ARENA <= ARENA_SZ, (a[0] - ARENA, ARENA_SZ)
        kcB, kiB, vcB, qcB, qiB, IB, mkB, ocB, obB, gsB, smB, wxB = [Buf(n) for n in
            ("kc2", "kidx2", "vc", "qcT", "qiT", "Isc", "msk", "ocT", "oblk", "gsc", "smC", "widx")]
        WI, LO, W0, MID, CNT, G, REC = 0, 8, 9, 10, 11, 12, 16
        P.barrier()
        src = dr["w_in"][l].rearrange("(k p) n -> p k n", p=128)
        b1, b1B = wpool.next()
        wkc = b1[:, 0:1024].rearrange("p (k n) -> p k n", k=KC)
        wki = b1[:, 1024:2048].rearrange("p (k n) -> p k n", k=KC)
        for hh in range(2):
            P.dma(wkc[:, :, hh * 64:(hh + 1) * 64], src[:, :, OFF["k_c"]:OFF["k_c"] + 64], writes=[b1B], q="pool")
            P.dma(wki[:, :, hh * 64:(hh + 1) * 64], src[:, :, OFF["k_idx"]:OFF["k_idx"] + 64], writes=[b1B], q="pool")
        b2, b2B = wpool.next()
        wvc = b2[:, 0:512].rearrange("p (k n) -> p k n", k=KC)
        wwi = b2[:, 512:576].rearrange("p (k n) -> p k n", k=KC)
        P.dma(wvc, src[:, :, OFF["v_c"]:OFF["v_c"] + 64], writes=[b2B], q="pool")
        P.dma(wwi, src[:, :, OFF["w_idx"]:OFF["w_idx"] + 8], writes=[b2B], q="pool")
        P.op("pool", lambda e: e.memset(vc[:, :, 64:66], 1.0), writes=[vcB])
        P.op("pool", lambda e: e.memset(sm[:], 0.0), writes=[smB])
        for tb in range(NTB):
            for (w_, dst, dB) in ((wkc, kc2, kcB), (wki, kidx2, kiB)):
                ps, psB = pspool.next()
                mm_group(ps[:], [(w_[:, k, :], hT[:, k, tbs(tb)]) for k in range(KC)], reads=[b1B, hB[tb]], writes=[psB])
                copy_op(evac_engine(), dst[:, tbs(tb)], ps[:], reads=[psB], writes=[dB])
            ps, psB = pspool.next()

            def fnv(e):
                ins = None
                for j in range(4):
                    blk = tb * 4 + j
                    for k in range(KC):
                        ins = e.matmul(ps[:, j * 64:(j + 1) * 64], hT[:, k, blk * 128:(blk + 1) * 128], wvc[:, k, :],
                                       start=(k == 0), stop=(k == KC - 1))
                return ins
            P.op("pe", fnv, reads=[b2B, hB[tb]], writes=[psB])
            copy_op(evac_engine(), vc[:, tb * 4:(tb + 1) * 4, 0:64], ps[:, 0:256].rearrange("p (j d) -> p j d", j=4),
                    reads=[psB], writes=[vcB])
            ps, psB = pspool.next()

            def fnw(e):
                ins = None
                for j in range(4):
                    blk = tb * 4 + j
                    for k in range(KC):
                        ins = e.matmul(ps[:, j * 8:(j + 1) * 8], hT[:, k, blk * 128:(blk + 1) * 128], wwi[:, k, :],
                                       start=(k == 0), stop=(k == KC - 1))
                return ins
            P.op("pe", fnw, reads=[b2B, hB[tb]], writes=[psB])
            copy_op("dve", widx[:, tb * 4:(tb + 1) * 4, :], ps[:, 0:32].rearrange("p (j d) -> p j d", j=4), reads=[psB], writes=[wxB])
        stop_at(1)
        for tb in range(NTB):
            for (name, dstT, dB) in (("q_c", qcT, qcB), ("q_idx", qiT, qiB)):
                for c2 in range(2):
                    w_, wB_ = load_w(dr["w_in"][l][:, OFF[name] + c2 * 256:OFF[name] + (c2 + 1) * 256], D, 256)
                    for sub in range(2):
                        ps, psB = pspool.next()
                        mm_group(ps[:], [(w_[:, k, sub * 128:(sub + 1) * 128], hT[:, k, tbs(tb)]) for k in range(KC)],
                                 reads=[wB_, hB[tb]], writes=[psB])
                        copy_op(evac_engine(), dstT[:, c2 * 2 + sub, :], ps[:], reads=[psB], writes=[dB])
            wg0 = load_w(dr["w_in"][l][:, OFF["gate_c"]:OFF["gate_c"] + 256], D, 256)
            wg1 = load_w(dr["w_in"][l][:, OFF["gate_c"] + 256:OFF["gate_c"] + 512], D, 256)
            def qblock(qi):
                mskT = mskT2[qi % 2]
                mtB = mtB2[qi % 2]
                qb = tb * 4 + qi
                nk = (qb + 1) * 128
                tl = slice(qi * 128, (qi + 1) * 128)
                for k0 in range(0, nk, 512):
                    n = min(512, nk - k0)
                    for h in range(8):
                        hp = slice((h % 2) * 64, (h % 2) * 64 + 64)
                        ps, psB = pspool.next()
                        P.op("pe", lambda e: e.matmul(ps[:, 0:n], qiT[hp, h // 2, tl], kidx2[hp, k0:k0 + n], start=True, stop=True),
                             reads=[qiB, kiB], writes=[psB])
                        if h == 0:
                            P.op("dve", lambda e: e.tensor_scalar(out=Isc[:, k0:k0 + n], in0=ps[:, 0:n], scalar1=0.0,
                                                                  scalar2=widx[:, qb, 0:1], op0=ALU.max, op1=ALU.mult),
                                 reads=[psB, wxB], writes=[IB])
                        else:
                            rl, rlB = relupool.next()
                            P.op("act", lambda e: e.activation(out=rl[:, 0:n], in_=ps[:, 0:n], func=AF.Relu), reads=[psB], writes=[rlB])
                            P.op("dve", lambda e: e.scalar_tensor_tensor(out=Isc[:, k0:k0 + n], in0=rl[:, 0:n],
                                                                         scalar=widx[:, qb, h:h + 1], in1=Isc[:, k0:k0 + n],
                                                                         op0=ALU.mult, op1=ALU.add), reads=[rlB, wxB, IB], writes=[IB])
                yield
                if qb >= 2:
                    P.op("dve", lambda e: e.tensor_reduce(out=sm[:, LO:LO + 1], in_=Isc[:, 0:nk], axis=AX.X, op=ALU.min),
                         reads=[IB], writes=[smB])
                    P.op("dve", lambda e: e.tensor_reduce(out=sm[:, W0:W0 + 1], in_=Isc[:, 0:nk - 128], axis=AX.X, op=ALU.max),
                         reads=[IB], writes=[smB])
                P.op("dve", lambda e: e.tensor_tensor(out=Isc[:, nk - 128:nk], in0=Isc[:, nk - 128:nk], in1=negm[:], op=ALU.add),
                     reads=[IB, cB], writes=[IB])
                if qb >= 2:
                    rl, rlB = relupool.next()
                    P.op("dve", lambda e: e.tensor_reduce(out=rl[:, 0:1], in_=Isc[:, nk - 128:nk], axis=AX.X, op=ALU.max),
                         reads=[IB], writes=[rlB])
                    P.op("dve", lambda e: e.tensor_tensor(out=sm[:, W0:W0 + 1], in0=sm[:, W0:W0 + 1], in1=rl[:, 0:1], op=ALU.max),
                         reads=[rlB, smB], writes=[smB])
                    P.op("dve", lambda e: e.tensor_tensor(out=sm[:, W0:W0 + 1], in0=sm[:, W0:W0 + 1], in1=sm[:, LO:LO + 1], op=ALU.subtract),
                         reads=[smB], writes=[smB])
                    midB, midmB, cntB, gB = Buf("mid"), Buf("midm"), Buf("cnt"), Buf("g")
                    MIDM = 13
                    P.op("dve", lambda e: e.scalar_tensor_tensor(out=sm[:, MID:MID + 1], in0=sm[:, W0:W0 + 1], scalar=0.5,
                                                                 in1=sm[:, LO:LO + 1], op0=ALU.mult, op1=ALU.add),
                         reads=[smB], writes=[midB])
                    for it in range(NBIS):
                        q = 0.5 ** (it + 2)
                        P.op("dve", lambda e: e.scalar_tensor_tensor(out=sm[:, MIDM:MIDM + 1], in0=sm[:, W0:W0 + 1], scalar=-q,
                                                                     in1=sm[:, MID:MID + 1], op0=ALU.mult, op1=ALU.add),
                             reads=[smB, midB], writes=[midmB])
                        P.op("dve", lambda e: e.tensor_scalar(out=msk[:, 0:nk], in0=Isc[:, 0:nk], scalar1=sm[:, MID:MID + 1], scalar2=0.0,
                                                              op0=ALU.is_ge, op1=ALU.add, accum_out=sm[:, CNT:CNT + 1]),
                             reads=[IB, midB], writes=[mkB, cntB])
                        P.op("dve", lambda e: e.tensor_scalar(out=sm[:, G:G + 1], in0=sm[:, CNT:CNT + 1], scalar1=255.5, scalar2=2.0 * q,
                                                              op0=ALU.is_ge, op1=ALU.mult), reads=[cntB], writes=[gB])
                        P.op("dve", lambda e: e.scalar_tensor_tensor(out=sm[:, MID:MID + 1], in0=sm[:, G:G + 1], scalar=sm[:, W0:W0 + 1],
                                                                     in1=sm[:, MIDM:MIDM + 1], op0=ALU.mult, op1=ALU.add),
                             reads=[smB, gB, midmB], writes=[midB])
                    P.op("dve", lambda e: e.scalar_tensor_tensor(out=sm[:, LO:LO + 1], in0=sm[:, W0:W0 + 1], scalar=-(0.5 ** (NBIS + 1)),
                                                                 in1=sm[:, MID:MID + 1], op0=ALU.mult, op1=ALU.add),
                         reads=[smB, midB], writes=[smB])
                    P.op("dve", lambda e: e.tensor_scalar(out=msk[:, 0:nk], in0=Isc[:, 0:nk], scalar1=sm[:, LO:LO + 1], scalar2=None,
                                                          op0=ALU.is_ge), reads=[IB, smB], writes=[mkB])
                else:
                    P.op("dve", lambda e: e.tensor_scalar(out=msk[:, 0:nk], in0=Isc[:, 0:nk], scalar1=-1.0e29, scalar2=None,
                                                          op0=ALU.is_ge), reads=[IB], writes=[mkB])
                if "sm" in dbg and l == 0:
                    P.dma(dr["dbg_sm"][qb * 128:(qb + 1) * 128, :], sm[:], reads=[smB])
                    if qb == 2:
                        P.dma(dr["dbg_I"][:, 0:nk], Isc[:, 0:nk], reads=[IB])
                        P.dma(dr["dbg_msk"][:, 0:nk], msk[:, 0:nk], reads=[mkB], q="pool")
                for g0 in range(0, qb + 1, 8):
                    gn = min(8, qb + 1 - g0)
                    ps, psB = pspool.next()
                    psb = ps[:].bitcast(BF16)

                    def ftr(e):
                        ins = None
                        for i in range(gn):
                            ins = e.transpose(psb[:, i * 128:(i + 1) * 128], msk[:, (g0 + i) * 128:(g0 + i + 1) * 128], identb[:])
                        return ins
                    P.op("pe", ftr, reads=[mkB, cB], writes=[psB])
                    P.op("dve", lambda e: e.tensor_scalar(out=mskT[:, g0:g0 + gn, :], in0=psb[:, 0:gn * 128].rearrange("p (g t) -> p g t", g=gn),
                                                          scalar1=-1.0, scalar2=30000.0, op0=ALU.add, op1=ALU.mult),
                         reads=[psB], writes=[mtB])
                yield
                po0, po0B = accpool.next()
                po1, po1B = accpool.next()
                P.op("dve", lambda e: e.memset(po0[:], 0.0), writes=[po0B])
                P.op("dve", lambda e: e.memset(po1[:], 0.0), writes=[po1B])
                pos_ = (po0[:].rearrange("p (j d) -> p j d", j=4), po1[:].rearrange("p (j d) -> p j d", j=4))
                def pv_c(st):
                    kb, PT, PTB = st
                    for half in range(2):
                        def fpv(e):
                            ins = None
                            for hh in range(4):
                                ins = e.matmul(pos_[half][:, hh, 0:65], PT[:, half * 4 + hh, :], vc[:, kb, 0:65],
                                               start=False, stop=(kb == qb), skip_group_check=True)
                            return ins
                        P.op("pe", fpv, reads=[PTB, vcB], acc=[(po0B, po1B)[half]])
                pendq = []
                for kb in range(qb + 1):
                    PT, PTB = PTpool.next()
                    for half in range(2):
                        ps, psB = pspool.next()

                        def fqk(e):
                            ins = e.matmul(ps[:], identb[:], mskT[:, kb:kb + 1, :].broadcast_to([128, 4, 128]), start=True, stop=False)
                            for hh in range(4):
                                h = hh * 2 + half
                                hp = slice((h % 2) * 64, (h % 2) * 64 + 64)
                                ins = e.matmul(ps[:, hh * 128:(hh + 1) * 128], kc2[hp, kb * 128:(kb + 1) * 128], qcT[hp, h // 2, tl],
                                               start=False, stop=(hh == 3))
                            return ins
                        P.op("pe", fqk, reads=[kcB, qcB, mtB, cB], writes=[psB])
                        P.op("act", lambda e: e.activation(out=PT[:, half * 4:(half + 1) * 4, :].rearrange("p h t -> p (h t)"), in_=ps[:],
                                                           func=AF.Exp, scale=SCALE_C), reads=[psB], writes=[PTB])
                    if qb - kb <= 1:
                        P.op("pool", lambda e: e.tensor_tensor(out=PT[:], in0=PT[:], in1=EBr[:, qb - kb, :, :], op=ALU.mult),
                             reads=[PTB, ebB], writes=[PTB])
                    pendq.append((kb, PT, PTB))
                    if len(pendq) > 2:
                        pv_c(pendq.pop(0))
                for st_ in pendq:
                    pv_c(st_)
                yield
                for hf, (wg, wgB) in enumerate((wg0, wg1)):
                    ps, psB = pspool.next()
                    mm_group(ps[:, 0:256], [(hT[:, k, qb * 128:(qb + 1) * 128], wg[:, k, :]) for k in range(KC)],
                             reads=[wgB, hB[tb]], writes=[psB])
                    P.op("act", lambda e: e.activation(out=gsc[:, hf * 256:(hf + 1) * 256], in_=ps[:, 0:256], func=AF.Silu),
                         reads=[psB], writes=[gsB])
                for half in range(2):
                    P.op("dve", lambda e: e.reciprocal(out=recC[:, half * 4:half * 4 + 4], in_=pos_[half][:, :, 64]),
                         reads=[(po0B, po1B)[half]], writes=[rcB])
                    P.op("dve", lambda e: e.tensor_tensor(
                        out=oblk[:].rearrange("p (j two d) -> p j two d", two=2, d=64)[:, :, half, :], in0=pos_[half][:, :, 0:64],
                        in1=recC[:, half * 4:half * 4 + 4].unsqueeze(2).broadcast_to([128, 4, 64]), op=ALU.mult),
                        reads=[(po0B, po1B)[half], rcB], writes=[obB])
                if "o_c" in dbg and l == 0:
                    P.op("pool", lambda e: e.tensor_tensor(out=gsc[:], in0=oblk[:], in1=gsc[:], op=ALU.mult), reads=[gsB, obB], writes=[gsB])
                    P.dma(dr["dbg_o_c"][qb * 128:(qb + 1) * 128, :], gsc[:], reads=[gsB], q="pool")
                    P.op("dve", lambda e: e.tensor_copy(out=oblk[:], in_=gsc[:]), reads=[gsB, obB], writes=[obB])
                else:
                    P.op("dve", lambda e: e.tensor_tensor(out=oblk[:], in0=oblk[:], in1=gsc[:], op=ALU.mult), reads=[gsB, obB], writes=[obB])
                ps, psB = pspool.next()
                psb = ps[:].bitcast(BF16)

                def ftr2(e):
                    ins = None
                    for c in range(4):
                        ins = e.transpose(psb[:, c * 128:(c + 1) * 128], oblk[:, c * 128:(c + 1) * 128], identb[:])
                    return ins
                P.op("pe", ftr2, reads=[obB, cB], writes=[psB])
                copy_op(evac_engine(), ocT[:, :, tl], psb[:, 0:512].rearrange("p (c t) -> p c t", c=4), reads=[psB], writes=[ocB])
                yield
            gens = [qblock(qi) for qi in range(4)]
            next(gens[0]); next(gens[0])
            for qi in range(1, 4):
                next(gens[qi])
                next(gens[qi - 1])
                next(gens[qi])
                next(gens[qi - 1])
            next(gens[3]); next(gens[3])
            stop_at(7)
            merge_branch(l, first, 512, dr["w_br_c"][l], ocT, [ocB], OFF["merge"] + 2 * D, tb_list=[tb], local=True)

    if use_c:
        setup_bias()
    for l in range(depth):
        gcol = VTC["norm_g"] + l * 8
        for tb in range(NTB):
            ps, psB = pspool.next()
            for c in range(KC):
                sq, sqB = scrpool.next()
                P.op("act", lambda e, sq=sq, c=c, tb=tb: e.activation(out=sq[:].bitcast(BF16)[:, 0:512], in_=xT[:, c, tbs(tb)], func=AF.Square),
                     reads=[xB[tb]], writes=[sqB])
                P.op("pe", lambda e, sq=sq, c=c, ps=ps: e.matmul(ps[:], onesb[:], sq[:].bitcast(BF16)[:, 0:512], start=(c == 0), stop=(c == KC - 1)),
                     reads=[sqB, cB], writes=[psB])
            rs, rsB = scrpool.next()
            rms_rstd(ps[:], D, rs[:], [psB], [rsB])
            for c in range(KC):
                P.op("dve", lambda e, c=c, tb=tb, rs=rs: e.scalar_tensor_tensor(
                    out=hT[:, c, tbs(tb)], in0=xT[:, c, tbs(tb)], scalar=vt[:, gcol + c:gcol + c + 1], in1=rs[:],
                    op0=ALU.mult, op1=ALU.mult), reads=[xB[tb], rsB, cB], writes=[hB[tb]])

        any_branch = use_a or use_b or use_c
        if use_a:
            phase_A(l)
        if use_b:
            phase_B(l, first=not use_a)
        if use_c:
            phase_C(l, first=not (use_a or use_b))

        if any_branch:
            for oc2 in range(4):
                w, wB = load_w(dr["w_out"][l][:, oc2 * 256:(oc2 + 1) * 256], D, 256)
                for sub in range(2):
                    oc = oc2 * 2 + sub
                    for tb in range(NTB):
                        ps, psB = pspool.next()
                        mm_group(ps[:], [(w[:, k, sub * 128:(sub + 1) * 128], mT[:, k, tbs(tb)]) for k in range(KC)],
                                 reads=[wB, mB[tb]], writes=[psB])
                        P.op("dve", lambda e, oc=oc, tb=tb, ps=ps: e.tensor_tensor(
                            out=xT[:, oc, tbs(tb)], in0=xT[:, oc, tbs(tb)], in1=ps[:], op=ALU.add),
                            reads=[psB, xB[tb]], writes=[xB[tb]])
        for tb in range(NTB):
            copy_op("dve", hT[:, :, tbs(tb)], xT[:, :, tbs(tb)], reads=[xB[tb]], writes=[hB[tb]])
            pt, ptB = tokpool.next()
            P.dma(pt[:].rearrange("p (j f) -> p j f", j=4),
                  dr["p"][l][tb * TBW:(tb + 1) * TBW, :].rearrange("(j p) f -> p j f", p=128), writes=[ptB])
            for c in range(2):
                ps, psB = pspool.next()

                def fn(e, pt=pt, ps=ps, c=c):
                    ins = None
                    for j in range(4):
                        ins = e.transpose(ps[:, j * 128:(j + 1) * 128], pt[:, j * 256 + c * 128:j * 256 + (c + 1) * 128], ident[:])
                    return ins
                P.op("pe", fn, reads=[ptB, cB], writes=[psB])
                copy_op(evac_engine(), mT[:, c, tbs(tb)], ps[:], reads=[psB], writes=[mB[tb]])
        for oc2 in range(4):
            wg, wgB = load_w(dr["w_ple_gate"][l][:, oc2 * 256:(oc2 + 1) * 256], D, 256)
            wp, wpB = load_w(dr["w_ple"][l][:, oc2 * 256:(oc2 + 1) * 256], 256, 256)
            for sub in range(2):
                oc = oc2 * 2 + sub
                for tb in range(NTB):
                    ps, psB = pspool.next()
                    mm_group(ps[:], [(wg[:, k, sub * 128:(sub + 1) * 128], hT[:, k, tbs(tb)]) for k in range(KC)],
                             reads=[wgB, hB[tb]], writes=[psB])
                    sg, sgB = scrpool.next()
                    P.op("act", lambda e, sg=sg, ps=ps: e.activation(out=sg[:], in_=ps[:], func=AF.Sigmoid),
                         reads=[psB], writes=[sgB])
                    ps2, ps2B = pspool.next()
                    mm_group(ps2[:], [(wp[:, k, sub * 128:(sub + 1) * 128], mT[:, k, tbs(tb)]) for k in range(2)],
                             reads=[wpB, mB[tb]], writes=[ps2B])
                    P.op("dve", lambda e, sg=sg, ps2=ps2: e.tensor_tensor(out=sg[:], in0=sg[:], in1=ps2[:], op=ALU.mult),
                         reads=[ps2B, sgB], writes=[sgB])
                    P.op("dve", lambda e, sg=sg, oc=oc, tb=tb: e.tensor_tensor(
                        out=xT[:, oc, tbs(tb)], in0=xT[:, oc, tbs(tb)], in1=sg[:], op=ALU.add),
                        reads=[sgB, xB[tb]], writes=[xB[tb]])

    yB = Buf("y")
    P.barrier()
    fnbB = Buf("fnb")
    P.dma(fnb[:], dr["fnb"], writes=[fnbB])
    for blk in range(NBLK):
        ot, otB = tokpool.next()
        P.op("dve", lambda e: e.memset(small[:, 0:2], 0.0), writes=[smallB])
        pss = []
        for half in range(2):
            ps, psB = pspool.next()

            def fn(e, ps=ps, half=half, blk=blk):
                ins = None
                for j in range(4):
                    c = half * 4 + j
                    ins = e.transpose(ps[:, j * 128:(j + 1) * 128], xT[:, c, blk * 128:(blk + 1) * 128], ident[:])
                return ins
            P.op("pe", fn, reads=[xB[blk // 4], cB], writes=[psB])
            sq, sqB = scrpool.next()
            P.op("act", lambda e, sq=sq, ps=ps, half=half: e.activation(out=sq[:], in_=ps[:], func=AF.Square,
                                                                       accum_out=small[:, half:half + 1]),
                 reads=[psB], writes=[sqB, smallB])
            pss.append((ps, psB))
        P.op("dve", lambda e: e.tensor_tensor(out=small[:, 2:3], in0=small[:, 0:1], in1=small[:, 1:2], op=ALU.add),
             reads=[smallB], writes=[smallB])
        rms_rstd(small[:, 2:3], D, small[:, 3:4], [smallB], [smallB])
        for half in range(2):
            ps, psB = pss[half]
            P.op("dve", lambda e, ps=ps, half=half, ot=ot: e.scalar_tensor_tensor(
                out=ot[:, half * 512:(half + 1) * 512], in0=ps[:], scalar=small[:, 3:4],
                in1=fnb[:, half * 512:(half + 1) * 512], op0=ALU.mult, op1=ALU.mult),
                reads=[psB, smallB, fnbB], writes=[otB])
        P.dma(dr["y"][blk * 128:(blk + 1) * 128, :], ot[:], reads=[otB], writes=[yB])
    P.barrier()
    return nc, P


def rope_consts():
    c = np.zeros((128, 4), np.float32)
    inv_freq = (1.0 / (10000.0 ** (np.arange(0, 32, 2, dtype=np.float32) / 32.0))).astype(np.float32)
    for p in range(64, 96):
        c[p, 0] = inv_freq[(p - 64) % 16]
        c[p, 1] = -1.0 if p < 80 else 1.0
    return c


def host_layout(inputs, b):
    VTC, NV = vt_layout()
    vt = np.zeros((128, NV), np.float32)

    def put(name, arr2d):
        r, f = arr2d.shape
        ch = f // 128
        vt[:, VTC[name]:VTC[name] + r * ch] = arr2d.reshape(r, ch, 128).transpose(2, 0, 1).reshape(128, r * ch)
    put("norm_g", inputs["norm_g"])
    put("q_norm", inputs["mla_q_norm"])
    put("kv_norm", inputs["mla_kv_norm"])
    put("conv_w", inputs["conv_w"].reshape(DEPTH * 4, 1536))
    put("conv_b", inputs["conv_b"])
    put("ssm_norm", inputs["ssm_norm"])
    put("d_skip", np.repeat(inputs["d_skip"], 64, axis=1))
    m = {
        "x": np.ascontiguousarray(inputs["x"][b]),
        "p": np.ascontiguousarray(inputs["p"][:, b]),
        "pos": np.ascontiguousarray(inputs["positions"][b:b + 1]).astype(np.int32),
        "vt": vt,
        "fnb": np.ascontiguousarray(np.broadcast_to(inputs["final_norm"][None, :], (128, D))).astype(np.float32),
        "ident": np.eye(128, dtype=np.float32),
        "tri": np.triu(np.ones((128, 128), np.float32)),
        "ropec": rope_consts(),
        "negm": np.where(np.arange(128)[None, :] > np.arange(128)[:, None], np.float32(-1.0e30), np.float32(0.0)).astype(np.float32),
        "rel_bias": np.ascontiguousarray(inputs["rel_bias"], dtype=np.float32).reshape(1, 256),
        "negs": np.where(np.arange(128)[None, :] < np.arange(128)[:, None], np.float32(-30000.0), np.float32(0.0)).astype(np.float32),
        "sut": np.tril(np.ones((128, 128), np.float32), -1),
        "dt_bias": np.ascontiguousarray(inputs["dt_bias"], dtype=np.float32),
        "a_log": np.ascontiguousarray(inputs["a_log"], dtype=np.float32),
    }
    for k in ("w_in", "w_uq", "w_ukv", "w_br_a", "w_br_b", "w_br_c", "w_out", "w_ple", "w_ple_gate"):
        m[k] = np.ascontiguousarray(inputs[k], dtype=np.float32)
    return m


def kernel(**inputs):
    inputs = {k: np.asarray(v) for k, v in inputs.items()}
    nc, P = build_program()
    in_maps = [host_layout(inputs, b) for b in range(8)]
    res = run_bass_kernel_spmd(nc, in_maps, core_ids=list(range(8)))
    return np.stack([np.asarray(r["y"]) for r in res.results], axis=0).astype(np.float32)
```
